# Optimizing a Trainium2 kernel written in Bass

```python
import math
import jax
import jax.numpy as jnp
from jax import lax
import numpy as np

D_MODEL = 1024
BATCH = 8
SEQ = 4096
DEPTH = 4

GRID_W = 64
CTX_LEN = 256
N_MIXERS = 4
W_GROUP = D_MODEL // N_MIXERS
HEAD_DIM = 64
SHORT_CONV = 3
GDN_HEADS = W_GROUP // HEAD_DIM
GDN_CHUNK = 64
HY_BANDS = 16
HY_EMB = 2 * HY_BANDS + 1
HY_HIDDEN = 64
HY_FILTER_INIT = 0.005
HG_HEADS = W_GROUP // HEAD_DIM
HG_DK = HEAD_DIM
HG_CHUNK = 64
DA_HEADS = W_GROUP // HEAD_DIM
DA_DK = HEAD_DIM // 2
Q_BLOCK = 128
ROPE_THETA = 10000.0
D_FF = 4 * D_MODEL
EPS = 1e-6

GDN_COLS = 4 * W_GROUP + 4 * GDN_HEADS
HY_COLS = 3 * W_GROUP
HG_COLS = 5 * W_GROUP
DA_COLS = 3 * W_GROUP
GROUP_COLS = (GDN_COLS, HY_COLS, HG_COLS, DA_COLS)
P_IN = GDN_COLS + HY_COLS + HG_COLS + DA_COLS

kernel_name = 'hybrid_parallel_heads_flow_block'


def _split(t, sizes):
    return jnp.split(t, np.cumsum(sizes)[:-1].tolist(), axis=-1)


def rms_norm(x, w):
    xf = x.astype(jnp.float32)
    y = xf * lax.rsqrt(jnp.mean(xf * xf, axis=-1, keepdims=True) + EPS)
    return (y * w.astype(jnp.float32)).astype(x.dtype)


def l2_norm(x):
    xf = x.astype(jnp.float32)
    return xf * lax.rsqrt(jnp.sum(xf * xf, axis=-1, keepdims=True) + EPS)


def modulate(x, norm_w, shift, scale):
    return rms_norm(x, norm_w) * (1.0 + scale) + shift


def conv_centred(u, w):
    k, ch = w.shape
    return lax.conv_general_dilated(u, w.reshape(k, 1, ch).astype(u.dtype), window_strides=(1,),
                                    padding=[(k // 2, k // 2)], dimension_numbers=('NWC', 'WIO', 'NWC'),
                                    feature_group_count=ch)


def _flip(t, d):
    return t if d == 0 else jnp.flip(t, axis=1)


def _to_chunks(t, size):
    b, n, h = t.shape[:3]
    t = t.reshape(b, n // size, size, h, *t.shape[3:])
    return jnp.moveaxis(t, 3, 1)


def _from_chunks(t):
    t = jnp.moveaxis(t, 1, 3)
    b, n, c, h = t.shape[:4]
    return t.reshape(b, n * c, h, *t.shape[4:])


def gated_delta_chunked(q, k, v, g, beta, s0):
    dk = q.shape[-1]
    dv = v.shape[-1]
    f32 = jnp.float32
    q, k, v = [_to_chunks(t.astype(f32), GDN_CHUNK) for t in (q, k, v)]
    g, beta = [_to_chunks(t.astype(f32), GDN_CHUNK) for t in (g, beta)]
    q = q * dk ** -0.5
    gc = jnp.cumsum(g, axis=-1)
    tril = jnp.tril(jnp.ones((GDN_CHUNK, GDN_CHUNK), bool))
    stril = jnp.tril(jnp.ones((GDN_CHUNK, GDN_CHUNK), bool), -1)
    decay = jnp.exp(jnp.where(tril, gc[..., :, None] - gc[..., None, :], -jnp.inf))
    kk = jnp.einsum('bhnid,bhnjd->bhnij', k, k)
    t_mat = jnp.where(stril, kk * beta[..., :, None] * decay, 0.0) + jnp.eye(GDN_CHUNK, dtype=f32)
    rhs = jnp.concatenate([v * beta[..., None], k * (beta * jnp.exp(gc))[..., None]], axis=-1)
    sol = lax.linalg.triangular_solve(t_mat, rhs, left_side=True, lower=True)
    u, w = sol[..., :dv], sol[..., dv:]
    attn = jnp.einsum('bhnid,bhnjd->bhnij', q, k) * decay
    qg = q * jnp.exp(gc)[..., None]
    kg = k * jnp.exp(gc[..., -1:] - gc)[..., None]
    glast = jnp.exp(gc[..., -1])
    xs = [jnp.moveaxis(t, 2, 0) for t in (u, w, attn, qg, kg, glast)]

    def step(s, inp):
        u_i, w_i, a_i, qg_i, kg_i, gl_i = inp
        v_new = u_i - jnp.einsum('bhck,bhkv->bhcv', w_i, s)
        o = jnp.einsum('bhck,bhkv->bhcv', qg_i, s) + jnp.einsum('bhcs,bhsv->bhcv', a_i, v_new)
        s = s * gl_i[..., None, None] + jnp.einsum('bhck,bhcv->bhkv', kg_i, v_new)
        return s, o

    s_fin, o = lax.scan(step, s0, xs)
    return _from_chunks(jnp.moveaxis(o, 0, 2)), s_fin


def gla_chunked(q, k, v, logf, s0):
    f32 = jnp.float32
    xs = [jnp.moveaxis(_to_chunks(t.astype(f32), HG_CHUNK), 2, 0) for t in (q, k, v, logf)]
    tril = jnp.tril(jnp.ones((HG_CHUNK, HG_CHUNK), bool))[:, :, None]

    def step(s, inp):
        q_i, k_i, v_i, lf_i = inp
        gcum = jnp.cumsum(lf_i, axis=2)
        dec = jnp.exp(jnp.where(tril, gcum[:, :, :, None, :] - gcum[:, :, None, :, :], -jnp.inf))
        a = jnp.einsum('bhtd,bhsd,bhtsd->bhts', q_i, k_i, dec)
        o = jnp.einsum('bhtd,bhdv->bhtv', q_i * jnp.exp(gcum), s) + jnp.einsum('bhts,bhsv->bhtv', a, v_i)
        g_end = gcum[:, :, -1:, :]
        s = s * jnp.exp(g_end)[:, :, 0, :, None] + jnp.einsum('bhsd,bhsv->bhdv', k_i * jnp.exp(g_end - gcum), v_i)
        return s, o

    s_fin, o = lax.scan(step, s0, xs)
    return _from_chunks(jnp.moveaxis(o, 0, 2)), s_fin


def gdn_mixer(p_lat, p_ctx, conv_w, a_log, dt_bias, norm_w, want_ctx):
    def prep(t):
        b, n = t.shape[:2]
        qkv, z, a, bt = _split(t, [3 * W_GROUP, W_GROUP, 2 * GDN_HEADS, 2 * GDN_HEADS])
        q, k, v = _split(jax.nn.silu(conv_centred(qkv, conv_w)), [W_GROUP] * 3)
        hs = (b, n, GDN_HEADS, HEAD_DIM)
        a = a.astype(jnp.float32).reshape(b, n, 2, GDN_HEADS)
        g = -jnp.exp(a_log.astype(jnp.float32)) * jax.nn.softplus(a + dt_bias.astype(jnp.float32))
        beta = jax.nn.sigmoid(bt.astype(jnp.float32).reshape(b, n, 2, GDN_HEADS))
        return l2_norm(q.reshape(hs)), l2_norm(k.reshape(hs)), v.reshape(hs), z, g, beta

    def run(pk, d, s_init):
        q, k, v, _, g, beta = pk
        o, s = gated_delta_chunked(_flip(q, d), _flip(k, d), _flip(v, d), _flip(g[:, :, d], d),
                                   _flip(beta[:, :, d], d), s_init)
        return _flip(o, d), s

    def finish(o, z):
        b, n = z.shape[:2]
        gate = jax.nn.silu(z.astype(jnp.float32)).reshape(b, n, GDN_HEADS, HEAD_DIM)
        return (rms_norm(o, norm_w) * gate).reshape(b, n, W_GROUP).astype(z.dtype)

    pk_lat, pk_ctx = prep(p_lat), prep(p_ctx)
    s0 = jnp.zeros((p_lat.shape[0], GDN_HEADS, HEAD_DIM, HEAD_DIM), jnp.float32)
    o_lat, o_ctx = 0.0, 0.0
    for d in range(2):
        oc, s_ctx = run(pk_ctx, d, s0)
        ol, _ = run(pk_lat, d, s_ctx)
        o_lat = o_lat + ol
        o_ctx = o_ctx + oc
    return finish(o_lat, pk_lat[3]), (finish(o_ctx, pk_ctx[3]) if want_ctx else None)


def hyena_filters(n, w1, b1, f1, w2, b2, f2, w3, decay):
    f32 = jnp.float32
    pos = jnp.arange(n, dtype=f32)
    t01 = pos / max(n - 1, 1)
    bands = jnp.linspace(1e-4, HY_BANDS - 1, HY_BANDS, dtype=f32)
    ang = (2.0 * math.pi / n) * pos[:, None] * bands
    z = jnp.concatenate([t01[:, None], jnp.cos(ang), -jnp.sin(ang)], axis=-1)
    h = jnp.sin(f1.astype(f32) * (z @ w1.astype(f32) + b1.astype(f32)))
    h = jnp.sin(f2.astype(f32) * (h @ w2.astype(f32) + b2.astype(f32)))
    h = (h @ w3.astype(f32)).reshape(n, 2, 2, W_GROUP)
    window = jnp.exp(-t01[:, None, None] * jnp.abs(decay.astype(f32)))
    return h * window[:, None]


def long_conv_bidir(u, h_fwd, h_bwd, skip):
    n, ch = h_fwd.shape
    kern = jnp.concatenate([h_fwd, jnp.zeros((1, ch), jnp.float32), jnp.flip(h_bwd[1:], axis=0)], axis=0)
    kf = jnp.fft.rfft(kern, n=2 * n, axis=0)
    uf = jnp.fft.rfft(u.astype(jnp.float32), n=2 * n, axis=1)
    y = jnp.fft.irfft(uf * kf, n=2 * n, axis=1)[:, :n]
    return (y + u.astype(jnp.float32) * skip.astype(jnp.float32)).astype(u.dtype)


def hyena_mixer(p_lat, p_ctx, conv_w, filt_params, decay, skip, want_ctx):
    def run(t):
        v, x1, x2 = _split(conv_centred(t, conv_w), [W_GROUP] * 3)
        h = hyena_filters(t.shape[1], *filt_params, decay)
        z = x1 * long_conv_bidir(v, h[:, 0, 0], h[:, 1, 0], skip[0])
        return x2 * long_conv_bidir(z, h[:, 0, 1], h[:, 1, 1], skip[1])
    return run(p_lat), (run(p_ctx) if want_ctx else None)


def hgrn2_mixer(p_lat, p_ctx, lb, norm_w, want_ctx):
    lbh = lb.reshape(HG_HEADS, HG_DK)

    def prep(t):
        b, n = t.shape[:2]
        q, i, f_fwd, f_bwd, og = _split(t, [W_GROUP] * 5)
        fl = jnp.stack([f_fwd, f_bwd], axis=2).astype(jnp.float32).reshape(b, n, 2, HG_HEADS, HG_DK)
        logf = jnp.logaddexp(jnp.log(lbh), jnp.log1p(-lbh) + jax.nn.log_sigmoid(fl))
        key = (1.0 - lbh) * jax.nn.sigmoid(-fl)
        return (jax.nn.silu(q).reshape(b, n, HG_HEADS, HG_DK), i.reshape(b, n, HG_HEADS, HEAD_DIM), logf, key, og)

    def run(pk, d, s_init):
        q, i, logf, key, _ = pk
        o, s = gla_chunked(_flip(q, d), _flip(key[:, :, d], d), _flip(i, d), _flip(logf[:, :, d], d), s_init)
        return _flip(o, d), s

    def finish(o, og):
        b, n = og.shape[:2]
        gate = jax.nn.silu(og.astype(jnp.float32)).reshape(b, n, HG_HEADS, HEAD_DIM)
        return (rms_norm(o, norm_w) * gate).reshape(b, n, W_GROUP).astype(og.dtype)

    pk_lat, pk_ctx = prep(p_lat), prep(p_ctx)
    s0 = jnp.zeros((p_lat.shape[0], HG_HEADS, HG_DK, HEAD_DIM), jnp.float32)
    o_lat, o_ctx = 0.0, 0.0
    for d in range(2):
        oc, s_ctx = run(pk_ctx, d, s0)
        ol, _ = run(pk_lat, d, s_ctx)
        o_lat = o_lat + ol
        o_ctx = o_ctx + oc
    return finish(o_lat, pk_lat[4]), (finish(o_ctx, pk_ctx[4]) if want_ctx else None)


def axial_rope_tables(rows):
    n_freq = DA_DK // 4
    inv = ROPE_THETA ** (-jnp.arange(n_freq, dtype=jnp.float32) / n_freq)
    r = jnp.repeat(jnp.arange(rows, dtype=jnp.float32), GRID_W)
    col = jnp.tile(jnp.arange(GRID_W, dtype=jnp.float32), rows)
    ang_r = r[:, None] * inv
    ang_c = col[:, None] * inv
    return tuple(t[None, :, None, None, :] for t in (jnp.cos(ang_r), jnp.sin(ang_r), jnp.cos(ang_c), jnp.sin(ang_c)))


def _rotate(x, cos, sin):
    x1, x2 = jnp.split(x, 2, axis=-1)
    return jnp.concatenate([x1 * cos - x2 * sin, x2 * cos + x1 * sin], axis=-1)


def apply_axial_rope(x, rope):
    cos_r, sin_r, cos_c, sin_c = rope
    xr, xcol = jnp.split(x.astype(jnp.float32), 2, axis=-1)
    return jnp.concatenate([_rotate(xr, cos_r, sin_r), _rotate(xcol, cos_c, sin_c)], axis=-1).astype(x.dtype)


def diff_attn_mixer(p_lat, p_ctx, rope, qn_w, kn_w, lam_p, subln_w, lam_init, want_ctx):
    def prep(t):
        b, n = t.shape[:2]
        q, k, v = _split(t, [W_GROUP] * 3)
        hs = (b, n, DA_HEADS, 2, DA_DK)
        return rms_norm(q.reshape(hs), qn_w), rms_norm(k.reshape(hs), kn_w), v.reshape(b, n, DA_HEADS, 2 * DA_DK)

    q, k, v = prep(p_lat)
    qc, kc, vc = prep(p_ctx)
    q = apply_axial_rope(q, rope)
    k = apply_axial_rope(k, rope)
    lp = lam_p.astype(jnp.float32)
    lam = jnp.exp(jnp.sum(lp[0] * lp[1])) - jnp.exp(jnp.sum(lp[2] * lp[3])) + lam_init
    scale = DA_DK ** -0.5

    def attend(qb, kk, vv):
        s = jnp.einsum('bqhjd,bkhjd->bhjqk', qb, kk).astype(jnp.float32) * scale
        pr = jax.nn.softmax(s, axis=-1)
        w = (pr[:, :, 0] - lam * pr[:, :, 1]).astype(vv.dtype)
        return jnp.einsum('bhqk,bkhe->bqhe', w, vv)

    def finish(o):
        b, n = o.shape[:2]
        return (rms_norm(o, subln_w) * (1.0 - lam_init)).reshape(b, n, W_GROUP)

    b, n = q.shape[:2]
    k_all = jnp.concatenate([k, kc], axis=1)
    v_all = jnp.concatenate([v, vc], axis=1)
    q_blocks = jnp.moveaxis(q.reshape(b, n // Q_BLOCK, Q_BLOCK, DA_HEADS, 2, DA_DK), 1, 0)
    o = lax.map(lambda qb: attend(qb, k_all, v_all), q_blocks)
    o = jnp.moveaxis(o, 0, 1).reshape(b, n, DA_HEADS, 2 * DA_DK)
    o_ctx = finish(attend(qc, kc, vc)) if want_ctx else None
    return finish(o), o_ctx


def sq_relu_mlp(h, w1, w2):
    return jnp.square(jax.nn.relu(h @ w1)) @ w2


def setup_inputs(seed: int = 0) -> dict:
    key = jax.random.key(seed)
    ks = iter(jax.random.split(key, 40))
    f32 = jnp.float32

    def nrm(shape, s):
        return jax.random.normal(next(ks), shape, f32) * s

    d = D_MODEL
    x = nrm((BATCH, SEQ, d), 1.0)
    c = nrm((BATCH, d), 1.0)
    ctx = nrm((BATCH, CTX_LEN, d), 1.0)
    c_ctx = nrm((d,), 1.0)
    mod_w = nrm((DEPTH, d, 6 * d), 0.5 * d ** -0.5)
    mod_b = nrm((DEPTH, 6 * d), 0.02)
    ln1_w = 1.0 + nrm((DEPTH, d), 0.05)
    ln2_w = 1.0 + nrm((DEPTH, d), 0.05)
    w_in = nrm((DEPTH, d, P_IN), d ** -0.5)
    gdn_conv_w = nrm((DEPTH, SHORT_CONV, 3 * W_GROUP), SHORT_CONV ** -0.5)
    gdn_a_log = jnp.log(jax.random.uniform(next(ks), (DEPTH, 2, GDN_HEADS), f32, 1.0, 16.0))
    dt = jnp.exp(jax.random.uniform(next(ks), (DEPTH, 2, GDN_HEADS), f32, math.log(1e-3), math.log(1e-1)))
    gdn_dt_bias = dt + jnp.log(-jnp.expm1(-dt))
    gdn_norm_w = 1.0 + nrm((DEPTH, HEAD_DIM), 0.05)
    hy_conv_w = nrm((DEPTH, SHORT_CONV, 3 * W_GROUP), SHORT_CONV ** -0.5)
    hy_w1 = nrm((DEPTH, HY_EMB, HY_HIDDEN), HY_EMB ** -0.5)
    hy_b1 = nrm((DEPTH, HY_HIDDEN), 0.1)
    hy_f1 = 1.0 + nrm((DEPTH, HY_HIDDEN), 0.05)
    hy_w2 = nrm((DEPTH, HY_HIDDEN, HY_HIDDEN), HY_HIDDEN ** -0.5)
    hy_b2 = nrm((DEPTH, HY_HIDDEN), 0.1)
    hy_f2 = 1.0 + nrm((DEPTH, HY_HIDDEN), 0.05)
    hy_w3 = nrm((DEPTH, HY_HIDDEN, 4 * W_GROUP), HY_FILTER_INIT)
    base = jnp.linspace(abs(math.log(1e-2)) / 1.5, abs(math.log(1e-2)) / 0.3, W_GROUP, dtype=f32)
    hy_decay = base + nrm((DEPTH, 2, W_GROUP), 0.1)
    hy_bias = nrm((DEPTH, 2, W_GROUP), 0.3)
    hg_lb_raw = nrm((DEPTH, W_GROUP), 0.5)
    hg_norm_w = 1.0 + nrm((DEPTH, HEAD_DIM), 0.05)
    da_q_norm = 1.0 + nrm((DEPTH, DA_DK), 0.05)
    da_k_norm = 1.0 + nrm((DEPTH, DA_DK), 0.05)
    da_lam = nrm((DEPTH, 4, DA_DK), 0.1)
    da_subln = 1.0 + nrm((DEPTH, 2 * DA_DK), 0.05)
    w_out = nrm((DEPTH, d, d), d ** -0.5)
    mlp_w1 = nrm((DEPTH, d, D_FF), d ** -0.5)
    mlp_w2 = nrm((DEPTH, D_FF, d), D_FF ** -0.5)
    return {'x': x, 'c': c, 'ctx': ctx, 'c_ctx': c_ctx, 'mod_w': mod_w, 'mod_b': mod_b,
            'ln1_w': ln1_w, 'ln2_w': ln2_w, 'w_in': w_in, 'gdn_conv_w': gdn_conv_w,
            'gdn_a_log': gdn_a_log, 'gdn_dt_bias': gdn_dt_bias, 'gdn_norm_w': gdn_norm_w,
            'hy_conv_w': hy_conv_w, 'hy_w1': hy_w1, 'hy_b1': hy_b1, 'hy_f1': hy_f1,
            'hy_w2': hy_w2, 'hy_b2': hy_b2, 'hy_f2': hy_f2, 'hy_w3': hy_w3,
            'hy_decay': hy_decay, 'hy_bias': hy_bias, 'hg_lb_raw': hg_lb_raw,
            'hg_norm_w': hg_norm_w, 'da_q_norm': da_q_norm, 'da_k_norm': da_k_norm,
            'da_lam': da_lam, 'da_subln': da_subln, 'w_out': w_out, 'mlp_w1': mlp_w1,
            'mlp_w2': mlp_w2}


def reference(x, c, ctx, c_ctx, mod_w, mod_b, ln1_w, ln2_w, w_in, gdn_conv_w, gdn_a_log,
              gdn_dt_bias, gdn_norm_w, hy_conv_w, hy_w1, hy_b1, hy_f1, hy_w2, hy_b2, hy_f2,
              hy_w3, hy_decay, hy_bias, hg_lb_raw, hg_norm_w, da_q_norm, da_k_norm, da_lam,
              da_subln, w_out, mlp_w1, mlp_w2):
    rows = x.shape[1] // GRID_W
    rope = axial_rope_tables(rows)
    lb_all = jnp.cumsum(jax.nn.softmax(hg_lb_raw.astype(jnp.float32), axis=0), axis=0)
    lb_all = lb_all - lb_all[0]
    s_lat = jax.nn.silu(c)
    s_ctx = jax.nn.silu(c_ctx)
    xc = ctx
    for l in range(DEPTH):
        want_ctx = l < DEPTH - 1
        sh1, s1, g1, sh2, s2, g2 = jnp.split((s_lat @ mod_w[l] + mod_b[l])[:, None, :], 6, axis=-1)
        ch1, cs1, cg1, ch2, cs2, cg2 = jnp.split(s_ctx @ mod_w[l] + mod_b[l], 6, axis=-1)
        lat_gdn, lat_hy, lat_hg, lat_da = _split(modulate(x, ln1_w[l], sh1, s1) @ w_in[l], GROUP_COLS)
        ctx_gdn, ctx_hy, ctx_hg, ctx_da = _split(modulate(xc, ln1_w[l], ch1, cs1) @ w_in[l], GROUP_COLS)
        o_a, o_a_c = gdn_mixer(lat_gdn, ctx_gdn, gdn_conv_w[l], gdn_a_log[l], gdn_dt_bias[l],
                               gdn_norm_w[l], want_ctx)
        o_b, o_b_c = hyena_mixer(lat_hy, ctx_hy, hy_conv_w[l],
                                 (hy_w1[l], hy_b1[l], hy_f1[l], hy_w2[l], hy_b2[l], hy_f2[l], hy_w3[l]),
                                 hy_decay[l], hy_bias[l], want_ctx)
        o_c, o_c_c = hgrn2_mixer(lat_hg, ctx_hg, lb_all[l], hg_norm_w[l], want_ctx)
        lam_init = 0.8 - 0.6 * math.exp(-0.3 * l)
        o_d, o_d_c = diff_attn_mixer(lat_da, ctx_da, rope, da_q_norm[l], da_k_norm[l], da_lam[l],
                                     da_subln[l], lam_init, want_ctx)
        x = x + g1 * (jnp.concatenate([o_a, o_b, o_c, o_d], axis=-1) @ w_out[l])
        x = x + g2 * sq_relu_mlp(modulate(x, ln2_w[l], sh2, s2), mlp_w1[l], mlp_w2[l])
        if want_ctx:
            xc = xc + cg1 * (jnp.concatenate([o_a_c, o_b_c, o_c_c, o_d_c], axis=-1) @ w_out[l])
            xc = xc + cg2 * sq_relu_mlp(modulate(xc, ln2_w[l], ch2, cs2), mlp_w1[l], mlp_w2[l])
    return x
```

```python
import numpy as np
from contextlib import ExitStack
import concourse.bass as bass
import concourse.mybir as mybir
from concourse.bass_utils import run_bass_kernel_spmd

F32 = mybir.dt.float32
BF16 = mybir.dt.bfloat16
AF = mybir.ActivationFunctionType
ALU = mybir.AluOpType
AX = mybir.AxisListType

COMPUTE = ("tensor", "vector", "scalar", "gpsimd")
NSLOT = 8


class Op:
    __slots__ = ("stream", "eng", "fn", "waits", "inc", "idx", "is_dma", "slot", "slot_use", "signal")

    def __init__(self):
        self.waits = []
        self.inc = None
        self.signal = False


class T:
    def __init__(self, name, ap):
        self.name = name
        self.ap = ap

    def __getitem__(self, k):
        return self.ap[k]


class Prog:
    def __init__(self, nc, arena_words=53000):
        self.nc = nc
        self.es = ExitStack()
        self.arena = self.es.enter_context(nc.sbuf_tensor("arena", [128, arena_words], F32))
        self.arena_words = arena_words
        self.bump = 0
        self.barrier_ops = []
        self.barrier_seen = set()
        self.ntile = 0
        self.ops = []
        self.last_w = {}
        self.readers = {}
        self.streams = {}
        self.sem = {}
        self.cnt = {}
        self.dma_slots = {}
        self.slot_uses = {}

    def tile(self, name, free, dt=F32, parts=128):
        free = list(free)
        n = 1
        for f in free:
            n *= f
        words = n if dt == F32 else (n + 1) // 2
        words = (words + 7) // 8 * 8
        assert self.bump + words <= self.arena_words, (name, self.bump, words)
        v = self.arena[0:parts, self.bump:self.bump + words]
        if dt != F32:
            v = v.bitcast(dt)
        v = v[:, 0:n]
        if len(free) == 2:
            v = v.rearrange("p (a b) -> p a b", a=free[0])
        elif len(free) == 3:
            v = v.rearrange("p (a b c) -> p a b c", a=free[0], b=free[1])
        self.bump += words
        self.ntile += 1
        return T("%s#%d" % (name, self.ntile), v)

    def mark(self):
        return self.bump

    def reset(self, mark):
        self.barrier()
        self.bump = mark

    def barrier(self):
        ops = []
        for st, lst in self.streams.items():
            n = NSLOT if st.startswith("dma_") else 1
            ops.extend(lst[-n:])
        for o in ops:
            o.signal = True
        self.barrier_ops = ops
        self.barrier_seen = set()

    def ps(self, name, shape, dt=F32):
        return T(name, self.es.enter_context(self.nc.psum_tensor(name, list(shape), dt))[:])

    def dram(self, name, shape, dt=F32, kind="Internal"):
        return self.nc.dram_tensor(name, list(shape), dt, kind=kind)

    @staticmethod
    def _k(r):
        if isinstance(r, (str, int)):
            return r
        if isinstance(r, tuple):
            return tuple(Prog._k(x) for x in r)
        return "T:" + r.name

    def _record(self, op, reads, writes, acc=False):
        reads = [self._k(r) for r in reads]
        writes = [self._k(r) for r in writes]
        deps = set()
        for r in reads:
            w = self.last_w.get(r)
            if w is not None:
                deps.add(w)
            if isinstance(r, str) and r.startswith("T:ps"):
                for rd in self.readers.get(r, ()):
                    if rd.stream != op.stream:
                        deps.add(rd)
        for wr in writes:
            w = self.last_w.get(wr)
            if w is not None and not (acc and w.stream == "tensor" and op.stream == "tensor"):
                deps.add(w)
            for rd in self.readers.get(wr, ()):
                deps.add(rd)
        if op.eng not in self.barrier_seen:
            self.barrier_seen.add(op.eng)
            for b in self.barrier_ops:
                deps.add(b)
        deps.discard(op)
        for d in deps:
            if d.stream == "tensor" and op.stream == "tensor":
                continue
            d.signal = True
            op.waits.append(d)
        for r in reads:
            self.readers.setdefault(r, []).append(op)
        for wr in writes:
            self.last_w[wr] = op
            self.readers[wr] = []
        op.idx = len(self.ops)
        self.ops.append(op)
        self.streams.setdefault(op.stream, []).append(op)

    def op(self, eng, fn, reads=(), writes=(), acc=False):
        o = Op()
        o.stream = eng
        o.eng = eng
        o.fn = fn
        o.is_dma = False
        self._record(o, reads, writes, acc)
        return o

    def dma(self, queue, out, in_, reads=(), writes=(), **kw):
        o = Op()
        o.stream = "dma_" + queue
        o.eng = queue
        o.fn = lambda e, out=out, in_=in_, kw=kw: e.dma_start(out=out, in_=in_, **kw)
        o.is_dma = True
        s = self.dma_slots.get(queue, 0)
        self.dma_slots[queue] = (s + 1) % NSLOT
        o.slot = s
        u = self.slot_uses.get((queue, s), 0) + 1
        self.slot_uses[(queue, s)] = u
        o.slot_use = u
        o.signal = True
        self._record(o, reads, writes)
        return o

    def finalize(self, final_wait_ops=()):
        nc = self.nc
        es = self.es
        for st in COMPUTE:
            self.sem[st] = es.enter_context(nc.semaphore("s_" + st))
        for q in self.dma_slots:
            for s in range(NSLOT):
                self.sem[("dma", q, s)] = es.enter_context(nc.semaphore("d_%s_%d" % (q, s)))
        for st in COMPUTE:
            c = 0
            for o in self.streams.get(st, ()):
                if o.signal:
                    c += 1
                    o.inc = c
        per_eng = {}
        for o in self.ops:
            per_eng.setdefault(o.eng, []).append(o)
        block = es.enter_context(nc.Block())

        def target(d):
            if d.is_dma:
                return self.sem[("dma", d.eng, d.slot)], 16 * d.slot_use
            return self.sem[d.stream], d.inc

        def emit(engname):
            ops = per_eng.get(engname, [])

            def body(e):
                waited = {}
                for o in ops:
                    ws = {}
                    for d in o.waits:
                        sem, val = target(d)
                        k = id(sem)
                        if waited.get(k, 0) >= val:
                            continue
                        if k not in ws or ws[k][1] < val:
                            ws[k] = (sem, val)
                    if o.is_dma and o.slot_use > 1:
                        sem = self.sem[("dma", o.eng, o.slot)]
                        val = 16 * (o.slot_use - 1)
                        k = id(sem)
                        if waited.get(k, 0) < val and (k not in ws or ws[k][1] < val):
                            ws[k] = (sem, val)
                    for k, (sem, val) in ws.items():
                        e.wait_ge(sem, val)
                        waited[k] = val
                    ins = o.fn(e)
                    if o.is_dma:
                        ins.then_inc(self.sem[("dma", o.eng, o.slot)], 16)
                    elif o.signal:
                        ins.then_inc(self.sem[o.stream], 1)
                if engname == "sync":
                    for d in final_wait_ops:
                        sem, val = target(d)
                        e.wait_ge(sem, val)
            return body

        for engname in ("sync", "scalar", "vector", "gpsimd", "tensor"):
            if engname in per_eng or engname == "sync":
                getattr(block, engname)(emit(engname))
        es.close()

import math
import numpy as np

U8 = mybir.dt.uint8
L = 4
D = 1024
TC = 256
TL = 4096
TT = TC + TL
NT = TT // 128
PIN = 3856
EPS = 1e-6
C_GDN = 0
C_HY = 1040
C_HG = 1808
C_DA = 3088


def host_consts():
    c = {}
    c["ident"] = np.eye(128, dtype=np.float32)
    c["ones"] = np.ones((128, 128), np.float32)
    j = np.arange(128)[:, None]
    i = np.arange(128)[None, :]
    same = (j // 64) == (i // 64)
    c["tri_f"] = (same & (j <= i)).astype(np.float32)
    c["tri_fs"] = (same & (j < i)).astype(np.float32)
    c["tri_b"] = (same & (j >= i)).astype(np.float32)
    c["tri_bs"] = (same & (j > i)).astype(np.float32)
    c["blk"] = same.astype(np.float32)
    return c


class Ctx:
    pass


def load_bcast(P, q, tile, src_ap, n):
    P.dma(q, tile[:], src_ap.partition_broadcast(128), writes=[tile])


def phase_mods(K, l):
    P = K.P
    nc = K.nc
    m0 = P.mark()
    sT = K.sT
    srep = P.tile("srep", [8, 2, 128])
    P.op("vector", lambda e: e.tensor_copy(srep[:], sT[:].unsqueeze(3).broadcast_to([128, 8, 2, 128])), reads=[sT], writes=[srep])
    fm = K.fm
    wblk = [P.tile("mw%d" % i, [8, 512]) for i in range(2)]
    mb = P.tile("mb", [48])
    P.dma("sync", mb[:], K.mod_bT[l], writes=[mb])
    grp = {0: 0, 1: 1, 3: 2, 4: 3}
    for gi, g in ((0, 2), (1, 5)):
        for s in range(2):
            load_bcast(P, "sync", K.G[gi][s], K.mod_b[l][g * 1024:(g + 1) * 1024], 1024)
    pb = 0
    for g in range(6):
        for hb in range(2):
            w = wblk[(g * 2 + hb) % 2]
            src = K.mod_w[l][:, g * 1024 + hb * 512:g * 1024 + hb * 512 + 512].rearrange("(c p) n -> p c n", p=128)
            P.dma("sync" if (g * 2 + hb) % 2 == 0 else "gpsimd", w[:], src, writes=[w])
            if g in grp:
                ps = K.ps[pb % 2]
                pb += 1
                for fb in range(4):
                    for c in range(8):
                        P.op("tensor", lambda e, ps=ps, w=w, fb=fb, c=c: e.matmul(
                            ps[:, fb * 2:fb * 2 + 2], w[:, c, fb * 128:(fb + 1) * 128], sT[:, c, :],
                            start=(c == 0), stop=(c == 7)), reads=[w, sT], writes=[ps], acc=True)
                gi = grp[g]
                for fb in range(4):
                    ch = hb * 4 + fb
                    P.op("vector", lambda e, ps=ps, fb=fb, gi=gi, ch=ch, g=g: e.tensor_scalar(
                        fm[:, gi, ch, :], ps[:, fb * 2:fb * 2 + 2], mb[:, g * 8 + ch:g * 8 + ch + 1], 1.0, ALU.add, ALU.mult),
                        reads=[ps, mb], writes=[fm])
            else:
                gi = 0 if g == 2 else 1
                for s in range(2):
                    ps = K.ps[2 + (pb % 2)]
                    pb += 1
                    for c in range(8):
                        P.op("tensor", lambda e, ps=ps, w=w, c=c, s=s: e.matmul(
                            ps[:, 0:512], srep[:, c, s, :], w[:, c, :], start=(c == 0), stop=(c == 7)),
                            reads=[w, srep], writes=[ps], acc=True)
                    G = K.G[gi][s]
                    P.op("vector", lambda e, ps=ps, G=G, hb=hb: e.tensor_tensor(
                        G[:, hb * 512:(hb + 1) * 512], ps[:, 0:512], G[:, hb * 512:(hb + 1) * 512], ALU.add),
                        reads=[ps, G], writes=[G])
    for (Av, lnw, si) in ((K.A1, K.ln1T, 1), (K.A2, K.ln2T, 3)):
        P.op("vector", lambda e, Av=Av, si=si: e.tensor_scalar(Av[:], fm[:, si, :, :], 1.0, 1.0, ALU.add, ALU.mult), reads=[fm], writes=[Av])
        P.op("vector", lambda e, Av=Av, lnw=lnw: e.tensor_tensor(
            Av[:], Av[:], lnw[:, l, :].unsqueeze(2).broadcast_to([128, 8, 2]), ALU.mult), reads=[Av, lnw], writes=[Av])
    P.reset(m0)


def rms_tile(K, xt, xn_bf, scr, ss, rs):
    P = K.P
    P.op("scalar", lambda e: e.activation(scr[:], xt[:], AF.Square, accum_out=ss[:]), reads=[xt], writes=[scr, ss])
    P.op("scalar", lambda e: e.activation(rs[:], ss[:], AF.Sqrt, bias=K.epsc[:, 0:1], scale=1.0 / D), reads=[ss, K.epsc], writes=[rs])
    P.op("vector", lambda e: e.reciprocal(rs[:], rs[:]), reads=[rs], writes=[rs])
    P.op("vector", lambda e: e.tensor_scalar(xn_bf[:], xt[:], rs[:, 0:1], 1.0, ALU.mult, ALU.mult), reads=[xt, rs], writes=[xn_bf])


def transpose_mod(K, xn_bf, hT, Av, Bv, s, psb_t, col0=0):
    P = K.P
    psb = psb_t.ap[:, 0:512].bitcast(BF16).rearrange("p (c t) -> p c t", c=8)
    for c in range(8):
        P.op("tensor", lambda e, c=c: e.transpose(psb[:, c, :], xn_bf[:, c * 128:(c + 1) * 128], K.identb[:]),
             reads=[xn_bf, K.identb], writes=[psb_t])
    tmp = K.tmod
    P.op("vector", lambda e: e.tensor_tensor(tmp[:], psb, Av[:, :, s:s + 1].broadcast_to([128, 8, 128]), ALU.mult),
         reads=[psb_t, Av], writes=[tmp])
    P.op("gpsimd", lambda e: e.tensor_tensor(hT[:, :, col0:col0 + 128], tmp[:], Bv[:, :, s:s + 1].broadcast_to([128, 8, 128]), ALU.add),
         reads=[tmp, Bv], writes=[hT])


def phase_proj(K, l):
    P = K.P
    m0 = P.mark()
    W = P.tile("win", [8, PIN], BF16)
    for c in range(8):
        P.dma("gpsimd", W[:, c, :], K.w_in[l][c * 128:(c + 1) * 128, :], writes=[W])
    xt = [P.tile("xt%d" % i, [D]) for i in range(2)]
    xn = [P.tile("xn%d" % i, [D], BF16) for i in range(2)]
    hT = [P.tile("hT%d" % i, [8, 128], BF16) for i in range(2)]
    ot = [P.tile("ot%d" % i, [PIN]) for i in range(2)]
    scr = P.tile("scr", [D])
    ss = [P.tile("ss%d" % i, [1]) for i in range(2)]
    rs = [P.tile("rs%d" % i, [1]) for i in range(2)]
    K.tmod = P.tile("tmod", [8, 128])
    Bv = K.fm
    P.dma("sync", xt[0][:], K.X[0:128, :], writes=[xt[0]])
    nblk = (PIN + 511) // 512
    for i in range(NT):
        b = i % 2
        if i + 1 < NT:
            P.dma("sync", xt[1 - b][:], K.X[(i + 1) * 128:(i + 2) * 128, :], writes=[xt[1 - b]])
        s = 1 if i < 2 else 0
        rms_tile(K, xt[b], xn[b], scr, ss[b], rs[b])
        transpose_mod(K, xn[b], hT[b], K.A1, T(K.fm.name, K.fm[:, 0, :, :]), s, K.ps[7])
        for nb in range(nblk):
            c0 = nb * 512
            w = min(512, PIN - c0)
            ps = K.ps[nb % 4]
            for c in range(8):
                P.op("tensor", lambda e, ps=ps, c=c, c0=c0, w=w, b=b: e.matmul(
                    ps[:, 0:w], hT[b][:, c, :], W[:, c, c0:c0 + w], start=(c == 0), stop=(c == 7)),
                    reads=[hT[b], W], writes=[ps], acc=True)
            if nb % 2 == 0:
                P.op("scalar", lambda e, ps=ps, c0=c0, w=w, b=b: e.copy(ot[b][:, c0:c0 + w], ps[:, 0:w]), reads=[ps], writes=[ot[b]])
            else:
                P.op("vector", lambda e, ps=ps, c0=c0, w=w, b=b: e.tensor_copy(ot[b][:, c0:c0 + w], ps[:, 0:w]), reads=[ps], writes=[ot[b]])
        P.dma("sync", K.Pj[i * 128:(i + 1) * 128, :], ot[b][:], reads=[ot[b]])
    P.reset(m0)


GQW = 768 + 16


def V(P, fn, reads, writes):
    return P.op("vector", fn, reads=reads, writes=writes)


def G_(P, fn, reads, writes):
    return P.op("gpsimd", fn, reads=reads, writes=writes)


def A_(P, fn, reads, writes):
    return P.op("scalar", fn, reads=reads, writes=writes)


def MM(P, ps, out_ap, lhsT, rhs, reads, start=True, stop=True):
    return P.op("tensor", lambda e: e.matmul(out_ap, lhsT, rhs, start=start, stop=stop), reads=reads, writes=[ps], acc=True)


def TR(P, ps, out_ap, in_ap, ident_ap, reads):
    return P.op("tensor", lambda e: e.transpose(out_ap, in_ap, ident_ap), reads=reads, writes=[ps])


def load_shift3(K, q, dst, src, i, c0, c1):
    P = K.P
    r0 = i * 128
    first = i in (0, 2)
    last = i in (1, NT - 1)
    if first:
        V(P, lambda e: e.memset(dst[0][:], 0.0), [], [dst[0]])
        P.dma(q, dst[0][1:128, :], src[r0:r0 + 127, c0:c1], writes=[dst[0]])
    else:
        P.dma(q, dst[0][:], src[r0 - 1:r0 + 127, c0:c1], writes=[dst[0]])
    P.dma(q, dst[1][:], src[r0:r0 + 128, c0:c1], writes=[dst[1]])
    if last:
        V(P, lambda e: e.memset(dst[2][:], 0.0), [], [dst[2]])
        P.dma(q, dst[2][0:127, :], src[r0 + 1:r0 + 128, c0:c1], writes=[dst[2]])
    else:
        P.dma(q, dst[2][:], src[r0 + 1:r0 + 129, c0:c1], writes=[dst[2]])


def conv3(K, x3, cw, acc, t0, t2, n):
    P = K.P
    G_(P, lambda e: e.tensor_tensor(t0[:, 0:n], x3[0][:, 0:n], cw[:, 0, 0:n], ALU.mult), [x3[0], cw], [t0])
    V(P, lambda e: e.tensor_tensor(acc[:, 0:n], x3[1][:, 0:n], cw[:, 1, 0:n], ALU.mult), [x3[1], cw], [acc])
    G_(P, lambda e: e.tensor_tensor(t2[:, 0:n], x3[2][:, 0:n], cw[:, 2, 0:n], ALU.mult), [x3[2], cw], [t2])
    V(P, lambda e: e.tensor_tensor(acc[:, 0:n], acc[:, 0:n], t0[:, 0:n], ALU.add), [acc, t0], [acc])
    V(P, lambda e: e.tensor_tensor(acc[:, 0:n], acc[:, 0:n], t2[:, 0:n], ALU.add), [acc, t2], [acc])


def phase_gdn_pre(K, l):
    P = K.P
    m0 = P.mark()
    cw = P.tile("cw", [3, 768])
    for k in range(3):
        P.dma("sync", cw[:, k, :], K.gdn_conv_w[l][k].partition_broadcast(128), writes=[cw])
    negA = P.tile("negA", [8])
    dtb = P.tile("dtb", [8])
    P.dma("sync", negA[:], K.gdn_a_log[l].rearrange("a b -> (a b)").partition_broadcast(128), writes=[negA])
    P.dma("sync", dtb[:], K.gdn_dt_bias[l].rearrange("a b -> (a b)").partition_broadcast(128), writes=[dtb])
    A_(P, lambda e: e.activation(negA[:], negA[:], AF.Exp), [negA], [negA])
    V(P, lambda e: e.tensor_scalar(negA[:], negA[:], -1.0, 1.0, ALU.mult, ALU.mult), [negA], [negA])
    x3 = [[P.tile("x3_%d_%d" % (b, k), [768]) for k in range(3)] for b in range(2)]
    zab = [P.tile("zab%d" % b, [16]) for b in range(2)]
    acc = P.tile("acc", [768])
    t0 = P.tile("t0", [768])
    t2 = P.tile("t2", [768])
    sq = P.tile("sq", [512])
    ssum = P.tile("ssum", [8])
    tg = P.tile("tg", [8])
    tb = P.tile("tb", [8])
    og = [P.tile("og%d" % b, [GQW]) for b in range(2)]

    def loads(i):
        b = i % 2
        load_shift3(K, "sync", x3[b], K.Pj, i, 0, 768)
        P.dma("sync", zab[b][:], K.Pj[i * 128:(i + 1) * 128, 1024:1040], writes=[zab[b]])
    loads(0)
    for i in range(NT):
        b = i % 2
        if i + 1 < NT:
            loads(i + 1)
        o = og[b]
        conv3(K, x3[b], cw, acc, t0, t2, 768)
        A_(P, lambda e, o=o: e.activation(o[:, 0:768], acc[:], AF.Silu), [acc], [o])
        G_(P, lambda e, o=o: e.tensor_tensor(sq[:], o[:, 0:512], o[:, 0:512], ALU.mult), [o], [sq])
        V(P, lambda e: e.tensor_reduce(ssum[:], sq[:].rearrange("p (g d) -> p g d", g=8), AX.X, ALU.add), [sq], [ssum])
        A_(P, lambda e: e.activation(ssum[:], ssum[:], AF.Sqrt, bias=K.epsc[:, 0:1], scale=1.0), [ssum, K.epsc], [ssum])
        V(P, lambda e: e.reciprocal(ssum[:], ssum[:]), [ssum], [ssum])
        V(P, lambda e: e.tensor_scalar(ssum[:, 0:4], ssum[:, 0:4], 0.125, 1.0, ALU.mult, ALU.mult), [ssum], [ssum])
        V(P, lambda e, o=o: e.tensor_tensor(o[:, 0:512].rearrange("p (g d) -> p g d", g=8), o[:, 0:512].rearrange("p (g d) -> p g d", g=8),
                                            ssum[:].unsqueeze(2).broadcast_to([128, 8, 64]), ALU.mult), [o, ssum], [o])
        z = zab[b]
        V(P, lambda e, z=z: e.tensor_tensor(tg[:], z[:, 0:8], dtb[:], ALU.add), [z, dtb], [tg])
        A_(P, lambda e: e.activation(tg[:], tg[:], AF.Exp), [tg], [tg])
        A_(P, lambda e: e.activation(tg[:], tg[:], AF.Ln, bias=K.onec[:, 0:1], scale=1.0), [tg, K.onec], [tg])
        V(P, lambda e: e.tensor_tensor(tg[:], tg[:], negA[:], ALU.mult), [tg, negA], [tg])
        A_(P, lambda e, z=z: e.activation(tb[:], z[:, 8:16], AF.Sigmoid), [z], [tb])
        for gb, src in ((0, tg), (1, tb)):
            V(P, lambda e, o=o, gb=gb, src=src: e.tensor_copy(
                o[:, 768:784].rearrange("p (d hh gb pr) -> p d hh gb pr", d=2, hh=2, gb=2)[:, :, :, gb, :],
                src[:].rearrange("p (d pr hh) -> p d hh pr", d=2, pr=2)), [src], [o])
        P.dma("sync", K.GQ[i * 128:(i + 1) * 128, :], o[:], reads=[o])
    P.reset(m0)


def phase_gdn_scan(K, l, d):
    P = K.P
    m0 = P.mark()
    tri = P.tile("tri", [128])
    tris = P.tile("tris", [128])
    blk = P.tile("blk", [128])
    P.dma("sync", tri[:], K.cin["tri_f" if d == 0 else "tri_b"], writes=[tri])
    P.dma("sync", tris[:], K.cin["tri_fs" if d == 0 else "tri_bs"], writes=[tris])
    P.dma("sync", blk[:], K.cin["blk"], writes=[blk])
    ident, ones = K.ident, K.ones
    S = [P.tile("S%d" % p, [64]) for p in range(2)]
    for p in range(2):
        V(P, lambda e, p=p: e.memset(S[p][:], 0.0), [], [S[p]])
    NB = 2
    qkv = [P.tile("qkv%d" % b, [3, 2, 64]) for b in range(NB)]
    gb = [P.tile("gb%d" % b, [2, 2]) for b in range(NB)]
    ot = [P.tile("ogo%d" % b, [2, 64]) for b in range(NB)]
    def pt(name, free):
        return [P.tile("%s%d" % (name, p), free) for p in range(2)]
    gc = P.tile("gc", [2]); egc = P.tile("egc", [2]); glt = P.tile("glt", [2]); eglt = P.tile("eglt", [2]); ekl = P.tile("ekl", [2]); nbeta = P.tile("nbeta", [2])
    dg = pt("dg", [256]); rows = pt("rows", [256]); E = pt("E", [128]); Dm = pt("Dm", [128]); Dms = pt("Dms", [128])
    kT = pt("kT", [128]); qT = pt("qT", [128]); NTm = pt("NTm", [128]); Nm = pt("Nm", [128]); PT2 = pt("PT2", [128]); P2 = pt("P2", [128])
    RT = pt("RT", [128]); aT = pt("aT", [128]); MTb = pt("MTb", [128]); kg0 = pt("kg0", [128]); kgl = pt("kgl", [128]); qgb = pt("qgb", [128])
    wT = pt("wT", [128]); qgT = pt("qgT", [128]); u = pt("u", [64]); vn = pt("vn", [64])
    for p in range(2):
        for t in (kg0[p], kgl[p], qgb[p]):
            G_(P, lambda e, t=t: e.memset(t[:], 0.0), [], [t])
    psn = [0]

    def nps():
        psn[0] = (psn[0] + 1) % 8
        return K.ps[psn[0]]

    ctx_ch = list(range(0, TC // 64))
    lat_ch = list(range(TC // 64, TT // 64))
    order = ctx_ch + lat_ch if d == 0 else ctx_ch[::-1] + lat_ch[::-1]
    import os
    if os.environ.get("GDN_NCH"):
        order = order[:int(os.environ["GDN_NCH"])]

    def loads(ci):
        c = order[ci]
        b = ci % NB
        r0 = c * 64
        for hh in range(2):
            src = K.GQ[r0:r0 + 64, 0:768].rearrange("r (t pr hh e) -> r t pr hh e", t=3, pr=2, hh=2)[:, :, :, hh, :]
            P.dma("sync", qkv[b][hh * 64:(hh + 1) * 64, :, :, :], src, writes=[qkv[b]])
            c0 = 768 + d * 8 + hh * 4
            P.dma("sync", gb[b][hh * 64:(hh + 1) * 64, :, :], K.GQ[r0:r0 + 64, c0:c0 + 4].rearrange("r (a b) -> r a b", a=2), writes=[gb[b]])
    loads(0)
    for ci in range(len(order)):
        c = order[ci]
        b = ci % NB
        if ci + 1 < len(order):
            loads(ci + 1)
        Q = qkv[b]
        g = gb[b]
        ps = nps()
        MM(P, ps, ps[:, 0:2], tri[:], g[:, 0, :], [tri, g])
        MM(P, ps, ps[:, 2:4], blk[:], g[:, 0, :], [blk, g])
        V(P, lambda e, ps=ps: e.tensor_copy(gc[:], ps[:, 0:2]), [ps], [gc])
        V(P, lambda e, ps=ps: e.tensor_copy(glt[:], ps[:, 2:4]), [ps], [glt])
        A_(P, lambda e: e.activation(egc[:], gc[:], AF.Exp), [gc], [egc])
        A_(P, lambda e: e.activation(eglt[:], glt[:], AF.Exp), [glt], [eglt])
        V(P, lambda e: e.tensor_tensor(ekl[:], glt[:], gc[:], ALU.subtract), [glt, gc], [ekl])
        A_(P, lambda e: e.activation(ekl[:], ekl[:], AF.Exp), [ekl], [ekl])
        V(P, lambda e, g=g: e.tensor_scalar(nbeta[:], g[:, 1, :], -1.0, 1.0, ALU.mult, ALU.mult), [g], [nbeta])
        CUT = int(os.environ.get("CUT", "9"))
        for p in range(2):
            if CUT < 2:
                break
            kn = Q[:, 1, p, :]
            qn = Q[:, 0, p, :]
            vv = Q[:, 2, p, :]
            V(P, lambda e, p=p: e.tensor_scalar(dg[p][:, 0:128], ident[:], gc[:, p:p + 1], 1.0, ALU.mult, ALU.mult), [ident, gc], [dg[p]])
            G_(P, lambda e, p=p, g=g: e.tensor_scalar(dg[p][:, 128:256], ident[:], g[:, 1, p:p + 1], 1.0, ALU.mult, ALU.mult), [ident, g], [dg[p]])
            ps = nps()
            psb_ = nps()
            MM(P, ps, ps[:, 0:128], ones[:], dg[p][:, 0:128], [ones, dg[p]])
            MM(P, psb_, psb_[:, 0:128], ones[:], dg[p][:, 128:256], [ones, dg[p]])
            A_(P, lambda e, p=p, ps=psb_: e.copy(rows[p][:, 128:256], ps[:, 0:128]), [psb_], [rows[p]])
            V(P, lambda e, p=p, ps=ps: e.tensor_scalar(E[p][:], ps[:, 0:128], gc[:, p:p + 1], 0.0, ALU.subtract, ALU.min), [ps, gc], [E[p]])
            A_(P, lambda e, p=p: e.activation(E[p][:], E[p][:], AF.Exp), [E[p]], [E[p]])
            G_(P, lambda e, p=p: e.tensor_tensor(Dm[p][:], E[p][:], tri[:], ALU.mult), [E[p], tri], [Dm[p]])
            G_(P, lambda e, p=p: e.tensor_tensor(Dms[p][:], E[p][:], tris[:], ALU.mult), [E[p], tris], [Dms[p]])
            if CUT < 3:
                continue
            ps = nps()
            psb_ = nps()
            TR(P, ps, ps[0:64, 0:128], kn, ident[:], [Q, ident])
            TR(P, psb_, psb_[0:64, 0:128], qn, ident[:], [Q, ident])
            A_(P, lambda e, p=p, ps=ps: e.copy(kT[p][0:64, :], ps[0:64, 0:128]), [ps], [kT[p]])
            V(P, lambda e, p=p, ps=psb_: e.tensor_copy(qT[p][0:64, :], ps[0:64, 0:128]), [psb_], [qT[p]])
            ps = nps()
            MM(P, ps, ps[:, 0:128], kT[p][0:64, :], kT[p][0:64, :], [kT[p]])
            MM(P, ps, ps[:, 128:256], kT[p][0:64, :], qT[p][0:64, :], [kT[p], qT[p]])
            V(P, lambda e, p=p, ps=ps: e.scalar_tensor_tensor(NTm[p][:], ps[:, 0:128], nbeta[:, p:p + 1], Dms[p][:], ALU.mult, ALU.mult),
              [ps, nbeta, Dms[p]], [NTm[p]])
            V(P, lambda e, p=p, ps=ps: e.tensor_tensor(aT[p][:], ps[:, 128:256], Dm[p][:], ALU.mult), [ps, Dm[p]], [aT[p]])
            if CUT < 4:
                continue
            ps = nps()
            TR(P, ps, ps[:, 0:128], NTm[p][:], ident[:], [NTm[p], ident])
            A_(P, lambda e, p=p, ps=ps: e.copy(Nm[p][:], ps[:, 0:128]), [ps], [Nm[p]])
            G_(P, lambda e, p=p: e.tensor_tensor(RT[p][:], NTm[p][:], ident[:], ALU.add), [NTm[p], ident], [RT[p]])
            Pk, PTk = Nm[p], NTm[p]
            Pn_t, PTn_t = P2[p], PT2[p]
            for k in range(5):
                ps = nps()
                MM(P, ps, ps[:, 0:128], PTk[:], Pk[:], [PTk, Pk])
                if k < 4:
                    psb_ = nps()
                    MM(P, psb_, psb_[:, 0:128], Pk[:], PTk[:], [PTk, Pk])
                A_(P, lambda e, ps=ps, t=Pn_t: e.copy(t[:], ps[:, 0:128]), [ps], [Pn_t])
                if k < 4:
                    V(P, lambda e, ps=psb_, t=PTn_t: e.tensor_copy(t[:], ps[:, 0:128]), [psb_], [PTn_t])
                ps2 = nps()
                MM(P, ps2, ps2[:, 0:128], Pn_t[:], RT[p][:], [Pn_t, RT[p]])
                V(P, lambda e, ps2=ps2, p=p: e.tensor_tensor(RT[p][:], ps2[:, 0:128], RT[p][:], ALU.add), [ps2, RT[p]], [RT[p]])
                Pk, PTk, Pn_t, PTn_t = Pn_t, PTn_t, Pk, PTk
            if CUT < 5:
                continue
            G_(P, lambda e, p=p: e.tensor_tensor(MTb[p][:], RT[p][:], rows[p][:, 128:256], ALU.mult), [RT[p], rows[p]], [MTb[p]])
            for hh in range(2):
                r = slice(hh * 64, hh * 64 + 64)
                V(P, lambda e, p=p, r=r, kn=kn: e.tensor_scalar(kg0[p][r, r], kn[r, :], egc[r, p:p + 1], 1.0, ALU.mult, ALU.mult), [Q, egc], [kg0[p]])
                G_(P, lambda e, p=p, r=r, kn=kn: e.tensor_scalar(kgl[p][r, r], kn[r, :], ekl[r, p:p + 1], 1.0, ALU.mult, ALU.mult), [Q, ekl], [kgl[p]])
                V(P, lambda e, p=p, r=r, qn=qn: e.tensor_scalar(qgb[p][r, r], qn[r, :], egc[r, p:p + 1], 1.0, ALU.mult, ALU.mult), [Q, egc], [qgb[p]])
            ps = nps()
            psb_ = nps()
            MM(P, ps, ps[:, 0:64], MTb[p][:], vv, [MTb[p], Q])
            MM(P, psb_, psb_[:, 0:128], kg0[p][:], MTb[p][:], [kg0[p], MTb[p]])
            TR(P, ps, ps[:, 256:384], qgb[p][:], ident[:], [qgb[p], ident])
            A_(P, lambda e, p=p, ps=ps: e.copy(u[p][:], ps[:, 0:64]), [ps], [u[p]])
            V(P, lambda e, p=p, ps=psb_: e.tensor_copy(wT[p][:], ps[:, 0:128]), [psb_], [wT[p]])
            A_(P, lambda e, p=p, ps=ps: e.copy(qgT[p][:], ps[:, 256:384]), [ps], [qgT[p]])
            if CUT < 6:
                continue
            ps = nps()
            MM(P, ps, ps[:, 0:64], wT[p][:], S[p][:], [wT[p], S[p]])
            V(P, lambda e, p=p, ps=ps: e.tensor_tensor(vn[p][:], u[p][:], ps[:, 0:64], ALU.subtract), [u[p], ps], [vn[p]])
            ps = nps()
            psb_ = nps()
            MM(P, ps, ps[:, 0:64], qgT[p][:], S[p][:], [qgT[p], S[p]], start=True, stop=False)
            MM(P, ps, ps[:, 0:64], aT[p][:], vn[p][:], [aT[p], vn[p]], start=False, stop=True)
            MM(P, psb_, psb_[:, 0:64], kgl[p][:], vn[p][:], [kgl[p], vn[p]])
            A_(P, lambda e, p=p, ps=ps, b=b: e.copy(ot[b][:, p, :], ps[:, 0:64]), [ps], [ot[b]])
            V(P, lambda e, p=p, ps=psb_: e.scalar_tensor_tensor(S[p][:], S[p][:], eglt[:, p:p + 1], ps[:, 0:64], ALU.mult, ALU.add),
              [S[p], eglt, psb_], [S[p]])
        r0 = c * 64
        for hh in range(2):
            dst = K.OG[d][r0:r0 + 64, :].rearrange("r (pr hh e) -> r pr hh e", pr=2, hh=2)[:, :, hh, :]
            P.dma("sync", dst, ot[b][hh * 64:(hh + 1) * 64, :, :], reads=[ot[b]])
    P.reset(m0)


def phase_gdn_fin(K, l):
    P = K.P
    m0 = P.mark()
    nw = P.tile("gnw", [64])
    P.dma("sync", nw[:], K.gdn_norm_w[l].partition_broadcast(128), writes=[nw])
    o0 = [P.tile("o0_%d" % b, [256]) for b in range(2)]
    o1 = [P.tile("o1_%d" % b, [256]) for b in range(2)]
    zt = [P.tile("zt%d" % b, [256]) for b in range(2)]
    sq = P.tile("sq", [256]); ss = P.tile("ss", [4])

    def loads(i):
        b = i % 2
        rs = slice(i * 128, (i + 1) * 128)
        P.dma("sync", o0[b][:], K.OG[0][rs, :], writes=[o0[b]])
        P.dma("sync", o1[b][:], K.OG[1][rs, :], writes=[o1[b]])
        P.dma("sync", zt[b][:], K.Pj[rs, 768:1024], writes=[zt[b]])
    loads(0)
    for i in range(NT):
        b = i % 2
        if i + 1 < NT:
            loads(i + 1)
        o = o0[b]
        V(P, lambda e, o=o, b=b: e.tensor_tensor(o[:], o[:], o1[b][:], ALU.add), [o, o1[b]], [o])
        head_rms_gate(K, o, zt[b], nw, sq, ss, 4, 64, 1.0)
        P.dma("sync", K.O[i * 128:(i + 1) * 128, 0:256], o[:], reads=[o])
    P.reset(m0)


def head_rms_gate(K, o, zt, nw, sq, ss, nh, hd, mult):
    P = K.P
    n = nh * hd
    G_(P, lambda e: e.tensor_tensor(sq[:, 0:n], o[:, 0:n], o[:, 0:n], ALU.mult), [o], [sq])
    V(P, lambda e: e.tensor_reduce(ss[:, 0:nh], sq[:, 0:n].rearrange("p (g d) -> p g d", g=nh), AX.X, ALU.add), [sq], [ss])
    A_(P, lambda e: e.activation(ss[:, 0:nh], ss[:, 0:nh], AF.Sqrt, bias=K.epsc[:, 0:1], scale=1.0 / hd), [ss, K.epsc], [ss])
    V(P, lambda e: e.reciprocal(ss[:, 0:nh], ss[:, 0:nh]), [ss], [ss])
    if mult != 1.0:
        V(P, lambda e: e.tensor_scalar(ss[:, 0:nh], ss[:, 0:nh], mult, 1.0, ALU.mult, ALU.mult), [ss], [ss])
    o3 = o[:, 0:n].rearrange("p (g d) -> p g d", g=nh)
    V(P, lambda e: e.tensor_tensor(o3, o3, ss[:, 0:nh].unsqueeze(2).broadcast_to([128, nh, hd]), ALU.mult), [o, ss], [o])
    G_(P, lambda e: e.tensor_tensor(o3, o3, nw[:, 0:hd].unsqueeze(1).broadcast_to([128, nh, hd]), ALU.mult), [o, nw], [o])
    if zt is not None:
        A_(P, lambda e: e.activation(zt[:, 0:n], zt[:, 0:n], AF.Silu), [zt], [zt])
        V(P, lambda e: e.tensor_tensor(o[:, 0:n], o[:, 0:n], zt[:, 0:n], ALU.mult), [o, zt], [o])


def phase_wout(K, l):
    P = K.P
    m0 = P.mark()
    W = P.tile("wout", [8, D], BF16)
    for c in range(8):
        P.dma("gpsimd", W[:, c, :], K.w_out[l][c * 128:(c + 1) * 128, :], writes=[W])
    ot = [P.tile("wo_o%d" % b, [D]) for b in range(2)]
    xt = [P.tile("wo_x%d" % b, [D]) for b in range(2)]
    ob = P.tile("wo_ob", [D], BF16)
    oT = P.tile("wo_oT", [8, 128], BF16)
    tmp = P.tile("wo_tmp", [512])
    psb_t = K.ps[7]
    psb = psb_t.ap[:, 0:512].bitcast(BF16).rearrange("p (c t) -> p c t", c=8)

    def loads(i):
        b = i % 2
        rs = slice(i * 128, (i + 1) * 128)
        P.dma("sync", ot[b][:], K.O[rs, :], writes=[ot[b]])
        P.dma("sync", xt[b][:], K.X[rs, :], writes=[xt[b]])
    loads(0)
    for i in range(NT):
        b = i % 2
        if i + 1 < NT:
            loads(i + 1)
        s = 1 if i < 2 else 0
        A_(P, lambda e, b=b: e.copy(ob[:], ot[b][:]), [ot[b]], [ob])
        for c in range(8):
            TR(P, psb_t, psb[:, c, :], ob[:, c * 128:(c + 1) * 128], K.identb[:], [ob, K.identb])
        V(P, lambda e: e.tensor_copy(oT[:], psb), [psb_t], [oT])
        for nb in range(2):
            ps = K.ps[nb]
            for c in range(8):
                MM(P, ps, ps[:, 0:512], oT[:, c, :], W[:, c, nb * 512:(nb + 1) * 512], [oT, W], start=(c == 0), stop=(c == 7))
            G1 = K.G[0][s]
            V(P, lambda e, ps=ps, nb=nb, G1=G1: e.tensor_tensor(tmp[:], ps[:, 0:512], G1[:, nb * 512:(nb + 1) * 512], ALU.mult), [ps, G1], [tmp])
            G_(P, lambda e, nb=nb, b=b: e.tensor_tensor(xt[b][:, nb * 512:(nb + 1) * 512], xt[b][:, nb * 512:(nb + 1) * 512], tmp[:], ALU.add), [xt[b], tmp], [xt[b]])
        P.dma("sync", K.X[i * 128:(i + 1) * 128, :], xt[b][:], reads=[xt[b]])
    P.reset(m0)


def phase_mlp(K, l, last):
    P = K.P
    m0 = P.mark()
    W1 = P.tile("w1", [8, 4 * D], BF16)
    W2 = P.tile("w2", [32, D], BF16)
    for c in range(8):
        P.dma("gpsimd", W1[:, c, :], K.mlp_w1[l][c * 128:(c + 1) * 128, :], writes=[W1])
    for c in range(8):
        P.dma("gpsimd", W2[:, c * 4:(c + 1) * 4, :], K.mlp_w2[l][c * 512:(c + 1) * 512, :].rearrange("(f p) n -> p f n", p=128), writes=[W2])
    GT = 2
    NG = NT // GT
    xt = [[P.tile("ml_x%d_%d" % (b, t), [D]) for t in range(GT)] for b in range(2)]
    xn = P.tile("ml_xn", [D], BF16)
    hT = P.tile("ml_hT", [8, GT * 128], BF16)
    hid = P.tile("ml_hid", [32, GT * 128], BF16)
    K.tmod = P.tile("ml_tmod", [8, 128])
    scr = T(K.tmod.name, K.tmod[:].rearrange("p a b -> p (a b)"))
    ss = P.tile("ml_ss", [1]); rs = P.tile("ml_rs", [1])
    rl = [P.tile("ml_rl%d" % b, [GT * 128]) for b in range(2)]
    tmp = P.tile("ml_tmp", [512])
    B2 = T(K.fm.name, K.fm[:, 2, :, :])

    def loads(g):
        b = g % 2
        for t in range(GT):
            i = g * GT + t
            P.dma("sync", xt[b][t][:], K.X[i * 128:(i + 1) * 128, :], writes=[xt[b][t]])
    loads(0)
    for g in range(NG):
        b = g % 2
        if g + 1 < NG:
            loads(g + 1)
        s = 1 if g == 0 else 0
        for t in range(GT):
            rms_tile(K, xt[b][t], xn, scr, ss, rs)
            transpose_mod(K, xn, hT, K.A2, B2, s, K.ps[7], col0=t * 128)
        for fb in range(32):
            ps = K.ps[fb % 4]
            for c in range(8):
                MM(P, ps, ps[:, 0:GT * 128], W1[:, c, fb * 128:(fb + 1) * 128], hT[:, c, :], [W1, hT], start=(c == 0), stop=(c == 7))
            r = rl[fb % 2]
            A_(P, lambda e, ps=ps, r=r: e.activation(r[:], ps[:, 0:GT * 128], AF.Relu), [ps], [r])
            G_(P, lambda e, r=r, fb=fb: e.tensor_tensor(hid[:, fb, :], r[:], r[:], ALU.mult), [r], [hid])
        G2 = K.G[1][s]
        for t in range(GT):
            i = g * GT + t
            x = xt[b][t]
            for nb in range(2):
                ps = K.ps[4 + nb]
                for fb in range(32):
                    MM(P, ps, ps[:, 0:512], hid[:, fb, t * 128:(t + 1) * 128], W2[:, fb, nb * 512:(nb + 1) * 512], [hid, W2], start=(fb == 0), stop=(fb == 31))
                V(P, lambda e, ps=ps, nb=nb, G2=G2: e.tensor_tensor(tmp[:], ps[:, 0:512], G2[:, nb * 512:(nb + 1) * 512], ALU.mult), [ps, G2], [tmp])
                V(P, lambda e, nb=nb, x=x: e.tensor_tensor(x[:, nb * 512:(nb + 1) * 512], x[:, nb * 512:(nb + 1) * 512], tmp[:], ALU.add), [x, tmp], [x])
            if last:
                if i >= 2:
                    K.final_ops.append(P.dma("sync", K.out[(i - 2) * 128:(i - 1) * 128, :], x[:], reads=[x]))
            else:
                P.dma("sync", K.X[i * 128:(i + 1) * 128, :], x[:], reads=[x])
    P.reset(m0)


def rope_tables():
    n_freq = 8
    inv = 10000.0 ** (-np.arange(n_freq, dtype=np.float64) / n_freq)
    t = np.arange(TL)
    ang_r = (t // 64)[:, None] * inv
    ang_c = (t % 64)[:, None] * inv
    C = np.ones((TT, 32), np.float64)
    S = np.zeros((TT, 32), np.float64)
    C[TC:, 0:8] = np.cos(ang_r); C[TC:, 8:16] = np.cos(ang_r)
    C[TC:, 16:24] = np.cos(ang_c); C[TC:, 24:32] = np.cos(ang_c)
    S[TC:, 0:8] = -np.sin(ang_r); S[TC:, 8:16] = np.sin(ang_r)
    S[TC:, 16:24] = -np.sin(ang_c); S[TC:, 24:32] = np.sin(ang_c)
    return C.astype(np.float32), S.astype(np.float32)


def phase_attn(K, l):
    P = K.P
    m0 = P.mark()
    lam_init = 0.8 - 0.6 * math.exp(-0.3 * l)
    qT = P.tile("qT", [4, TT], BF16, parts=64)
    kT = P.tile("kT", [4, TT], BF16, parts=64)
    vaug = P.tile("vaug", [NT, 4, 65], BF16)
    nw = P.tile("nwqk", [16, 32])
    subw = P.tile("subw", [64])
    lamt = P.tile("lamt", [128])
    lamv = P.tile("lamv", [4])
    for g in range(16):
        src = K.da_q_norm[l] if g < 8 else K.da_k_norm[l]
        P.dma("sync", nw[:, g, :], src.partition_broadcast(128), writes=[nw])
    P.dma("sync", subw[:], K.da_subln[l].partition_broadcast(128), writes=[subw])
    P.dma("sync", lamt[:], K.da_lam[l].rearrange("a b -> (a b)").partition_broadcast(128), writes=[lamt])
    V(P, lambda e: e.memset(vaug[:], 1.0), [], [vaug])
    V(P, lambda e: e.tensor_tensor(lamt[:, 0:32], lamt[:, 0:32], lamt[:, 32:64], ALU.mult), [lamt], [lamt])
    V(P, lambda e: e.tensor_tensor(lamt[:, 64:96], lamt[:, 64:96], lamt[:, 96:128], ALU.mult), [lamt], [lamt])
    V(P, lambda e: e.tensor_reduce(lamv[:, 0:1], lamt[:, 0:32], AX.X, ALU.add), [lamt], [lamv])
    V(P, lambda e: e.tensor_reduce(lamv[:, 1:2], lamt[:, 64:96], AX.X, ALU.add), [lamt], [lamv])
    A_(P, lambda e: e.activation(lamv[:, 0:2], lamv[:, 0:2], AF.Exp), [lamv], [lamv])
    V(P, lambda e: e.tensor_tensor(lamv[:, 2:3], lamv[:, 1:2], lamv[:, 0:1], ALU.subtract), [lamv], [lamv])
    V(P, lambda e: e.tensor_scalar(lamv[:, 2:3], lamv[:, 2:3], -lam_init, 1.0, ALU.add, ALU.mult), [lamv], [lamv])
    qk = [P.tile("qk%d" % b, [512]) for b in range(2)]
    vt = [P.tile("vt%d" % b, [256]) for b in range(2)]
    rc = [P.tile("rc%d" % b, [32]) for b in range(2)]
    rsn = [P.tile("rsn%d" % b, [32]) for b in range(2)]
    sq = P.tile("asq", [512]); ss = P.tile("ass", [16]); t1 = P.tile("at1", [512]); t2 = P.tile("at2", [512])
    qkb = P.tile("qkb", [512], BF16)
    pa, pb = K.ps[6], K.ps[7]
    pav = pa.ap[:, 0:256].bitcast(BF16).rearrange("p (c t) -> p c t", c=4)
    pbv = pb.ap[:, 0:256].bitcast(BF16).rearrange("p (c t) -> p c t", c=4)

    def loads(i):
        b = i % 2
        rs = slice(i * 128, (i + 1) * 128)
        P.dma("sync", qk[b][:], K.Pj[rs, C_DA:C_DA + 512], writes=[qk[b]])
        P.dma("sync", vt[b][:], K.Pj[rs, C_DA + 512:C_DA + 768], writes=[vt[b]])
        P.dma("sync", rc[b][:], K.cin["ropeC"][rs, :], writes=[rc[b]])
        P.dma("sync", rsn[b][:], K.cin["ropeS"][rs, :], writes=[rsn[b]])
    loads(0)
    for i in range(NT):
        b = i % 2
        if i + 1 < NT:
            loads(i + 1)
        x = qk[b]
        G_(P, lambda e, x=x: e.tensor_tensor(sq[:], x[:], x[:], ALU.mult), [x], [sq])
        V(P, lambda e: e.tensor_reduce(ss[:], sq[:].rearrange("p (g d) -> p g d", g=16), AX.X, ALU.add), [sq], [ss])
        A_(P, lambda e: e.activation(ss[:], ss[:], AF.Sqrt, bias=K.epsc[:, 0:1], scale=1.0 / 32), [ss, K.epsc], [ss])
        V(P, lambda e: e.reciprocal(ss[:], ss[:]), [ss], [ss])
        x3 = x[:].rearrange("p (g d) -> p g d", g=16)
        V(P, lambda e, x3=x3: e.tensor_tensor(x3, x3, ss[:].unsqueeze(2).broadcast_to([128, 16, 32]), ALU.mult), [x, ss], [x])
        G_(P, lambda e, x3=x3: e.tensor_tensor(x3, x3, nw[:], ALU.mult), [x, nw], [x])
        cb = rc[b]; sb = rsn[b]
        V(P, lambda e, x3=x3, cb=cb: e.tensor_tensor(t1[:].rearrange("p (g d) -> p g d", g=16), x3, cb[:].unsqueeze(1).broadcast_to([128, 16, 32]), ALU.mult), [x, cb], [t1])
        x5 = x[:].rearrange("p (g r h e) -> p g r h e", g=16, r=2, h=2)
        t5 = t2[:].rearrange("p (g r h e) -> p g r h e", g=16, r=2, h=2)
        s4 = sb[:].rearrange("p (r h e) -> p r h e", r=2, h=2)
        for h in range(2):
            G_(P, lambda e, h=h, x5=x5, t5=t5, s4=s4: e.tensor_tensor(t5[:, :, :, h, :], x5[:, :, :, 1 - h, :],
                                                                   s4[:, :, h, :].unsqueeze(1).broadcast_to([128, 16, 2, 8]), ALU.mult), [x, sb], [t2])
        V(P, lambda e: e.tensor_tensor(qkb[:], t1[:], t2[:], ALU.add), [t1, t2], [qkb])
        for h in range(4):
            TR(P, pa, pav[0:64, h, :], qkb[:, h * 64:(h + 1) * 64], K.identb[:], [qkb, K.identb])
        for h in range(4):
            TR(P, pb, pbv[0:64, h, :], qkb[:, 256 + h * 64:256 + (h + 1) * 64], K.identb[:], [qkb, K.identb])
        V(P, lambda e, i=i: e.tensor_copy(qT[:, :, i * 128:(i + 1) * 128], pav[0:64, :, :]), [pa], [qT])
        A_(P, lambda e, i=i: e.copy(kT[:, :, i * 128:(i + 1) * 128], pbv[0:64, :, :]), [pb], [kT])
        G_(P, lambda e, i=i, b=b: e.tensor_copy(vaug[:, i, :, 0:64], vt[b][:].rearrange("p (h e) -> p h e", h=4)), [vt[b]], [vaug])
    scale = 32 ** -0.5
    pT = [P.tile("pT%d" % b, [512], BF16) for b in range(3)]
    osb = [P.tile("osb%d" % j, [512], parts=65) for j in range(2)]
    obuf = P.tile("obuf", [4, 256])
    o1t = P.tile("o1t", [64]); rz = P.tile("rz", [2])
    asq = P.tile("bsq", [256]); ass = P.tile("bss", [4])
    blocks = [(0, 256, [0, 1])] + [(TC + qb * 512, 512, list(range(NT))) for qb in range(TL // 512)]
    n = 0
    for (q0, nq, kts) in blocks:
        nqt = nq // 128
        for h in range(4):
            acc = [K.ps[0], K.ps[1]]
            for j in range(2):
                for ki, kt in enumerate(kts):
                    sp = K.ps[2 + n % 3]
                    pt_ = pT[n % 3]
                    n += 1
                    MM(P, sp, sp[:, 0:nq], kT[32 * j:32 * j + 32, h, kt * 128:(kt + 1) * 128], qT[32 * j:32 * j + 32, h, q0:q0 + nq], [kT, qT])
                    A_(P, lambda e, sp=sp, pt_=pt_, nq=nq: e.activation(pt_[:, 0:nq], sp[:, 0:nq], AF.Exp, scale=scale), [sp], [pt_])
                    MM(P, acc[j], acc[j][0:65, 0:nq], vaug[:, kt, h, :], pt_[:, 0:nq], [vaug, pt_], start=(ki == 0), stop=(ki == len(kts) - 1))
                V(P, lambda e, j=j, nq=nq, a=acc[j]: e.tensor_copy(osb[j][:, 0:nq], a[0:65, 0:nq]), [acc[j]], [osb[j]])
            for qt in range(nqt):
                tp = K.ps[5]
                for j in range(2):
                    TR(P, tp, tp[:, j * 128:j * 128 + 65], osb[j][:, qt * 128:(qt + 1) * 128], K.ident[0:65, 0:65], [osb[j], K.ident])
                V(P, lambda e, tp=tp: e.reciprocal(rz[:].rearrange("p (a b) -> p a b", a=2), tp[:, 0:256].rearrange("p (a b) -> p a b", a=2)[:, :, 64:65]), [tp], [rz])
                V(P, lambda e: e.tensor_tensor(rz[:, 1:2], rz[:, 1:2], lamv[:, 2:3], ALU.mult), [rz, lamv], [rz])
                V(P, lambda e, tp=tp, qt=qt, h=h: e.tensor_scalar(obuf[:, qt, h * 64:(h + 1) * 64], tp[:, 0:64], rz[:, 0:1], 1.0, ALU.mult, ALU.mult), [tp, rz], [obuf])
                V(P, lambda e, tp=tp: e.tensor_scalar(o1t[:], tp[:, 128:192], rz[:, 1:2], 1.0, ALU.mult, ALU.mult), [tp, rz], [o1t])
                G_(P, lambda e, qt=qt, h=h: e.tensor_tensor(obuf[:, qt, h * 64:(h + 1) * 64], obuf[:, qt, h * 64:(h + 1) * 64], o1t[:], ALU.add), [obuf, o1t], [obuf])
        for qt in range(nqt):
            ov = T(obuf.name, obuf[:, qt, :])
            head_rms_gate(K, ov, None, subw, asq, ass, 4, 64, 1.0 - lam_init)
            r0 = q0 + qt * 128
            P.dma("sync", K.O[r0:r0 + 128, 768:1024], obuf[:, qt, :], reads=[obuf])
    P.reset(m0)


EXTRA_D = {"attn": phase_attn}


HQW = 1536


def hg_consts(c):
    j = np.arange(128)[:, None]
    i = np.arange(128)[None, :]
    same = (j // 64) == (i // 64)
    jl = j % 64
    c["mrel_f"] = (same * ((j <= i).astype(np.float32) - (jl <= 31).astype(np.float32))).astype(np.float32)
    c["mrel_b"] = (same * ((j >= i).astype(np.float32) - (jl >= 32).astype(np.float32))).astype(np.float32)


def phase_hg_pre(K, l):
    P = K.P
    m0 = P.mark()
    raw = P.tile("lbraw", [4, 256])
    P.dma("sync", raw[:], K.hg_lb_raw.rearrange("a b -> (a b)").partition_broadcast(128), writes=[raw])
    lb = P.tile("lb", [256]); oml = P.tile("oml", [256]); den = P.tile("lbden", [256])
    A_(P, lambda e: e.activation(raw[:], raw[:], AF.Exp), [raw], [raw])
    V(P, lambda e: e.tensor_tensor(den[:], raw[:, 0, :], raw[:, 1, :], ALU.add), [raw], [den])
    V(P, lambda e: e.tensor_tensor(den[:], den[:], raw[:, 2, :], ALU.add), [raw, den], [den])
    V(P, lambda e: e.tensor_tensor(den[:], den[:], raw[:, 3, :], ALU.add), [raw, den], [den])
    V(P, lambda e: e.reciprocal(den[:], den[:]), [den], [den])
    V(P, lambda e: e.memset(lb[:], 0.0), [], [lb])
    for ll in range(1, l + 1):
        V(P, lambda e, ll=ll: e.tensor_tensor(lb[:], lb[:], raw[:, ll, :], ALU.add), [lb, raw], [lb])
    V(P, lambda e: e.tensor_tensor(lb[:], lb[:], den[:], ALU.mult), [lb, den], [lb])
    V(P, lambda e: e.tensor_scalar(oml[:], lb[:], -1.0, 1.0, ALU.mult, ALU.add), [lb], [oml])
    xin = [P.tile("hgx%d" % b, [1024]) for b in range(2)]
    ho = [P.tile("hgo%d" % b, [HQW]) for b in range(2)]
    sg = P.tile("hgs", [512]); tt = P.tile("hgt", [512])

    def loads(i):
        b = i % 2
        P.dma("sync", xin[b][:], K.Pj[i * 128:(i + 1) * 128, C_HG:C_HG + 1024], writes=[xin[b]])
    loads(0)
    for i in range(NT):
        b = i % 2
        if i + 1 < NT:
            loads(i + 1)
        x = xin[b]; o = ho[b]
        A_(P, lambda e, x=x, o=o: e.activation(o[:, 0:256], x[:, 0:256], AF.Silu), [x], [o])
        G_(P, lambda e, x=x, o=o: e.tensor_copy(o[:, 256:512], x[:, 256:512]), [x], [o])
        A_(P, lambda e, x=x: e.activation(sg[:], x[:, 512:1024], AF.Sigmoid), [x], [sg])
        s3 = sg[:].rearrange("p (d c) -> p d c", d=2)
        t3 = tt[:].rearrange("p (d c) -> p d c", d=2)
        V(P, lambda e, s3=s3, t3=t3: e.tensor_tensor(t3, s3, oml[:].unsqueeze(1).broadcast_to([128, 2, 256]), ALU.mult), [sg, oml], [tt])
        o4 = o[:, 512:1536].rearrange("p (d k c) -> p d k c", d=2, k=2)
        G_(P, lambda e, s3=s3, t3=t3: e.tensor_tensor(s3, t3, lb[:].unsqueeze(1).broadcast_to([128, 2, 256]), ALU.add), [tt, lb], [sg])
        A_(P, lambda e, o4=o4, s3=s3, o=o: e.activation(o4[:, :, 0, :], s3, AF.Ln), [sg], [o])
        V(P, lambda e, o4=o4, t3=t3, o=o: e.scalar_tensor_tensor(o4[:, :, 1, :], t3, -1.0, oml[:].unsqueeze(1).broadcast_to([128, 2, 256]), ALU.mult, ALU.add), [tt, oml], [o])
        P.dma("sync", K.HQ[i * 128:(i + 1) * 128, :], o[:], reads=[o])
    P.reset(m0)


def phase_hg_scan(K, l, d):
    P = K.P
    m0 = P.mark()
    tri = P.tile("htri", [128]); tail = P.tile("htail", [128]); mrel = P.tile("hmrel", [128])
    trim = P.tile("htrim", [128], U8)
    P.dma("sync", tri[:], K.cin["tri_f" if d == 0 else "tri_b"], writes=[tri])
    P.dma("sync", tail[:], K.cin["tri_bs" if d == 0 else "tri_fs"], writes=[tail])
    P.dma("sync", mrel[:], K.cin["mrel_f" if d == 0 else "mrel_b"], writes=[mrel])
    V(P, lambda e: e.tensor_copy(trim[:], tri[:]), [tri], [trim])
    zeros = P.tile("hzeros", [128])
    V(P, lambda e: e.memset(zeros[:], 0.0), [], [zeros])
    ident, ones = K.ident, K.ones
    S = [P.tile("hS%d" % p, [64]) for p in range(2)]
    for p in range(2):
        V(P, lambda e, p=p: e.memset(S[p][:], 0.0), [], [S[p]])
    qv = [P.tile("hqv%d" % b, [2, 2, 64]) for b in range(2)]
    fk = [P.tile("hfk%d" % b, [2, 2, 64]) for b in range(2)]
    ot = [P.tile("hot%d" % b, [2, 64]) for b in range(2)]
    ex = [P.tile("hex%d" % k, [128]) for k in range(4)]
    qe = P.tile("hqe", [2, 64]); ke = P.tile("hke", [2, 64])

    def pt(name, free):
        return [P.tile("%s%d" % (name, p), free) for p in range(2)]
    keT = pt("hkeT", [128]); qeT = pt("hqeT", [128]); aT = pt("haT", [128]); qgb = pt("hqgb", [128]); kendb = pt("hkendb", [128])
    lfb = pt("hlfb", [128]); qgT = pt("hqgT", [128]); ege = pt("hege", [1])
    for p in range(2):
        for t in (qgb[p], kendb[p], lfb[p]):
            G_(P, lambda e, t=t: e.memset(t[:], 0.0), [], [t])
    psn = [0]

    def nps():
        psn[0] = (psn[0] + 1) % 8
        return K.ps[psn[0]]
    ctx_ch = list(range(0, TC // 64))
    lat_ch = list(range(TC // 64, TT // 64))
    order = ctx_ch + lat_ch if d == 0 else ctx_ch[::-1] + lat_ch[::-1]
    c0 = 512 + d * 512

    def loads(ci):
        c = order[ci]
        b = ci % 2
        r0 = c * 64
        for hh in range(2):
            src = K.HQ[r0:r0 + 64, 0:512].rearrange("r (t pr hh e) -> r t pr hh e", t=2, pr=2, hh=2)[:, :, :, hh, :]
            P.dma("sync", qv[b][hh * 64:(hh + 1) * 64, :, :, :], src, writes=[qv[b]])
            src = K.HQ[r0:r0 + 64, c0:c0 + 512].rearrange("r (t pr hh e) -> r t pr hh e", t=2, pr=2, hh=2)[:, :, :, hh, :]
            P.dma("sync", fk[b][hh * 64:(hh + 1) * 64, :, :, :], src, writes=[fk[b]])
    loads(0)
    for ci in range(len(order)):
        c = order[ci]
        b = ci % 2
        if ci + 1 < len(order):
            loads(ci + 1)
        Q = qv[b]; F = fk[b]
        lf3 = F[:, 0, :, :].rearrange("p a b -> p (a b)")
        mats = (mrel, None, tri, tail)
        pss = []
        for k, M in enumerate(mats):
            if M is None:
                pss.append(None)
                continue
            ps = nps()
            MM(P, ps, ps[:, 0:128], M[:], lf3, [M, F])
            pss.append(ps)
        A_(P, lambda e, ps=pss[0]: e.activation(ex[0][:], ps[:, 0:128], AF.Exp), [pss[0]], [ex[0]])
        A_(P, lambda e, ps=pss[0]: e.activation(ex[1][:], ps[:, 0:128], AF.Exp, scale=-1.0), [pss[0]], [ex[1]])
        A_(P, lambda e, ps=pss[2]: e.activation(ex[2][:], ps[:, 0:128], AF.Exp), [pss[2]], [ex[2]])
        A_(P, lambda e, ps=pss[3]: e.activation(ex[3][:], ps[:, 0:128], AF.Exp), [pss[3]], [ex[3]])
        V(P, lambda e, Q=Q: e.tensor_tensor(qe[:], Q[:, 0, :, :], ex[0][:].rearrange("p (a b) -> p a b", a=2), ALU.mult), [Q, ex[0]], [qe])
        G_(P, lambda e, F=F: e.tensor_tensor(ke[:], F[:, 1, :, :], ex[1][:].rearrange("p (a b) -> p a b", a=2), ALU.mult), [F, ex[1]], [ke])
        for p in range(2):
            vv = Q[:, 1, p, :]
            for hh in range(2):
                r = slice(hh * 64, hh * 64 + 64)
                V(P, lambda e, p=p, r=r, Q=Q: e.tensor_tensor(qgb[p][r, r], Q[r, 0, p, :], ex[2][r, p * 64:(p + 1) * 64], ALU.mult), [Q, ex[2]], [qgb[p]])
                G_(P, lambda e, p=p, r=r, F=F: e.tensor_tensor(kendb[p][r, r], F[r, 1, p, :], ex[3][r, p * 64:(p + 1) * 64], ALU.mult), [F, ex[3]], [kendb[p]])
                G_(P, lambda e, p=p, r=r, F=F: e.tensor_copy(lfb[p][r, r], F[r, 0, p, :]), [F], [lfb[p]])
            ps = nps(); ps2 = nps()
            TR(P, ps, ps[0:64, 0:128], ke[:, p, :], ident[:], [ke, ident])
            TR(P, ps2, ps2[0:64, 0:128], qe[:, p, :], ident[:], [qe, ident])
            A_(P, lambda e, p=p, ps=ps: e.copy(keT[p][0:64, :], ps[0:64, 0:128]), [ps], [keT[p]])
            V(P, lambda e, p=p, ps=ps2: e.tensor_copy(qeT[p][0:64, :], ps[0:64, 0:128]), [ps2], [qeT[p]])
            ps = nps(); ps2 = nps()
            MM(P, ps, ps[:, 0:128], keT[p][0:64, :], qeT[p][0:64, :], [keT[p], qeT[p]])
            TR(P, ps2, ps2[:, 0:128], qgb[p][:], ident[:], [qgb[p], ident])
            MM(P, ps2, ps2[:, 128:129], lfb[p][:], ones[:, 0:1], [lfb[p], ones])
            V(P, lambda e, p=p, ps=ps: e.select(aT[p][:], trim[:], ps[:, 0:128], zeros[:]), [ps, trim, zeros], [aT[p]])
            A_(P, lambda e, p=p, ps=ps2: e.copy(qgT[p][:], ps[:, 0:128]), [ps2], [qgT[p]])
            A_(P, lambda e, p=p, ps=ps2: e.activation(ege[p][:], ps[:, 128:129], AF.Exp), [ps2], [ege[p]])
            ps = nps(); ps2 = nps()
            MM(P, ps, ps[:, 0:64], qgT[p][:], S[p][:], [qgT[p], S[p]], start=True, stop=False)
            MM(P, ps, ps[:, 0:64], aT[p][:], vv, [aT[p], Q], start=False, stop=True)
            MM(P, ps2, ps2[:, 0:64], kendb[p][:], vv, [kendb[p], Q])
            A_(P, lambda e, p=p, ps=ps, b=b: e.copy(ot[b][:, p, :], ps[:, 0:64]), [ps], [ot[b]])
            V(P, lambda e, p=p, ps=ps2: e.scalar_tensor_tensor(S[p][:], S[p][:], ege[p][:, 0:1], ps[:, 0:64], ALU.mult, ALU.add), [S[p], ege[p], ps2], [S[p]])
        r0 = c * 64
        for hh in range(2):
            dst = K.OG[d][r0:r0 + 64, :].rearrange("r (pr hh e) -> r pr hh e", pr=2, hh=2)[:, :, hh, :]
            P.dma("sync", dst, ot[b][hh * 64:(hh + 1) * 64, :, :], reads=[ot[b]])
    P.reset(m0)


def phase_dir_fin(K, l, nw_ap, gate_c0, out_c0):
    P = K.P
    m0 = P.mark()
    nw = P.tile("fnw", [64])
    P.dma("sync", nw[:], nw_ap.partition_broadcast(128), writes=[nw])
    o0 = [P.tile("fo0_%d" % b, [256]) for b in range(2)]
    o1 = [P.tile("fo1_%d" % b, [256]) for b in range(2)]
    zt = [P.tile("fzt%d" % b, [256]) for b in range(2)]
    sq = P.tile("fsq", [256]); ss = P.tile("fss", [4])

    def loads(i):
        b = i % 2
        rs = slice(i * 128, (i + 1) * 128)
        P.dma("sync", o0[b][:], K.OG[0][rs, :], writes=[o0[b]])
        P.dma("sync", o1[b][:], K.OG[1][rs, :], writes=[o1[b]])
        P.dma("sync", zt[b][:], K.Pj[rs, gate_c0:gate_c0 + 256], writes=[zt[b]])
    loads(0)
    for i in range(NT):
        b = i % 2
        if i + 1 < NT:
            loads(i + 1)
        o = o0[b]
        V(P, lambda e, o=o, b=b: e.tensor_tensor(o[:], o[:], o1[b][:], ALU.add), [o, o1[b]], [o])
        head_rms_gate(K, o, zt[b], nw, sq, ss, 4, 64, 1.0)
        P.dma("sync", K.O[i * 128:(i + 1) * 128, out_c0:out_c0 + 256], o[:], reads=[o])
    P.reset(m0)


def phase_hg(K, l):
    phase_hg_pre(K, l)
    phase_hg_scan(K, l, 0)
    phase_hg_scan(K, l, 1)
    phase_dir_fin(K, l, K.hg_norm_w[l], C_HG + 1024, 512)


EXTRA_E = {"hg": phase_hg, "hgpre": phase_hg_pre}


HY_STREAMS = ((TC, 0), (TL, TC))


def hy_consts(c):
    for (n, _) in HY_STREAMS:
        N = 2 * n
        N1 = N // 128
        rows = np.arange(N)
        lag = np.where(rows < n, rows, N - rows).astype(np.float64)
        valid = (rows != n).astype(np.float64)
        t01 = lag / max(n - 1, 1)
        bands = np.linspace(1e-4, 15, 16)
        ang = (2.0 * np.pi / n) * lag[:, None] * bands
        z = np.concatenate([t01[:, None], np.cos(ang), -np.sin(ang)], axis=-1)
        c["hy_zT%d" % n] = np.ascontiguousarray(z.T).astype(np.float32)
        c["hy_t01_%d" % n] = np.ascontiguousarray((-t01).reshape(N1, 128).T).astype(np.float32)
        c["hy_val%d" % n] = np.ascontiguousarray(valid.reshape(N1, 128).T).astype(np.float32)
        n1 = np.arange(N1)[:, None]; k1 = np.arange(N1)[None, :]
        th = 2 * np.pi * n1 * k1 / N1
        c["hy_wf1_%d" % n] = np.concatenate([np.cos(th), -np.sin(th)], axis=1).astype(np.float32)
        c["hy_cf%d" % n] = (np.cos(th).T[:, :N1 // 2] / N).astype(np.float32)
        c["hy_nsf%d" % n] = (-np.sin(th).T[:, :N1 // 2] / N).astype(np.float32)
        n2 = np.arange(128)[None, :, None]; k2 = np.arange(128)[None, None, :]; kk1 = np.arange(N1)[:, None, None]
        th2 = 2 * np.pi * n2 * (kk1 + N1 * k2) / N
        c["hy_c2_%d" % n] = np.cos(th2).astype(np.float32)
        c["hy_s2_%d" % n] = np.sin(th2).astype(np.float32)
        c["hy_c2t_%d" % n] = np.ascontiguousarray(np.cos(th2).transpose(0, 2, 1)).astype(np.float32)
        c["hy_s2t_%d" % n] = np.ascontiguousarray(np.sin(th2).transpose(0, 2, 1)).astype(np.float32)


def phase_hy_pre(K, l):
    P = K.P
    m0 = P.mark()
    cw = P.tile("hcw", [3, 768])
    for k in range(3):
        P.dma("sync", cw[:, k, :], K.hy_conv_w[l][k].partition_broadcast(128), writes=[cw])
    x3 = [[P.tile("hx3_%d_%d" % (b, k), [768]) for k in range(3)] for b in range(2)]
    acc = [P.tile("hacc%d" % b, [768]) for b in range(2)]
    t0 = P.tile("ht0", [768]); t2 = P.tile("ht2", [768])
    load_shift3(K, "sync", x3[0], K.Pj, 0, C_HY, C_HY + 768)
    for i in range(NT):
        b = i % 2
        if i + 1 < NT:
            load_shift3(K, "sync", x3[1 - b], K.Pj, i + 1, C_HY, C_HY + 768)
        conv3(K, x3[b], cw, acc[b], t0, t2, 768)
        P.dma("sync", K.HC[i * 128:(i + 1) * 128, :], acc[b][:], reads=[acc[b]])
    P.reset(m0)


def hy_filters(K, l, n):
    P = K.P
    N = 2 * n
    N1 = N // 128
    m0 = P.mark()
    w1 = P.tile("hw1", [64], parts=33); w2 = P.tile("hw2", [64], parts=64); w3 = P.tile("hw3", [1024], parts=64)
    P.dma("sync", w1[:], K.hy_w1[l], writes=[w1]); P.dma("sync", w2[:], K.hy_w2[l], writes=[w2]); P.dma("sync", w3[:], K.hy_w3[l], writes=[w3])
    pv = P.tile("hpv", [4], parts=64)
    for k, src in enumerate((K.hy_f1, K.hy_b1, K.hy_f2, K.hy_b2)):
        P.dma("sync", pv[:, k:k + 1], src[l].rearrange("(a b) -> a b", b=1), writes=[pv])
    sc = P.tile("hsc", [8], parts=64)
    for (o, fi, bi) in ((0, 0, 1), (3, 2, 3)):
        V(P, lambda e, o=o, fi=fi: e.tensor_scalar(sc[:, o:o + 1], pv[:, fi:fi + 1], 0.25, 1.0, ALU.mult, ALU.mult), [pv], [sc])
        V(P, lambda e, o=o, fi=fi, bi=bi: e.tensor_tensor(sc[:, o + 1:o + 2], sc[:, o:o + 1], pv[:, bi:bi + 1], ALU.mult), [pv, sc], [sc])
        V(P, lambda e, o=o: e.tensor_scalar(sc[:, o + 2:o + 3], sc[:, o + 1:o + 2], math.pi / 2, 1.0, ALU.add, ALU.mult), [sc], [sc])
    nad = P.tile("hnad", [512])
    P.dma("sync", nad[:], K.hy_decay[l].rearrange("a b -> (a b)").partition_broadcast(128), writes=[nad])
    tneg = P.tile("htneg", [512])
    V(P, lambda e: e.tensor_scalar(tneg[:], nad[:], -1.0, 1.0, ALU.mult, ALU.mult), [nad], [tneg])
    V(P, lambda e: e.tensor_tensor(nad[:], nad[:], tneg[:], ALU.max), [nad, tneg], [nad])
    t01 = P.tile("ht01", [N1]); val = P.tile("hval", [N1])
    P.dma("sync", t01[:], K.cin["hy_t01_%d" % n], writes=[t01]); P.dma("sync", val[:], K.cin["hy_val%d" % n], writes=[val])
    zT = [P.tile("hzT%d" % b, [128], parts=33) for b in range(2)]
    sT = P.tile("hsT", [128], parts=64); cT = P.tile("hcT", [128], parts=64); hh = P.tile("hhh", [128], parts=64); h2 = P.tile("hh2", [128], parts=64)
    win = P.tile("hwin", [512]); ko = [P.tile("hko%d" % b, [512]) for b in range(2)]

    def sin_layer(ps, o, out):
        A_(P, lambda e: e.activation(sT[:], ps[0:64, 0:128], AF.Sin, bias=sc[:, o + 1:o + 2], scale=sc[:, o:o + 1]), [ps, sc], [sT])
        A_(P, lambda e: e.activation(cT[:], ps[0:64, 0:128], AF.Sin, bias=sc[:, o + 2:o + 3], scale=sc[:, o:o + 1]), [ps, sc], [cT])
        V(P, lambda e: e.tensor_tensor(cT[:], cT[:], sT[:], ALU.mult), [cT, sT], [cT])
        V(P, lambda e: e.tensor_tensor(sT[:], sT[:], sT[:], ALU.mult), [sT], [sT])
        V(P, lambda e: e.tensor_scalar(sT[:], sT[:], -2.0, 1.0, ALU.mult, ALU.add), [sT], [sT])
        V(P, lambda e: e.scalar_tensor_tensor(out[:], cT[:], 4.0, sT[:], ALU.mult, ALU.mult), [cT, sT], [out])

    P.dma("sync", zT[0][:], K.cin["hy_zT%d" % n][:, 0:128], writes=[zT[0]])
    for rt in range(N1):
        b = rt % 2
        if rt + 1 < N1:
            P.dma("sync", zT[1 - b][:], K.cin["hy_zT%d" % n][:, (rt + 1) * 128:(rt + 2) * 128], writes=[zT[1 - b]])
        side = 0 if rt < N1 // 2 else 1
        ps = K.ps[rt % 2]
        MM(P, ps, ps[0:64, 0:128], w1[:], zT[b][:], [w1, zT[b]])
        sin_layer(ps, 0, hh)
        ps = K.ps[2 + rt % 2]
        MM(P, ps, ps[0:64, 0:128], w2[:], hh[:], [w2, hh])
        sin_layer(ps, 3, h2)
        ps = K.ps[4 + rt % 2]
        MM(P, ps, ps[:, 0:512], h2[:], w3[:, side * 512:(side + 1) * 512], [h2, w3])
        A_(P, lambda e, rt=rt: e.activation(win[:], nad[:], AF.Exp, scale=t01[:, rt:rt + 1]), [nad, t01], [win])
        V(P, lambda e, ps=ps, rt=rt, b=b: e.scalar_tensor_tensor(ko[b][:], ps[:, 0:512], val[:, rt:rt + 1], win[:], ALU.mult, ALU.mult), [ps, val, win], [ko[b]])
        P.dma("sync", K.KERN[rt * 128:(rt + 1) * 128, :], ko[b][:], reads=[ko[b]])
    P.reset(m0)
    hy_dft_fwd(K, n, K.KERN, 512, N1, kernel=True)


def hy_views(K, n, ncol):
    N1 = 2 * n // 128
    A = K.Aflat[0:2 * N1 * 128 * ncol].rearrange("(a b c) -> a b c", a=2 * N1, b=128)
    return A


def hy_dft_fwd(K, n, src, ncol, nrows1, kernel=False, filt=0, dst_rows=None, skip_ap=None, gate_c0=None, u_c0=None, out_ap=None, out_c0=0, row0=0):
    P = K.P
    N = 2 * n
    N1 = N // 128
    A = hy_views(K, n, ncol)
    m0 = P.mark()
    wf1 = P.tile("hwf1", [2 * N1], parts=N1)
    P.dma("sync", wf1[:], K.cin["hy_wf1_%d" % n], writes=[wf1])
    CH = 2048 // ncol * 1
    CH = max(1, 2048 // ncol)
    xin = [P.tile("hxin%d" % b, [CH * ncol], parts=nrows1) for b in range(2)]
    ao = [P.tile("hao%d" % b, [CH * ncol], parts=2 * N1) for b in range(2)]
    srcv = src[0:nrows1 * 128, :].rearrange("(a b) c -> a b c", b=128)
    nch = 128 // CH
    P.dma("sync", xin[0][:].rearrange("p (b c) -> p b c", b=CH), srcv[:, 0:CH, :], writes=[xin[0]])
    for ch in range(nch):
        b = ch % 2
        if ch + 1 < nch:
            P.dma("sync", xin[1 - b][:].rearrange("p (b c) -> p b c", b=CH), srcv[:, (ch + 1) * CH:(ch + 2) * CH, :], writes=[xin[1 - b]])
        for q in range(CH * ncol // 512):
            ps = K.ps[q % 4]
            MM(P, ps, ps[0:2 * N1, 0:512], wf1[0:nrows1, :], xin[b][:, q * 512:(q + 1) * 512], [wf1, xin[b]])
            if q % 2 == 0:
                A_(P, lambda e, ps=ps, q=q, b=b: e.copy(ao[b][:, q * 512:(q + 1) * 512], ps[0:2 * N1, 0:512]), [ps], [ao[b]])
            else:
                V(P, lambda e, ps=ps, q=q, b=b: e.tensor_copy(ao[b][:, q * 512:(q + 1) * 512], ps[0:2 * N1, 0:512]), [ps], [ao[b]])
        P.dma("sync", A[:, ch * CH:(ch + 1) * CH, :], ao[b][:].rearrange("p (b c) -> p b c", b=CH), reads=[ao[b]])
    P.reset(m0)
    m0 = P.mark()
    NC2 = 2 * ncol
    Av = A.rearrange("(ri k1) n2 c -> k1 n2 ri c", ri=2)
    r1 = [P.tile("hr1_%d" % b, [2, ncol]) for b in range(2)]
    r2 = [P.tile("hr2_%d" % b, [2, ncol]) for b in range(2)]
    tb = [[P.tile("htb%d_%d" % (b, k), [128]) for k in range(4 if not kernel else 2)] for b in range(2)]
    xo = [P.tile("hxo%d" % b, [NC2]) for b in range(2)]
    if not kernel:
        kf = [P.tile("hkf%d" % b, [2, ncol]) for b in range(2)]
        ta = P.tile("hta", [2, ncol]); tbb = P.tile("htbb", [2, ncol])
        y1 = P.tile("hy1", [2, ncol]); y2 = P.tile("hy2", [2, ncol])
        Bv = K.Bflat[0:N1 * 128 * 2 * ncol].rearrange("(k1 n2 ri c) -> k1 n2 ri c", k1=N1, n2=128, ri=2)
    tabs = ("hy_c2_%d" % n, "hy_s2_%d" % n, "hy_c2t_%d" % n, "hy_s2t_%d" % n)

    def loads(k1):
        b = k1 % 2
        P.dma("sync", r1[b][:], Av[k1], writes=[r1[b]])
        for k in range(len(tb[b])):
            P.dma("sync", tb[b][k][:], K.cin[tabs[k]][k1], writes=[tb[b][k]])
        if not kernel:
            P.dma("sync", kf[b][:], K.Kf[k1, :, :, filt * 256:(filt + 1) * 256], writes=[kf[b]])
    loads(0)
    for k1 in range(N1):
        b = k1 % 2
        if k1 + 1 < N1:
            loads(k1 + 1)
        G_(P, lambda e, b=b: e.tensor_copy(r2[b][:, 0, :], r1[b][:, 1, :]), [r1[b]], [r2[b]])
        A_(P, lambda e, b=b: e.mul(r2[b][:, 1, :], r1[b][:, 0, :], -1.0), [r1[b]], [r2[b]])
        f1 = r1[b][:].rearrange("p a c -> p (a c)")
        f2 = r2[b][:].rearrange("p a c -> p (a c)")
        nh = NC2 // 512
        pss = []
        for q in range(nh):
            ps = K.ps[q % 2] if kernel else K.ps[0]
            if kernel:
                ps = K.ps[(k1 * nh + q) % 4]
            MM(P, ps, ps[:, 0:512], tb[b][0][:], f1[:, q * 512:(q + 1) * 512], [tb[b][0], r1[b]], start=True, stop=False)
            MM(P, ps, ps[:, 0:512], tb[b][1][:], f2[:, q * 512:(q + 1) * 512], [tb[b][1], r2[b]], start=False, stop=True)
            pss.append(ps)
            if kernel:
                if q % 2 == 0:
                    A_(P, lambda e, ps=ps, q=q, b=b: e.copy(xo[b][:, q * 512:(q + 1) * 512], ps[:, 0:512]), [ps], [xo[b]])
                else:
                    V(P, lambda e, ps=ps, q=q, b=b: e.tensor_copy(xo[b][:, q * 512:(q + 1) * 512], ps[:, 0:512]), [ps], [xo[b]])
        if kernel:
            P.dma("sync", K.Kf[k1], xo[b][:].rearrange("p (a c) -> p a c", a=2), reads=[xo[b]])
            continue
        ps = pss[0]
        X3 = ps[:, 0:512].rearrange("p (a c) -> p a c", a=2)
        V(P, lambda e, X3=X3, b=b: e.tensor_tensor(ta[:], X3, kf[b][:, 0:1, :].broadcast_to([128, 2, ncol]), ALU.mult), [ps, kf[b]], [ta])
        V(P, lambda e, X3=X3, b=b: e.tensor_tensor(tbb[:], X3, kf[b][:, 1:2, :].broadcast_to([128, 2, ncol]), ALU.mult), [ps, kf[b]], [tbb])
        G_(P, lambda e: e.tensor_tensor(y1[:, 0, :], ta[:, 0, :], tbb[:, 1, :], ALU.subtract), [ta, tbb], [y1])
        G_(P, lambda e: e.tensor_tensor(y1[:, 1, :], tbb[:, 0, :], ta[:, 1, :], ALU.add), [ta, tbb], [y1])
        A_(P, lambda e: e.mul(y2[:, 0, :], y1[:, 1, :], -1.0), [y1], [y2])
        G_(P, lambda e: e.tensor_copy(y2[:, 1, :], y1[:, 0, :]), [y1], [y2])
        ps2 = K.ps[1 + k1 % 2]
        MM(P, ps2, ps2[:, 0:512], tb[b][2][:], y1[:].rearrange("p a c -> p (a c)"), [tb[b][2], y1], start=True, stop=False)
        MM(P, ps2, ps2[:, 0:512], tb[b][3][:], y2[:].rearrange("p a c -> p (a c)"), [tb[b][3], y2], start=False, stop=True)
        A_(P, lambda e, ps2=ps2, b=b: e.copy(xo[b][:], ps2[:, 0:512]), [ps2], [xo[b]])
        P.dma("sync", Bv[k1], xo[b][:].rearrange("p (a c) -> p a c", a=2), reads=[xo[b]])
    P.reset(m0)
    if kernel:
        return
    m0 = P.mark()
    H = N1 // 2
    cf = P.tile("hcf", [H], parts=N1); nsf = P.tile("hnsf", [H], parts=N1)
    P.dma("sync", cf[:], K.cin["hy_cf%d" % n], writes=[cf]); P.dma("sync", nsf[:], K.cin["hy_nsf%d" % n], writes=[nsf])
    skp = P.tile("hskp", [256])
    P.dma("sync", skp[:], skip_ap.partition_broadcast(128), writes=[skp])
    CH2 = 4
    br = [P.tile("hbr%d" % b, [CH2, 2, 256], parts=N1) for b in range(2)]
    uu = [P.tile("huu%d" % b, [CH2, 256], parts=H) for b in range(2)]
    gg = [P.tile("hgg%d" % b, [CH2, 256], parts=H) for b in range(2)]
    yo = [P.tile("hyo%d" % b, [CH2, 256], parts=H) for b in range(2)]
    usrc = u_c0[0][row0 if u_c0[2] else 0:(row0 if u_c0[2] else 0) + n, u_c0[1]:u_c0[1] + 256].rearrange("(a b) c -> a b c", b=128)
    gsrc = K.HC[row0:row0 + n, gate_c0:gate_c0 + 256].rearrange("(a b) c -> a b c", b=128)
    dsrc = out_ap[(row0 if out_ap is K.O else 0):(row0 if out_ap is K.O else 0) + n, out_c0:out_c0 + 256].rearrange("(a b) c -> a b c", b=128)
    Bk = Bv.rearrange("k1 n2 ri c -> k1 n2 ri c")

    def loads3(ch):
        b = ch % 2
        P.dma("sync", br[b][:], Bk[:, ch * CH2:(ch + 1) * CH2, :, :], writes=[br[b]])
        P.dma("sync", uu[b][:], usrc[:, ch * CH2:(ch + 1) * CH2, :], writes=[uu[b]])
        P.dma("sync", gg[b][:], gsrc[:, ch * CH2:(ch + 1) * CH2, :], writes=[gg[b]])
    loads3(0)
    nch = 128 // CH2
    for ch in range(nch):
        b = ch % 2
        if ch + 1 < nch:
            loads3(ch + 1)
        for q in range(CH2 // 2):
            ps = K.ps[4 + q % 2]
            o3 = ps[0:H, 0:512].rearrange("p (a c) -> p a c", a=2)
            MM(P, ps, o3, cf[:], br[b][:, 2 * q:2 * q + 2, 0, :], [cf, br[b]], start=True, stop=False)
            MM(P, ps, o3, nsf[:], br[b][:, 2 * q:2 * q + 2, 1, :], [nsf, br[b]], start=False, stop=True)
            us = uu[b][:, 2 * q:2 * q + 2, :]
            V(P, lambda e, us=us: e.tensor_tensor(us, us, skp[0:H, :].unsqueeze(1).broadcast_to([H, 2, 256]), ALU.mult), [uu[b], skp], [uu[b]])
            V(P, lambda e, us=us, o3=o3: e.tensor_tensor(us, us, o3, ALU.add), [uu[b], ps], [uu[b]])
            G_(P, lambda e, us=us, b=b, q=q: e.tensor_tensor(yo[b][:, 2 * q:2 * q + 2, :], us, gg[b][:, 2 * q:2 * q + 2, :], ALU.mult), [uu[b], gg[b]], [yo[b]])
        P.dma("sync", dsrc[:, ch * CH2:(ch + 1) * CH2, :], yo[b][:], reads=[yo[b]])
    P.reset(m0)


def phase_hy(K, l):
    phase_hy_pre(K, l)
    for (n, row0) in HY_STREAMS:
        hy_filters(K, l, n)
        N1 = 2 * n // 128
        hy_dft_fwd(K, n, K.HC[row0:row0 + n, 0:256], 256, N1 // 2, filt=0, skip_ap=K.hy_bias[l][0], gate_c0=256,
                   u_c0=(K.HC, 0, True), out_ap=K.Zd, out_c0=0, row0=row0)
        hy_dft_fwd(K, n, K.Zd[0:n, :], 256, N1 // 2, filt=1, skip_ap=K.hy_bias[l][1], gate_c0=512,
                   u_c0=(K.Zd, 0, False), out_ap=K.O, out_c0=256, row0=row0)


EXTRA_F = {"hy": phase_hy, "hypre": phase_hy_pre}


def build(stop_after=None, dbg=(), pj_input=False, plan=None, ext=()):
    nc = bass.Bass("TRN2", target_bir_lowering=False)
    P = Prog(nc)
    K = Ctx(); K.P = P; K.nc = nc
    def din(name, shape, dt=F32):
        return nc.dram_tensor(name, list(shape), dt, kind="ExternalInput").ap()
    def dscr(name, shape, dt=F32):
        if name in ext:
            return din(name, shape, dt)
        kind = "ExternalOutput" if name in dbg else "Internal"
        return nc.dram_tensor(name, list(shape), dt, kind=kind).ap()
    K.xin = din("xin", [TT, D])
    K.sTin = din("sTin", [128, 8, 2])
    K.mod_w = din("mod_w", [L, D, 6 * D]); K.mod_b = din("mod_b", [L, 6 * D]); K.mod_bT = din("mod_bT", [L, 128, 48])
    K.ln1Tin = din("ln1T", [128, L, 8]); K.ln2Tin = din("ln2T", [128, L, 8])
    K.w_in = din("w_in", [L, D, PIN])
    K.gdn_conv_w = din("gdn_conv_w", [L, 3, 768]); K.gdn_a_log = din("gdn_a_log", [L, 2, 4]); K.gdn_dt_bias = din("gdn_dt_bias", [L, 2, 4])
    K.gdn_norm_w = din("gdn_norm_w", [L, 64])
    K.w_out = din("w_out", [L, D, D]); K.mlp_w1 = din("mlp_w1", [L, D, 4 * D]); K.mlp_w2 = din("mlp_w2", [L, 4 * D, D])
    K.final_ops = []
    K.da_q_norm = din("da_q_norm", [L, 32]); K.da_k_norm = din("da_k_norm", [L, 32]); K.da_lam = din("da_lam", [L, 4, 32]); K.da_subln = din("da_subln", [L, 64])
    hc = host_consts()
    hc["ropeC"], hc["ropeS"] = rope_tables()
    hg_consts(hc)
    hy_consts(hc)
    K.hy_conv_w = din("hy_conv_w", [L, 3, 768]); K.hy_w1 = din("hy_w1", [L, 33, 64]); K.hy_w2 = din("hy_w2", [L, 64, 64]); K.hy_w3 = din("hy_w3", [L, 64, 1024])
    K.hy_b1 = din("hy_b1", [L, 64]); K.hy_f1 = din("hy_f1", [L, 64]); K.hy_b2 = din("hy_b2", [L, 64]); K.hy_f2 = din("hy_f2", [L, 64])
    K.hy_decay = din("hy_decay", [L, 2, 256]); K.hy_bias = din("hy_bias", [L, 2, 256])
    K.HC = dscr("HC", [TT, 768]); K.KERN = dscr("KERN", [8192, 512]); K.Aflat = dscr("Aflat", [128 * 128 * 512]); K.Bflat = dscr("Bflat", [64 * 128 * 2 * 256])
    K.Kf = dscr("Kf", [64, 128, 2, 512]); K.Zd = dscr("Zd", [TL, 256])
    K.hg_lb_raw = din("hg_lb_raw", [L, 256]); K.hg_norm_w = din("hg_norm_w", [L, 64])
    K.HQ = dscr("HQ", [TT, HQW])
    K.cin = {k: din("c_" + k, v.shape) for k, v in hc.items()}
    K.out = nc.dram_tensor("out", [TL, D], F32, kind="ExternalOutput").ap()
    K.X = dscr("X", [TT, D]); K.Pj = din("Pj", [TT, PIN]) if pj_input else dscr("Pj", [TT, PIN])
    K.GQ = dscr("GQ", [TT, GQW]); K.OG = [dscr("OG%d" % d, [TT, 256]) for d in range(2)]; K.O = dscr("O", [TT, D])
    K.dbgfm = None
    if "dbgfm" in dbg:
        K.dbgfm = nc.dram_tensor("dbgfm", [128, 64 + 16 + 16], F32, kind="ExternalOutput").ap()
        K.dbgG = nc.dram_tensor("dbgG", [128, 4096], F32, kind="ExternalOutput").ap()
    K.ps = [P.ps("ps%d" % i, [128, 512]) for i in range(8)]
    K.ident = P.tile("ident", [128]); K.identb = P.tile("identb", [128], BF16)
    K.ones = P.tile("ones", [128])
    K.epsc = P.tile("epsc", [1]); K.onec = P.tile("onec", [1])
    P.op("vector", lambda e: e.memset(K.onec[:], 1.0), writes=[K.onec])
    K.sT = P.tile("sT", [8, 2])
    K.fm = P.tile("fm", [4, 8, 2]); K.A1 = P.tile("A1", [8, 2]); K.A2 = P.tile("A2", [8, 2])
    K.ln1T = P.tile("ln1T", [L, 8]); K.ln2T = P.tile("ln2T", [L, 8])
    K.G = [[P.tile("G%d%d" % (g, s), [D]) for s in range(2)] for g in range(2)]
    P.dma("sync", K.ident[:], K.cin["ident"], writes=[K.ident])
    P.dma("gpsimd", K.identb[:], K.cin["ident"], writes=[K.identb])
    P.dma("sync", K.ones[:], K.cin["ones"], writes=[K.ones])
    P.dma("sync", K.ln1T[:], K.ln1Tin, writes=[K.ln1T]); P.dma("sync", K.ln2T[:], K.ln2Tin, writes=[K.ln2T])
    P.op("vector", lambda e: e.memset(K.epsc[:], EPS), writes=[K.epsc])
    sraw = P.tile("sraw", [8, 2])
    P.dma("sync", sraw[:], K.sTin, writes=[sraw])
    P.op("scalar", lambda e: e.activation(K.sT[:], sraw[:], AF.Silu), reads=[sraw], writes=[K.sT])
    last = []
    for i in range(0, TT, 544):
        last.append(P.dma("sync" if (i // 544) % 2 == 0 else "gpsimd", K.X[i:i + 544, :], K.xin[i:i + 544, :]))
    P.barrier()
    fin = []
    def finish():
      if (not pj_input) and K.dbgfm is not None:
        fin.append(P.dma("sync", K.dbgfm[:, 0:64], K.fm[:].rearrange("p a b c -> p (a b c)"), reads=[K.fm]))
        fin.append(P.dma("sync", K.dbgfm[:, 64:80], K.A1[:].rearrange("p a b -> p (a b)"), reads=[K.A1]))
        fin.append(P.dma("sync", K.dbgfm[:, 80:96], K.A2[:].rearrange("p a b -> p (a b)"), reads=[K.A2]))
        for g in range(2):
            for s in range(2):
                fin.append(P.dma("sync", K.dbgG[:, (g * 2 + s) * 1024:(g * 2 + s + 1) * 1024], K.G[g][s][:], reads=[K.G[g][s]]))
      P.barrier()
      if not K.final_ops:
          fin.append(P.dma("sync", K.out[0:128, :], K.X[0:128, :]))
      P.finalize(fin + last + K.final_ops)
      return nc, hc
    if plan is not None:
        phases = {"mods": phase_mods, "proj": phase_proj, "gdnpre": phase_gdn_pre, "gdns0": lambda K, l: phase_gdn_scan(K, l, 0),
                  "gdns1": lambda K, l: phase_gdn_scan(K, l, 1), "gdnfin": phase_gdn_fin, "wout": phase_wout,
                  "mlp": lambda K, l: phase_mlp(K, l, False), "mlplast": lambda K, l: phase_mlp(K, l, True)}
        phases.update(EXTRA_PHASES); phases.update(EXTRA_D); phases.update(EXTRA_E); phases.update(EXTRA_F)
        for (nm, l) in plan:
            phases[nm](K, l)
        pj_input = "mods" not in [p for p, _ in plan]
        return finish()
    for l in range(L):
        if not pj_input:
            phase_mods(K, l)
            if stop_after == ("mods", l): return finish()
            phase_proj(K, l)
            if stop_after == ("proj", l): return finish()
        phase_gdn_pre(K, l)
        if stop_after == ("gdnpre", l): return finish()
        phase_gdn_scan(K, l, 0)
        if stop_after == ("gdns0", l): return finish()
        phase_gdn_scan(K, l, 1)
        phase_gdn_fin(K, l)
        if stop_after == ("gdn", l): return finish()
    return finish()

EXTRA_PHASES = {}

def host_inputs(inputs, hc):
    silu_in = []
    ins = []
    f = lambda a: np.ascontiguousarray(a, dtype=np.float32)
    ln1T = f(inputs["ln1_w"].reshape(L, 8, 128).transpose(2, 0, 1))
    ln2T = f(inputs["ln2_w"].reshape(L, 8, 128).transpose(2, 0, 1))
    mod_bT = f(inputs["mod_b"].reshape(L, 48, 128).transpose(0, 2, 1))
    for b in range(8):
        d = {}
        d["xin"] = f(np.concatenate([inputs["ctx"][b], inputs["x"][b]], axis=0))
        sT = np.stack([inputs["c"][b].reshape(8, 128).T, inputs["c_ctx"].reshape(8, 128).T], axis=-1)
        d["sTin"] = f(sT)
        d["mod_w"] = f(inputs["mod_w"]); d["mod_b"] = f(inputs["mod_b"]); d["mod_bT"] = mod_bT
        d["ln1T"] = ln1T; d["ln2T"] = ln2T
        d["w_in"] = f(inputs["w_in"])
        for k in ("w_out", "mlp_w1", "mlp_w2", "da_q_norm", "da_k_norm", "da_lam", "da_subln", "hg_lb_raw", "hg_norm_w", "hy_conv_w", "hy_w1", "hy_w2", "hy_w3", "hy_b1", "hy_f1", "hy_b2", "hy_f2", "hy_decay", "hy_bias"):
            d[k] = f(inputs[k])
        for k in ("gdn_conv_w", "gdn_a_log", "gdn_dt_bias", "gdn_norm_w"):
            d[k] = f(inputs[k])
        for k, v in hc.items():
            d["c_" + k] = v
        ins.append(d)
    return ins


FULL_PLAN = []
for _l in range(L):
    for _p in ("mods", "proj", "gdnpre", "gdns0", "gdns1", "gdnfin", "hy", "hg", "attn", "wout"):
        FULL_PLAN.append((_p, _l))
    FULL_PLAN.append(("mlplast" if _l == L - 1 else "mlp", _l))


def kernel(**inputs):
    nc, hc = build(plan=FULL_PLAN)
    ins = host_inputs({k: np.asarray(v) for k, v in inputs.items()}, hc)
    res = run_bass_kernel_spmd(nc, ins, core_ids=list(range(8)))
    return np.stack([np.asarray(r["out"], dtype=np.float32) for r in res.results], axis=0)
```

```python
import numpy as np
from contextlib import ExitStack
import concourse.bass as bass
import concourse.mybir as mybir
from concourse.bass_utils import run_bass_kernel_spmd

F32 = mybir.dt.float32
BF16 = mybir.dt.bfloat16
AF = mybir.ActivationFunctionType
ALU = mybir.AluOpType
AX = mybir.AxisListType

COMPUTE = ("tensor", "vector", "scalar", "gpsimd")
NSLOT = 8


class Op:
    __slots__ = ("stream", "eng", "fn", "waits", "inc", "idx", "is_dma", "slot", "slot_use", "signal")

    def __init__(self):
        self.waits = []
        self.inc = None
        self.signal = False


class T:
    def __init__(self, name, ap):
        self.name = name
        self.ap = ap

    def __getitem__(self, k):
        return self.ap[k]


class Prog:
    def __init__(self, nc, arena_words=53000):
        self.nc = nc
        self.es = ExitStack()
        self.arena = self.es.enter_context(nc.sbuf_tensor("arena", [128, arena_words], F32))
        self.arena_words = arena_words
        self.bump = 0
        self.barrier_ops = []
        self.barrier_seen = set()
        self.ntile = 0
        self.ops = []
        self.last_w = {}
        self.readers = {}
        self.streams = {}
        self.sem = {}
        self.cnt = {}
        self.dma_slots = {}
        self.slot_uses = {}

    def tile(self, name, free, dt=F32, parts=128):
        free = list(free)
        n = 1
        for f in free:
            n *= f
        words = n if dt == F32 else (n + 1) // 2
        words = (words + 7) // 8 * 8
        assert self.bump + words <= self.arena_words, (name, self.bump, words)
        v = self.arena[0:parts, self.bump:self.bump + words]
        if dt != F32:
            v = v.bitcast(dt)
        v = v[:, 0:n]
        if len(free) == 2:
            v = v.rearrange("p (a b) -> p a b", a=free[0])
        elif len(free) == 3:
            v = v.rearrange("p (a b c) -> p a b c", a=free[0], b=free[1])
        self.bump += words
        self.ntile += 1
        return T("%s#%d" % (name, self.ntile), v)

    def mark(self):
        return self.bump

    def reset(self, mark):
        self.barrier()
        self.bump = mark

    def barrier(self):
        ops = []
        for st, lst in self.streams.items():
            n = NSLOT if st.startswith("dma_") else 1
            ops.extend(lst[-n:])
        for o in ops:
            o.signal = True
        self.barrier_ops = ops
        self.barrier_seen = set()

    def ps(self, name, shape, dt=F32):
        return T(name, self.es.enter_context(self.nc.psum_tensor(name, list(shape), dt))[:])

    def dram(self, name, shape, dt=F32, kind="Internal"):
        return self.nc.dram_tensor(name, list(shape), dt, kind=kind)

    @staticmethod
    def _k(r):
        if isinstance(r, (str, int)):
            return r
        if isinstance(r, tuple):
            return tuple(Prog._k(x) for x in r)
        return "T:" + r.name

    def _record(self, op, reads, writes, acc=False):
        reads = [self._k(r) for r in reads]
        writes = [self._k(r) for r in writes]
        deps = set()
        for r in reads:
            w = self.last_w.get(r)
            if w is not None:
                deps.add(w)
            if isinstance(r, str) and r.startswith("T:ps"):
                for rd in self.readers.get(r, ()):
                    if rd.stream != op.stream:
                        deps.add(rd)
        for wr in writes:
            w = self.last_w.get(wr)
            if w is not None and not (acc and w.stream == "tensor" and op.stream == "tensor"):
                deps.add(w)
            for rd in self.readers.get(wr, ()):
                deps.add(rd)
        if op.eng not in self.barrier_seen:
            self.barrier_seen.add(op.eng)
            for b in self.barrier_ops:
                deps.add(b)
        deps.discard(op)
        for d in deps:
            if d.stream == "tensor" and op.stream == "tensor":
                continue
            d.signal = True
            op.waits.append(d)
        for r in reads:
            self.readers.setdefault(r, []).append(op)
        for wr in writes:
            self.last_w[wr] = op
            self.readers[wr] = []
        op.idx = len(self.ops)
        self.ops.append(op)
        self.streams.setdefault(op.stream, []).append(op)

    def op(self, eng, fn, reads=(), writes=(), acc=False):
        o = Op()
        o.stream = eng
        o.eng = eng
        o.fn = fn
        o.is_dma = False
        self._record(o, reads, writes, acc)
        return o

    def dma(self, queue, out, in_, reads=(), writes=(), **kw):
        o = Op()
        o.stream = "dma_" + queue
        o.eng = queue
        o.fn = lambda e, out=out, in_=in_, kw=kw: e.dma_start(out=out, in_=in_, **kw)
        o.is_dma = True
        s = self.dma_slots.get(queue, 0)
        self.dma_slots[queue] = (s + 1) % NSLOT
        o.slot = s
        u = self.slot_uses.get((queue, s), 0) + 1
        self.slot_uses[(queue, s)] = u
        o.slot_use = u
        o.signal = True
        self._record(o, reads, writes)
        return o

    def finalize(self, final_wait_ops=()):
        nc = self.nc
        es = self.es
        for st in COMPUTE:
            self.sem[st] = es.enter_context(nc.semaphore("s_" + st))
        for q in self.dma_slots:
            for s in range(NSLOT):
                self.sem[("dma", q, s)] = es.enter_context(nc.semaphore("d_%s_%d" % (q, s)))
        for st in COMPUTE:
            c = 0
            for o in self.streams.get(st, ()):
                if o.signal:
                    c += 1
                    o.inc = c
        per_eng = {}
        for o in self.ops:
            per_eng.setdefault(o.eng, []).append(o)
        block = es.enter_context(nc.Block())

        def target(d):
            if d.is_dma:
                return self.sem[("dma", d.eng, d.slot)], 16 * d.slot_use
            return self.sem[d.stream], d.inc

        def emit(engname):
            ops = per_eng.get(engname, [])

            def body(e):
                waited = {}
                for o in ops:
                    ws = {}
                    for d in o.waits:
                        sem, val = target(d)
                        k = id(sem)
                        if waited.get(k, 0) >= val:
                            continue
                        if k not in ws or ws[k][1] < val:
                            ws[k] = (sem, val)
                    if o.is_dma and o.slot_use > 1:
                        sem = self.sem[("dma", o.eng, o.slot)]
                        val = 16 * (o.slot_use - 1)
                        k = id(sem)
                        if waited.get(k, 0) < val and (k not in ws or ws[k][1] < val):
                            ws[k] = (sem, val)
                    for k, (sem, val) in ws.items():
                        e.wait_ge(sem, val)
                        waited[k] = val
                    ins = o.fn(e)
                    if o.is_dma:
                        ins.then_inc(self.sem[("dma", o.eng, o.slot)], 16)
                    elif o.signal:
                        ins.then_inc(self.sem[o.stream], 1)
                if engname == "sync":
                    for d in final_wait_ops:
                        sem, val = target(d)
                        e.wait_ge(sem, val)
            return body

        for engname in ("sync", "scalar", "vector", "gpsimd", "tensor"):
            if engname in per_eng or engname == "sync":
                getattr(block, engname)(emit(engname))
        es.close()

import math
import numpy as np

U8 = mybir.dt.uint8
L = 4
D = 1024
TC = 256
TL = 4096
TT = TC + TL
NT = TT // 128
PIN = 3856
EPS = 1e-6
C_GDN = 0
C_HY = 1040
C_HG = 1808
C_DA = 3088


def host_consts():
    c = {}
    c["ident"] = np.eye(128, dtype=np.float32)
    c["ones"] = np.ones((128, 128), np.float32)
    j = np.arange(128)[:, None]
    i = np.arange(128)[None, :]
    same = (j // 64) == (i // 64)
    c["tri_f"] = (same & (j <= i)).astype(np.float32)
    c["tri_fs"] = (same & (j < i)).astype(np.float32)
    c["tri_b"] = (same & (j >= i)).astype(np.float32)
    c["tri_bs"] = (same & (j > i)).astype(np.float32)
    c["blk"] = same.astype(np.float32)
    return c


class Ctx:
    pass


def load_bcast(P, q, tile, src_ap, n):
    P.dma(q, tile[:], src_ap.partition_broadcast(128), writes=[tile])


def phase_mods(K, l):
    P = K.P
    nc = K.nc
    m0 = P.mark()
    sT = K.sT
    srep = P.tile("srep", [8, 2, 128])
    P.op("vector", lambda e: e.tensor_copy(srep[:], sT[:].unsqueeze(3).broadcast_to([128, 8, 2, 128])), reads=[sT], writes=[srep])
    fm = K.fm
    wblk = [P.tile("mw%d" % i, [8, 512]) for i in range(2)]
    mb = P.tile("mb", [48])
    P.dma("sync", mb[:], K.mod_bT[l], writes=[mb])
    grp = {0: 0, 1: 1, 3: 2, 4: 3}
    for gi, g in ((0, 2), (1, 5)):
        for s in range(2):
            load_bcast(P, "sync", K.G[gi][s], K.mod_b[l][g * 1024:(g + 1) * 1024], 1024)
    pb = 0
    for g in range(6):
        for hb in range(2):
            w = wblk[(g * 2 + hb) % 2]
            src = K.mod_w[l][:, g * 1024 + hb * 512:g * 1024 + hb * 512 + 512].rearrange("(c p) n -> p c n", p=128)
            P.dma("sync" if (g * 2 + hb) % 2 == 0 else "gpsimd", w[:], src, writes=[w])
            if g in grp:
                ps = K.ps[pb % 2]
                pb += 1
                for fb in range(4):
                    for c in range(8):
                        P.op("tensor", lambda e, ps=ps, w=w, fb=fb, c=c: e.matmul(
                            ps[:, fb * 2:fb * 2 + 2], w[:, c, fb * 128:(fb + 1) * 128], sT[:, c, :],
                            start=(c == 0), stop=(c == 7)), reads=[w, sT], writes=[ps], acc=True)
                gi = grp[g]
                for fb in range(4):
                    ch = hb * 4 + fb
                    P.op("vector", lambda e, ps=ps, fb=fb, gi=gi, ch=ch, g=g: e.tensor_scalar(
                        fm[:, gi, ch, :], ps[:, fb * 2:fb * 2 + 2], mb[:, g * 8 + ch:g * 8 + ch + 1], 1.0, ALU.add, ALU.mult),
                        reads=[ps, mb], writes=[fm])
            else:
                gi = 0 if g == 2 else 1
                for s in range(2):
                    ps = K.ps[2 + (pb % 2)]
                    pb += 1
                    for c in range(8):
                        P.op("tensor", lambda e, ps=ps, w=w, c=c, s=s: e.matmul(
                            ps[:, 0:512], srep[:, c, s, :], w[:, c, :], start=(c == 0), stop=(c == 7)),
                            reads=[w, srep], writes=[ps], acc=True)
                    G = K.G[gi][s]
                    P.op("vector", lambda e, ps=ps, G=G, hb=hb: e.tensor_tensor(
                        G[:, hb * 512:(hb + 1) * 512], ps[:, 0:512], G[:, hb * 512:(hb + 1) * 512], ALU.add),
                        reads=[ps, G], writes=[G])
    for (Av, lnw, si) in ((K.A1, K.ln1T, 1), (K.A2, K.ln2T, 3)):
        P.op("vector", lambda e, Av=Av, si=si: e.tensor_scalar(Av[:], fm[:, si, :, :], 1.0, 1.0, ALU.add, ALU.mult), reads=[fm], writes=[Av])
        P.op("vector", lambda e, Av=Av, lnw=lnw: e.tensor_tensor(
            Av[:], Av[:], lnw[:, l, :].unsqueeze(2).broadcast_to([128, 8, 2]), ALU.mult), reads=[Av, lnw], writes=[Av])
    P.reset(m0)


def rms_tile(K, xt, xn_bf, scr, ss, rs):
    P = K.P
    P.op("scalar", lambda e: e.activation(scr[:], xt[:], AF.Square, accum_out=ss[:]), reads=[xt], writes=[scr, ss])
    P.op("scalar", lambda e: e.activation(rs[:], ss[:], AF.Sqrt, bias=K.epsc[:, 0:1], scale=1.0 / D), reads=[ss, K.epsc], writes=[rs])
    P.op("vector", lambda e: e.reciprocal(rs[:], rs[:]), reads=[rs], writes=[rs])
    P.op("vector", lambda e: e.tensor_scalar(xn_bf[:], xt[:], rs[:, 0:1], 1.0, ALU.mult, ALU.mult), reads=[xt, rs], writes=[xn_bf])


def transpose_mod(K, xn_bf, hT, Av, Bv, s, psb_t, col0=0):
    P = K.P
    psb = psb_t.ap[:, 0:512].bitcast(BF16).rearrange("p (c t) -> p c t", c=8)
    for c in range(8):
        P.op("tensor", lambda e, c=c: e.transpose(psb[:, c, :], xn_bf[:, c * 128:(c + 1) * 128], K.identb[:]),
             reads=[xn_bf, K.identb], writes=[psb_t])
    tmp = K.tmod
    P.op("vector", lambda e: e.tensor_tensor(tmp[:], psb, Av[:, :, s:s + 1].broadcast_to([128, 8, 128]), ALU.mult),
         reads=[psb_t, Av], writes=[tmp])
    P.op("gpsimd", lambda e: e.tensor_tensor(hT[:, :, col0:col0 + 128], tmp[:], Bv[:, :, s:s + 1].broadcast_to([128, 8, 128]), ALU.add),
         reads=[tmp, Bv], writes=[hT])


def phase_proj(K, l):
    P = K.P
    m0 = P.mark()
    W = P.tile("win", [8, PIN], BF16)
    for c in range(8):
        P.dma("gpsimd", W[:, c, :], K.w_in[l][c * 128:(c + 1) * 128, :], writes=[W])
    xt = [P.tile("xt%d" % i, [D]) for i in range(2)]
    xn = [P.tile("xn%d" % i, [D], BF16) for i in range(2)]
    hT = [P.tile("hT%d" % i, [8, 128], BF16) for i in range(2)]
    ot = [P.tile("ot%d" % i, [PIN]) for i in range(2)]
    scr = P.tile("scr", [D])
    ss = [P.tile("ss%d" % i, [1]) for i in range(2)]
    rs = [P.tile("rs%d" % i, [1]) for i in range(2)]
    K.tmod = P.tile("tmod", [8, 128])
    Bv = K.fm
    P.dma("sync", xt[0][:], K.X[0:128, :], writes=[xt[0]])
    nblk = (PIN + 511) // 512
    for i in range(NT):
        b = i % 2
        if i + 1 < NT:
            P.dma("sync", xt[1 - b][:], K.X[(i + 1) * 128:(i + 2) * 128, :], writes=[xt[1 - b]])
        s = 1 if i < 2 else 0
        rms_tile(K, xt[b], xn[b], scr, ss[b], rs[b])
        transpose_mod(K, xn[b], hT[b], K.A1, T(K.fm.name, K.fm[:, 0, :, :]), s, K.ps[7])
        for nb in range(nblk):
            c0 = nb * 512
            w = min(512, PIN - c0)
            ps = K.ps[nb % 4]
            for c in range(8):
                P.op("tensor", lambda e, ps=ps, c=c, c0=c0, w=w, b=b: e.matmul(
                    ps[:, 0:w], hT[b][:, c, :], W[:, c, c0:c0 + w], start=(c == 0), stop=(c == 7)),
                    reads=[hT[b], W], writes=[ps], acc=True)
            if nb % 2 == 0:
                P.op("scalar", lambda e, ps=ps, c0=c0, w=w, b=b: e.copy(ot[b][:, c0:c0 + w], ps[:, 0:w]), reads=[ps], writes=[ot[b]])
            else:
                P.op("vector", lambda e, ps=ps, c0=c0, w=w, b=b: e.tensor_copy(ot[b][:, c0:c0 + w], ps[:, 0:w]), reads=[ps], writes=[ot[b]])
        P.dma("sync", K.Pj[i * 128:(i + 1) * 128, :], ot[b][:], reads=[ot[b]])
    P.reset(m0)


GQW = 768 + 16


def V(P, fn, reads, writes):
    return P.op("vector", fn, reads=reads, writes=writes)


def G_(P, fn, reads, writes):
    return P.op("gpsimd", fn, reads=reads, writes=writes)


def A_(P, fn, reads, writes):
    return P.op("scalar", fn, reads=reads, writes=writes)


def MM(P, ps, out_ap, lhsT, rhs, reads, start=True, stop=True):
    return P.op("tensor", lambda e: e.matmul(out_ap, lhsT, rhs, start=start, stop=stop), reads=reads, writes=[ps], acc=True)


def TR(P, ps, out_ap, in_ap, ident_ap, reads):
    return P.op("tensor", lambda e: e.transpose(out_ap, in_ap, ident_ap), reads=reads, writes=[ps])


def load_shift3(K, q, dst, src, i, c0, c1):
    P = K.P
    r0 = i * 128
    first = i in (0, 2)
    last = i in (1, NT - 1)
    if first:
        V(P, lambda e: e.memset(dst[0][:], 0.0), [], [dst[0]])
        P.dma(q, dst[0][1:128, :], src[r0:r0 + 127, c0:c1], writes=[dst[0]])
    else:
        P.dma(q, dst[0][:], src[r0 - 1:r0 + 127, c0:c1], writes=[dst[0]])
    P.dma(q, dst[1][:], src[r0:r0 + 128, c0:c1], writes=[dst[1]])
    if last:
        V(P, lambda e: e.memset(dst[2][:], 0.0), [], [dst[2]])
        P.dma(q, dst[2][0:127, :], src[r0 + 1:r0 + 128, c0:c1], writes=[dst[2]])
    else:
        P.dma(q, dst[2][:], src[r0 + 1:r0 + 129, c0:c1], writes=[dst[2]])


def conv3(K, x3, cw, acc, t0, t2, n):
    P = K.P
    G_(P, lambda e: e.tensor_tensor(t0[:, 0:n], x3[0][:, 0:n], cw[:, 0, 0:n], ALU.mult), [x3[0], cw], [t0])
    V(P, lambda e: e.tensor_tensor(acc[:, 0:n], x3[1][:, 0:n], cw[:, 1, 0:n], ALU.mult), [x3[1], cw], [acc])
    G_(P, lambda e: e.tensor_tensor(t2[:, 0:n], x3[2][:, 0:n], cw[:, 2, 0:n], ALU.mult), [x3[2], cw], [t2])
    V(P, lambda e: e.tensor_tensor(acc[:, 0:n], acc[:, 0:n], t0[:, 0:n], ALU.add), [acc, t0], [acc])
    V(P, lambda e: e.tensor_tensor(acc[:, 0:n], acc[:, 0:n], t2[:, 0:n], ALU.add), [acc, t2], [acc])


def phase_gdn_pre(K, l):
    P = K.P
    m0 = P.mark()
    cw = P.tile("cw", [3, 768])
    for k in range(3):
        P.dma("sync", cw[:, k, :], K.gdn_conv_w[l][k].partition_broadcast(128), writes=[cw])
    negA = P.tile("negA", [8])
    dtb = P.tile("dtb", [8])
    P.dma("sync", negA[:], K.gdn_a_log[l].rearrange("a b -> (a b)").partition_broadcast(128), writes=[negA])
    P.dma("sync", dtb[:], K.gdn_dt_bias[l].rearrange("a b -> (a b)").partition_broadcast(128), writes=[dtb])
    A_(P, lambda e: e.activation(negA[:], negA[:], AF.Exp), [negA], [negA])
    V(P, lambda e: e.tensor_scalar(negA[:], negA[:], -1.0, 1.0, ALU.mult, ALU.mult), [negA], [negA])
    x3 = [[P.tile("x3_%d_%d" % (b, k), [768]) for k in range(3)] for b in range(2)]
    zab = [P.tile("zab%d" % b, [16]) for b in range(2)]
    acc = P.tile("acc", [768])
    t0 = P.tile("t0", [768])
    t2 = P.tile("t2", [768])
    sq = P.tile("sq", [512])
    ssum = P.tile("ssum", [8])
    tg = P.tile("tg", [8])
    tb = P.tile("tb", [8])
    og = [P.tile("og%d" % b, [GQW]) for b in range(2)]

    def loads(i):
        b = i % 2
        load_shift3(K, "sync", x3[b], K.Pj, i, 0, 768)
        P.dma("sync", zab[b][:], K.Pj[i * 128:(i + 1) * 128, 1024:1040], writes=[zab[b]])
    loads(0)
    for i in range(NT):
        b = i % 2
        if i + 1 < NT:
            loads(i + 1)
        o = og[b]
        conv3(K, x3[b], cw, acc, t0, t2, 768)
        A_(P, lambda e, o=o: e.activation(o[:, 0:768], acc[:], AF.Silu), [acc], [o])
        G_(P, lambda e, o=o: e.tensor_tensor(sq[:], o[:, 0:512], o[:, 0:512], ALU.mult), [o], [sq])
        V(P, lambda e: e.tensor_reduce(ssum[:], sq[:].rearrange("p (g d) -> p g d", g=8), AX.X, ALU.add), [sq], [ssum])
        A_(P, lambda e: e.activation(ssum[:], ssum[:], AF.Sqrt, bias=K.epsc[:, 0:1], scale=1.0), [ssum, K.epsc], [ssum])
        V(P, lambda e: e.reciprocal(ssum[:], ssum[:]), [ssum], [ssum])
        V(P, lambda e: e.tensor_scalar(ssum[:, 0:4], ssum[:, 0:4], 0.125, 1.0, ALU.mult, ALU.mult), [ssum], [ssum])
        V(P, lambda e, o=o: e.tensor_tensor(o[:, 0:512].rearrange("p (g d) -> p g d", g=8), o[:, 0:512].rearrange("p (g d) -> p g d", g=8),
                                            ssum[:].unsqueeze(2).broadcast_to([128, 8, 64]), ALU.mult), [o, ssum], [o])
        z = zab[b]
        V(P, lambda e, z=z: e.tensor_tensor(tg[:], z[:, 0:8], dtb[:], ALU.add), [z, dtb], [tg])
        A_(P, lambda e: e.activation(tg[:], tg[:], AF.Exp), [tg], [tg])
        A_(P, lambda e: e.activation(tg[:], tg[:], AF.Ln, bias=K.onec[:, 0:1], scale=1.0), [tg, K.onec], [tg])
        V(P, lambda e: e.tensor_tensor(tg[:], tg[:], negA[:], ALU.mult), [tg, negA], [tg])
        A_(P, lambda e, z=z: e.activation(tb[:], z[:, 8:16], AF.Sigmoid), [z], [tb])
        for gb, src in ((0, tg), (1, tb)):
            V(P, lambda e, o=o, gb=gb, src=src: e.tensor_copy(
                o[:, 768:784].rearrange("p (d hh gb pr) -> p d hh gb pr", d=2, hh=2, gb=2)[:, :, :, gb, :],
                src[:].rearrange("p (d pr hh) -> p d hh pr", d=2, pr=2)), [src], [o])
        P.dma("sync", K.GQ[i * 128:(i + 1) * 128, :], o[:], reads=[o])
    P.reset(m0)


def gdn_scan_gen(K, l, d, nps):
    P = K.P
    tri = P.tile("tri", [128])
    tris = P.tile("tris", [128])
    blk = P.tile("blk", [128])
    P.dma("sync", tri[:], K.cin["tri_f" if d == 0 else "tri_b"], writes=[tri])
    P.dma("sync", tris[:], K.cin["tri_fs" if d == 0 else "tri_bs"], writes=[tris])
    P.dma("sync", blk[:], K.cin["blk"], writes=[blk])
    ident, ones = K.ident, K.ones
    S = [P.tile("S%d" % p, [64]) for p in range(2)]
    for p in range(2):
        V(P, lambda e, p=p: e.memset(S[p][:], 0.0), [], [S[p]])
    NB = 2
    qkv = [P.tile("qkv%d" % b, [3, 2, 64]) for b in range(NB)]
    gb = [P.tile("gb%d" % b, [2, 2]) for b in range(NB)]
    ot = [P.tile("ogo%d" % b, [2, 64]) for b in range(NB)]
    def pt(name, free):
        return [P.tile("%s%d" % (name, p), free) for p in range(2)]
    gc = P.tile("gc", [2]); egc = P.tile("egc", [2]); glt = P.tile("glt", [2]); eglt = P.tile("eglt", [2]); ekl = P.tile("ekl", [2]); nbeta = P.tile("nbeta", [2])
    dg = pt("dg", [256]); rows = pt("rows", [256]); E = pt("E", [128]); Dm = pt("Dm", [128]); Dms = pt("Dms", [128])
    kT = pt("kT", [128]); qT = pt("qT", [128]); NTm = pt("NTm", [128]); Nm = pt("Nm", [128]); PT2 = pt("PT2", [128]); P2 = pt("P2", [128])
    RT = pt("RT", [128]); aT = pt("aT", [128]); MTb = pt("MTb", [128]); kg0 = pt("kg0", [128]); kgl = pt("kgl", [128]); qgb = pt("qgb", [128])
    wT = pt("wT", [128]); qgT = pt("qgT", [128]); u = pt("u", [64]); vn = pt("vn", [64])
    for p in range(2):
        for t in (kg0[p], kgl[p], qgb[p]):
            G_(P, lambda e, t=t: e.memset(t[:], 0.0), [], [t])
    ctx_ch = list(range(0, TC // 64))
    lat_ch = list(range(TC // 64, TT // 64))
    order = ctx_ch + lat_ch if d == 0 else ctx_ch[::-1] + lat_ch[::-1]
    import os
    if os.environ.get("GDN_NCH"):
        order = order[:int(os.environ["GDN_NCH"])]

    def loads(ci):
        c = order[ci]
        b = ci % NB
        r0 = c * 64
        for hh in range(2):
            src = K.GQ[r0:r0 + 64, 0:768].rearrange("r (t pr hh e) -> r t pr hh e", t=3, pr=2, hh=2)[:, :, :, hh, :]
            P.dma("sync", qkv[b][hh * 64:(hh + 1) * 64, :, :, :], src, writes=[qkv[b]])
            c0 = 768 + d * 8 + hh * 4
            P.dma("sync", gb[b][hh * 64:(hh + 1) * 64, :, :], K.GQ[r0:r0 + 64, c0:c0 + 4].rearrange("r (a b) -> r a b", a=2), writes=[gb[b]])
    loads(0)
    for ci in range(len(order)):
        c = order[ci]
        b = ci % NB
        if ci + 1 < len(order):
            loads(ci + 1)
        Q = qkv[b]
        g = gb[b]
        ps = nps()
        MM(P, ps, ps[:, 0:2], tri[:], g[:, 0, :], [tri, g])
        MM(P, ps, ps[:, 2:4], blk[:], g[:, 0, :], [blk, g])
        V(P, lambda e, ps=ps: e.tensor_copy(gc[:], ps[:, 0:2]), [ps], [gc])
        V(P, lambda e, ps=ps: e.tensor_copy(glt[:], ps[:, 2:4]), [ps], [glt])
        A_(P, lambda e: e.activation(egc[:], gc[:], AF.Exp), [gc], [egc])
        A_(P, lambda e: e.activation(eglt[:], glt[:], AF.Exp), [glt], [eglt])
        V(P, lambda e: e.tensor_tensor(ekl[:], glt[:], gc[:], ALU.subtract), [glt, gc], [ekl])
        A_(P, lambda e: e.activation(ekl[:], ekl[:], AF.Exp), [ekl], [ekl])
        V(P, lambda e, g=g: e.tensor_scalar(nbeta[:], g[:, 1, :], -1.0, 1.0, ALU.mult, ALU.mult), [g], [nbeta])
        yield

        def pair_gen(p, Q=Q, g=g, b=b):
            kn = Q[:, 1, p, :]
            qn = Q[:, 0, p, :]
            vv = Q[:, 2, p, :]
            V(P, lambda e, p=p: e.tensor_scalar(dg[p][:, 0:128], ident[:], gc[:, p:p + 1], 1.0, ALU.mult, ALU.mult), [ident, gc], [dg[p]])
            G_(P, lambda e, p=p, g=g: e.tensor_scalar(dg[p][:, 128:256], ident[:], g[:, 1, p:p + 1], 1.0, ALU.mult, ALU.mult), [ident, g], [dg[p]])
            yield
            ps = nps()
            psb_ = nps()
            MM(P, ps, ps[:, 0:128], ones[:], dg[p][:, 0:128], [ones, dg[p]])
            MM(P, psb_, psb_[:, 0:128], ones[:], dg[p][:, 128:256], [ones, dg[p]])
            yield
            A_(P, lambda e, p=p, ps=psb_: e.copy(rows[p][:, 128:256], ps[:, 0:128]), [psb_], [rows[p]])
            V(P, lambda e, p=p, ps=ps: e.tensor_scalar(E[p][:], ps[:, 0:128], gc[:, p:p + 1], 0.0, ALU.subtract, ALU.min), [ps, gc], [E[p]])
            yield
            A_(P, lambda e, p=p: e.activation(E[p][:], E[p][:], AF.Exp), [E[p]], [E[p]])
            yield
            G_(P, lambda e, p=p: e.tensor_tensor(Dm[p][:], E[p][:], tri[:], ALU.mult), [E[p], tri], [Dm[p]])
            G_(P, lambda e, p=p: e.tensor_tensor(Dms[p][:], E[p][:], tris[:], ALU.mult), [E[p], tris], [Dms[p]])
            yield
            ps = nps()
            psb_ = nps()
            TR(P, ps, ps[0:64, 0:128], kn, ident[:], [Q, ident])
            TR(P, psb_, psb_[0:64, 0:128], qn, ident[:], [Q, ident])
            yield
            A_(P, lambda e, p=p, ps=ps: e.copy(kT[p][0:64, :], ps[0:64, 0:128]), [ps], [kT[p]])
            V(P, lambda e, p=p, ps=psb_: e.tensor_copy(qT[p][0:64, :], ps[0:64, 0:128]), [psb_], [qT[p]])
            yield
            ps = nps()
            MM(P, ps, ps[:, 0:128], kT[p][0:64, :], kT[p][0:64, :], [kT[p]])
            MM(P, ps, ps[:, 128:256], kT[p][0:64, :], qT[p][0:64, :], [kT[p], qT[p]])
            yield
            V(P, lambda e, p=p, ps=ps: e.scalar_tensor_tensor(NTm[p][:], ps[:, 0:128], nbeta[:, p:p + 1], Dms[p][:], ALU.mult, ALU.mult),
              [ps, nbeta, Dms[p]], [NTm[p]])
            V(P, lambda e, p=p, ps=ps: e.tensor_tensor(aT[p][:], ps[:, 128:256], Dm[p][:], ALU.mult), [ps, Dm[p]], [aT[p]])
            yield
            ps = nps()
            TR(P, ps, ps[:, 0:128], NTm[p][:], ident[:], [NTm[p], ident])
            yield
            A_(P, lambda e, p=p, ps=ps: e.copy(Nm[p][:], ps[:, 0:128]), [ps], [Nm[p]])
            G_(P, lambda e, p=p: e.tensor_tensor(RT[p][:], NTm[p][:], ident[:], ALU.add), [NTm[p], ident], [RT[p]])
            Pk, PTk = Nm[p], NTm[p]
            Pn_t, PTn_t = P2[p], PT2[p]
            for k in range(5):
                yield
                ps = nps()
                MM(P, ps, ps[:, 0:128], PTk[:], Pk[:], [PTk, Pk])
                if k < 4:
                    psb_ = nps()
                    MM(P, psb_, psb_[:, 0:128], Pk[:], PTk[:], [PTk, Pk])
                yield
                A_(P, lambda e, ps=ps, t=Pn_t: e.copy(t[:], ps[:, 0:128]), [ps], [Pn_t])
                if k < 4:
                    V(P, lambda e, ps=psb_, t=PTn_t: e.tensor_copy(t[:], ps[:, 0:128]), [psb_], [PTn_t])
                yield
                ps2 = nps()
                MM(P, ps2, ps2[:, 0:128], Pn_t[:], RT[p][:], [Pn_t, RT[p]])
                yield
                V(P, lambda e, ps2=ps2, p=p: e.tensor_tensor(RT[p][:], ps2[:, 0:128], RT[p][:], ALU.add), [ps2, RT[p]], [RT[p]])
                Pk, PTk, Pn_t, PTn_t = Pn_t, PTn_t, Pk, PTk
            yield
            G_(P, lambda e, p=p: e.tensor_tensor(MTb[p][:], RT[p][:], rows[p][:, 128:256], ALU.mult), [RT[p], rows[p]], [MTb[p]])
            for hh in range(2):
                r = slice(hh * 64, hh * 64 + 64)
                V(P, lambda e, p=p, r=r, kn=kn: e.tensor_scalar(kg0[p][r, r], kn[r, :], egc[r, p:p + 1], 1.0, ALU.mult, ALU.mult), [Q, egc], [kg0[p]])
                G_(P, lambda e, p=p, r=r, kn=kn: e.tensor_scalar(kgl[p][r, r], kn[r, :], ekl[r, p:p + 1], 1.0, ALU.mult, ALU.mult), [Q, ekl], [kgl[p]])
                V(P, lambda e, p=p, r=r, qn=qn: e.tensor_scalar(qgb[p][r, r], qn[r, :], egc[r, p:p + 1], 1.0, ALU.mult, ALU.mult), [Q, egc], [qgb[p]])
            yield
            ps = nps()
            psb_ = nps()
            MM(P, ps, ps[:, 0:64], MTb[p][:], vv, [MTb[p], Q])
            MM(P, psb_, psb_[:, 0:128], kg0[p][:], MTb[p][:], [kg0[p], MTb[p]])
            TR(P, ps, ps[:, 256:384], qgb[p][:], ident[:], [qgb[p], ident])
            yield
            A_(P, lambda e, p=p, ps=ps: e.copy(u[p][:], ps[:, 0:64]), [ps], [u[p]])
            V(P, lambda e, p=p, ps=psb_: e.tensor_copy(wT[p][:], ps[:, 0:128]), [psb_], [wT[p]])
            A_(P, lambda e, p=p, ps=ps: e.copy(qgT[p][:], ps[:, 256:384]), [ps], [qgT[p]])
            yield
            ps = nps()
            MM(P, ps, ps[:, 0:64], wT[p][:], S[p][:], [wT[p], S[p]])
            yield
            V(P, lambda e, p=p, ps=ps: e.tensor_tensor(vn[p][:], u[p][:], ps[:, 0:64], ALU.subtract), [u[p], ps], [vn[p]])
            yield
            ps = nps()
            psb_ = nps()
            MM(P, ps, ps[:, 0:64], qgT[p][:], S[p][:], [qgT[p], S[p]], start=True, stop=False)
            MM(P, ps, ps[:, 0:64], aT[p][:], vn[p][:], [aT[p], vn[p]], start=False, stop=True)
            MM(P, psb_, psb_[:, 0:64], kgl[p][:], vn[p][:], [kgl[p], vn[p]])
            yield
            A_(P, lambda e, p=p, ps=ps, b=b: e.copy(ot[b][:, p, :], ps[:, 0:64]), [ps], [ot[b]])
            V(P, lambda e, p=p, ps=psb_: e.scalar_tensor_tensor(S[p][:], S[p][:], eglt[:, p:p + 1], ps[:, 0:64], ALU.mult, ALU.add),
              [S[p], eglt, psb_], [S[p]])
        pg = [pair_gen(0), pair_gen(1)]
        while pg:
            for gg_ in pg[:]:
                try:
                    next(gg_)
                except StopIteration:
                    pg.remove(gg_)
            yield
        r0 = c * 64
        for hh in range(2):
            dst = K.OG[d][r0:r0 + 64, :].rearrange("r (pr hh e) -> r pr hh e", pr=2, hh=2)[:, :, hh, :]
            P.dma("sync", dst, ot[b][hh * 64:(hh + 1) * 64, :, :], reads=[ot[b]])


def lockstep(gens):
    gens = list(gens)
    while gens:
        for g_ in gens[:]:
            try:
                next(g_)
            except StopIteration:
                gens.remove(g_)


def phase_gdn_scan_both(K, l):
    P = K.P
    m0 = P.mark()
    psn = [0]

    def nps():
        psn[0] = (psn[0] + 1) % 8
        return K.ps[psn[0]]
    lockstep([gdn_scan_gen(K, l, 0, nps), gdn_scan_gen(K, l, 1, nps)])
    P.reset(m0)


def phase_gdn_scan(K, l, d):
    P = K.P
    m0 = P.mark()
    psn = [0]

    def nps():
        psn[0] = (psn[0] + 1) % 8
        return K.ps[psn[0]]
    lockstep([gdn_scan_gen(K, l, d, nps)])
    P.reset(m0)


def phase_gdn_fin(K, l):
    P = K.P
    m0 = P.mark()
    nw = P.tile("gnw", [64])
    P.dma("sync", nw[:], K.gdn_norm_w[l].partition_broadcast(128), writes=[nw])
    o0 = [P.tile("o0_%d" % b, [256]) for b in range(2)]
    o1 = [P.tile("o1_%d" % b, [256]) for b in range(2)]
    zt = [P.tile("zt%d" % b, [256]) for b in range(2)]
    sq = P.tile("sq", [256]); ss = P.tile("ss", [4])

    def loads(i):
        b = i % 2
        rs = slice(i * 128, (i + 1) * 128)
        P.dma("sync", o0[b][:], K.OG[0][rs, :], writes=[o0[b]])
        P.dma("sync", o1[b][:], K.OG[1][rs, :], writes=[o1[b]])
        P.dma("sync", zt[b][:], K.Pj[rs, 768:1024], writes=[zt[b]])
    loads(0)
    for i in range(NT):
        b = i % 2
        if i + 1 < NT:
            loads(i + 1)
        o = o0[b]
        V(P, lambda e, o=o, b=b: e.tensor_tensor(o[:], o[:], o1[b][:], ALU.add), [o, o1[b]], [o])
        head_rms_gate(K, o, zt[b], nw, sq, ss, 4, 64, 1.0)
        P.dma("sync", K.O[i * 128:(i + 1) * 128, 0:256], o[:], reads=[o])
    P.reset(m0)


def head_rms_gate(K, o, zt, nw, sq, ss, nh, hd, mult):
    P = K.P
    n = nh * hd
    G_(P, lambda e: e.tensor_tensor(sq[:, 0:n], o[:, 0:n], o[:, 0:n], ALU.mult), [o], [sq])
    V(P, lambda e: e.tensor_reduce(ss[:, 0:nh], sq[:, 0:n].rearrange("p (g d) -> p g d", g=nh), AX.X, ALU.add), [sq], [ss])
    A_(P, lambda e: e.activation(ss[:, 0:nh], ss[:, 0:nh], AF.Sqrt, bias=K.epsc[:, 0:1], scale=1.0 / hd), [ss, K.epsc], [ss])
    V(P, lambda e: e.reciprocal(ss[:, 0:nh], ss[:, 0:nh]), [ss], [ss])
    if mult != 1.0:
        V(P, lambda e: e.tensor_scalar(ss[:, 0:nh], ss[:, 0:nh], mult, 1.0, ALU.mult, ALU.mult), [ss], [ss])
    o3 = o[:, 0:n].rearrange("p (g d) -> p g d", g=nh)
    V(P, lambda e: e.tensor_tensor(o3, o3, ss[:, 0:nh].unsqueeze(2).broadcast_to([128, nh, hd]), ALU.mult), [o, ss], [o])
    G_(P, lambda e: e.tensor_tensor(o3, o3, nw[:, 0:hd].unsqueeze(1).broadcast_to([128, nh, hd]), ALU.mult), [o, nw], [o])
    if zt is not None:
        A_(P, lambda e: e.activation(zt[:, 0:n], zt[:, 0:n], AF.Silu), [zt], [zt])
        V(P, lambda e: e.tensor_tensor(o[:, 0:n], o[:, 0:n], zt[:, 0:n], ALU.mult), [o, zt], [o])


def phase_wout(K, l):
    P = K.P
    m0 = P.mark()
    W = P.tile("wout", [8, D], BF16)
    for c in range(8):
        P.dma("gpsimd", W[:, c, :], K.w_out[l][c * 128:(c + 1) * 128, :], writes=[W])
    ot = [P.tile("wo_o%d" % b, [D]) for b in range(2)]
    xt = [P.tile("wo_x%d" % b, [D]) for b in range(2)]
    ob = P.tile("wo_ob", [D], BF16)
    oT = P.tile("wo_oT", [8, 128], BF16)
    tmp = P.tile("wo_tmp", [512])
    psb_t = K.ps[7]
    psb = psb_t.ap[:, 0:512].bitcast(BF16).rearrange("p (c t) -> p c t", c=8)

    def loads(i):
        b = i % 2
        rs = slice(i * 128, (i + 1) * 128)
        P.dma("sync", ot[b][:], K.O[rs, :], writes=[ot[b]])
        P.dma("sync", xt[b][:], K.X[rs, :], writes=[xt[b]])
    loads(0)
    for i in range(NT):
        b = i % 2
        if i + 1 < NT:
            loads(i + 1)
        s = 1 if i < 2 else 0
        A_(P, lambda e, b=b: e.copy(ob[:], ot[b][:]), [ot[b]], [ob])
        for c in range(8):
            TR(P, psb_t, psb[:, c, :], ob[:, c * 128:(c + 1) * 128], K.identb[:], [ob, K.identb])
        V(P, lambda e: e.tensor_copy(oT[:], psb), [psb_t], [oT])
        for nb in range(2):
            ps = K.ps[nb]
            for c in range(8):
                MM(P, ps, ps[:, 0:512], oT[:, c, :], W[:, c, nb * 512:(nb + 1) * 512], [oT, W], start=(c == 0), stop=(c == 7))
            G1 = K.G[0][s]
            V(P, lambda e, ps=ps, nb=nb, G1=G1: e.tensor_tensor(tmp[:], ps[:, 0:512], G1[:, nb * 512:(nb + 1) * 512], ALU.mult), [ps, G1], [tmp])
            G_(P, lambda e, nb=nb, b=b: e.tensor_tensor(xt[b][:, nb * 512:(nb + 1) * 512], xt[b][:, nb * 512:(nb + 1) * 512], tmp[:], ALU.add), [xt[b], tmp], [xt[b]])
        P.dma("sync", K.X[i * 128:(i + 1) * 128, :], xt[b][:], reads=[xt[b]])
    P.reset(m0)


def phase_mlp(K, l, last):
    P = K.P
    m0 = P.mark()
    W1 = P.tile("w1", [8, 4 * D], BF16)
    W2 = P.tile("w2", [32, D], BF16)
    for c in range(8):
        P.dma("gpsimd", W1[:, c, :], K.mlp_w1[l][c * 128:(c + 1) * 128, :], writes=[W1])
    for c in range(8):
        P.dma("gpsimd", W2[:, c * 4:(c + 1) * 4, :], K.mlp_w2[l][c * 512:(c + 1) * 512, :].rearrange("(f p) n -> p f n", p=128), writes=[W2])
    GT = 2
    NG = NT // GT
    xt = [[P.tile("ml_x%d_%d" % (b, t), [D]) for t in range(GT)] for b in range(2)]
    xn = P.tile("ml_xn", [D], BF16)
    hT = P.tile("ml_hT", [8, GT * 128], BF16)
    hid = P.tile("ml_hid", [32, GT * 128], BF16)
    K.tmod = P.tile("ml_tmod", [8, 128])
    scr = T(K.tmod.name, K.tmod[:].rearrange("p a b -> p (a b)"))
    ss = P.tile("ml_ss", [1]); rs = P.tile("ml_rs", [1])
    rl = [P.tile("ml_rl%d" % b, [GT * 128]) for b in range(2)]
    tmp = P.tile("ml_tmp", [512])
    B2 = T(K.fm.name, K.fm[:, 2, :, :])

    def loads(g):
        b = g % 2
        for t in range(GT):
            i = g * GT + t
            P.dma("sync", xt[b][t][:], K.X[i * 128:(i + 1) * 128, :], writes=[xt[b][t]])
    loads(0)
    for g in range(NG):
        b = g % 2
        if g + 1 < NG:
            loads(g + 1)
        s = 1 if g == 0 else 0
        for t in range(GT):
            rms_tile(K, xt[b][t], xn, scr, ss, rs)
            transpose_mod(K, xn, hT, K.A2, B2, s, K.ps[7], col0=t * 128)
        for fb in range(32):
            ps = K.ps[fb % 4]
            for c in range(8):
                MM(P, ps, ps[:, 0:GT * 128], W1[:, c, fb * 128:(fb + 1) * 128], hT[:, c, :], [W1, hT], start=(c == 0), stop=(c == 7))
            r = rl[fb % 2]
            A_(P, lambda e, ps=ps, r=r: e.activation(r[:], ps[:, 0:GT * 128], AF.Relu), [ps], [r])
            G_(P, lambda e, r=r, fb=fb: e.tensor_tensor(hid[:, fb, :], r[:], r[:], ALU.mult), [r], [hid])
        G2 = K.G[1][s]
        for t in range(GT):
            i = g * GT + t
            x = xt[b][t]
            for nb in range(2):
                ps = K.ps[4 + nb]
                for fb in range(32):
                    MM(P, ps, ps[:, 0:512], hid[:, fb, t * 128:(t + 1) * 128], W2[:, fb, nb * 512:(nb + 1) * 512], [hid, W2], start=(fb == 0), stop=(fb == 31))
                V(P, lambda e, ps=ps, nb=nb, G2=G2: e.tensor_tensor(tmp[:], ps[:, 0:512], G2[:, nb * 512:(nb + 1) * 512], ALU.mult), [ps, G2], [tmp])
                V(P, lambda e, nb=nb, x=x: e.tensor_tensor(x[:, nb * 512:(nb + 1) * 512], x[:, nb * 512:(nb + 1) * 512], tmp[:], ALU.add), [x, tmp], [x])
            if last:
                if i >= 2:
                    K.final_ops.append(P.dma("sync", K.out[(i - 2) * 128:(i - 1) * 128, :], x[:], reads=[x]))
            else:
                P.dma("sync", K.X[i * 128:(i + 1) * 128, :], x[:], reads=[x])
    P.reset(m0)


def rope_tables():
    n_freq = 8
    inv = 10000.0 ** (-np.arange(n_freq, dtype=np.float64) / n_freq)
    t = np.arange(TL)
    ang_r = (t // 64)[:, None] * inv
    ang_c = (t % 64)[:, None] * inv
    C = np.ones((TT, 32), np.float64)
    S = np.zeros((TT, 32), np.float64)
    C[TC:, 0:8] = np.cos(ang_r); C[TC:, 8:16] = np.cos(ang_r)
    C[TC:, 16:24] = np.cos(ang_c); C[TC:, 24:32] = np.cos(ang_c)
    S[TC:, 0:8] = -np.sin(ang_r); S[TC:, 8:16] = np.sin(ang_r)
    S[TC:, 16:24] = -np.sin(ang_c); S[TC:, 24:32] = np.sin(ang_c)
    return C.astype(np.float32), S.astype(np.float32)


def phase_attn(K, l):
    P = K.P
    m0 = P.mark()
    lam_init = 0.8 - 0.6 * math.exp(-0.3 * l)
    qT = P.tile("qT", [4, TT], BF16, parts=64)
    kT = P.tile("kT", [4, TT], BF16, parts=64)
    vaug = P.tile("vaug", [NT, 4, 65], BF16)
    nw = P.tile("nwqk", [16, 32])
    subw = P.tile("subw", [64])
    lamt = P.tile("lamt", [128])
    lamv = P.tile("lamv", [4])
    for g in range(16):
        src = K.da_q_norm[l] if g < 8 else K.da_k_norm[l]
        P.dma("sync", nw[:, g, :], src.partition_broadcast(128), writes=[nw])
    P.dma("sync", subw[:], K.da_subln[l].partition_broadcast(128), writes=[subw])
    P.dma("sync", lamt[:], K.da_lam[l].rearrange("a b -> (a b)").partition_broadcast(128), writes=[lamt])
    V(P, lambda e: e.memset(vaug[:], 1.0), [], [vaug])
    V(P, lambda e: e.tensor_tensor(lamt[:, 0:32], lamt[:, 0:32], lamt[:, 32:64], ALU.mult), [lamt], [lamt])
    V(P, lambda e: e.tensor_tensor(lamt[:, 64:96], lamt[:, 64:96], lamt[:, 96:128], ALU.mult), [lamt], [lamt])
    V(P, lambda e: e.tensor_reduce(lamv[:, 0:1], lamt[:, 0:32], AX.X, ALU.add), [lamt], [lamv])
    V(P, lambda e: e.tensor_reduce(lamv[:, 1:2], lamt[:, 64:96], AX.X, ALU.add), [lamt], [lamv])
    A_(P, lambda e: e.activation(lamv[:, 0:2], lamv[:, 0:2], AF.Exp), [lamv], [lamv])
    V(P, lambda e: e.tensor_tensor(lamv[:, 2:3], lamv[:, 1:2], lamv[:, 0:1], ALU.subtract), [lamv], [lamv])
    V(P, lambda e: e.tensor_scalar(lamv[:, 2:3], lamv[:, 2:3], -lam_init, 1.0, ALU.add, ALU.mult), [lamv], [lamv])
    qk = [P.tile("qk%d" % b, [512]) for b in range(2)]
    vt = [P.tile("vt%d" % b, [256]) for b in range(2)]
    rc = [P.tile("rc%d" % b, [32]) for b in range(2)]
    rsn = [P.tile("rsn%d" % b, [32]) for b in range(2)]
    sq = P.tile("asq", [512]); ss = P.tile("ass", [16]); t1 = P.tile("at1", [512]); t2 = P.tile("at2", [512])
    qkb = P.tile("qkb", [512], BF16)
    pa, pb = K.ps[6], K.ps[7]
    pav = pa.ap[:, 0:256].bitcast(BF16).rearrange("p (c t) -> p c t", c=4)
    pbv = pb.ap[:, 0:256].bitcast(BF16).rearrange("p (c t) -> p c t", c=4)

    def loads(i):
        b = i % 2
        rs = slice(i * 128, (i + 1) * 128)
        P.dma("sync", qk[b][:], K.Pj[rs, C_DA:C_DA + 512], writes=[qk[b]])
        P.dma("sync", vt[b][:], K.Pj[rs, C_DA + 512:C_DA + 768], writes=[vt[b]])
        P.dma("sync", rc[b][:], K.cin["ropeC"][rs, :], writes=[rc[b]])
        P.dma("sync", rsn[b][:], K.cin["ropeS"][rs, :], writes=[rsn[b]])
    loads(0)
    for i in range(NT):
        b = i % 2
        if i + 1 < NT:
            loads(i + 1)
        x = qk[b]
        G_(P, lambda e, x=x: e.tensor_tensor(sq[:], x[:], x[:], ALU.mult), [x], [sq])
        V(P, lambda e: e.tensor_reduce(ss[:], sq[:].rearrange("p (g d) -> p g d", g=16), AX.X, ALU.add), [sq], [ss])
        A_(P, lambda e: e.activation(ss[:], ss[:], AF.Sqrt, bias=K.epsc[:, 0:1], scale=1.0 / 32), [ss, K.epsc], [ss])
        V(P, lambda e: e.reciprocal(ss[:], ss[:]), [ss], [ss])
        x3 = x[:].rearrange("p (g d) -> p g d", g=16)
        V(P, lambda e, x3=x3: e.tensor_tensor(x3, x3, ss[:].unsqueeze(2).broadcast_to([128, 16, 32]), ALU.mult), [x, ss], [x])
        G_(P, lambda e, x3=x3: e.tensor_tensor(x3, x3, nw[:], ALU.mult), [x, nw], [x])
        cb = rc[b]; sb = rsn[b]
        V(P, lambda e, x3=x3, cb=cb: e.tensor_tensor(t1[:].rearrange("p (g d) -> p g d", g=16), x3, cb[:].unsqueeze(1).broadcast_to([128, 16, 32]), ALU.mult), [x, cb], [t1])
        x5 = x[:].rearrange("p (g r h e) -> p g r h e", g=16, r=2, h=2)
        t5 = t2[:].rearrange("p (g r h e) -> p g r h e", g=16, r=2, h=2)
        s4 = sb[:].rearrange("p (r h e) -> p r h e", r=2, h=2)
        for h in range(2):
            G_(P, lambda e, h=h, x5=x5, t5=t5, s4=s4: e.tensor_tensor(t5[:, :, :, h, :], x5[:, :, :, 1 - h, :],
                                                                   s4[:, :, h, :].unsqueeze(1).broadcast_to([128, 16, 2, 8]), ALU.mult), [x, sb], [t2])
        V(P, lambda e: e.tensor_tensor(qkb[:], t1[:], t2[:], ALU.add), [t1, t2], [qkb])
        for h in range(4):
            TR(P, pa, pav[0:64, h, :], qkb[:, h * 64:(h + 1) * 64], K.identb[:], [qkb, K.identb])
        for h in range(4):
            TR(P, pb, pbv[0:64, h, :], qkb[:, 256 + h * 64:256 + (h + 1) * 64], K.identb[:], [qkb, K.identb])
        V(P, lambda e, i=i: e.tensor_copy(qT[:, :, i * 128:(i + 1) * 128], pav[0:64, :, :]), [pa], [qT])
        A_(P, lambda e, i=i: e.copy(kT[:, :, i * 128:(i + 1) * 128], pbv[0:64, :, :]), [pb], [kT])
        G_(P, lambda e, i=i, b=b: e.tensor_copy(vaug[:, i, :, 0:64], vt[b][:].rearrange("p (h e) -> p h e", h=4)), [vt[b]], [vaug])
    scale = 32 ** -0.5
    pT = [P.tile("pT%d" % b, [512], BF16) for b in range(3)]
    osb = [P.tile("osb%d" % j, [512], parts=65) for j in range(2)]
    obuf = P.tile("obuf", [4, 256])
    o1t = P.tile("o1t", [64]); rz = P.tile("rz", [2])
    asq = P.tile("bsq", [256]); ass = P.tile("bss", [4])
    blocks = [(0, 256, [0, 1])] + [(TC + qb * 512, 512, list(range(NT))) for qb in range(TL // 512)]
    NSB = 4
    pT = pT + [P.tile("pT3", [512], BF16)]
    sbanks = [K.ps[2], K.ps[3], K.ps[4], K.ps[5]]
    tp = K.ps[6]
    accs = [[K.ps[0], K.ps[1]], [K.ps[7], K.ps[1]]]
    items = []
    for (q0, nq, kts) in blocks:
        for h in range(4):
            for j in range(2):
                for ki, kt in enumerate(kts):
                    items.append((q0, nq, kts, h, j, ki, kt))
    LA = 3

    def emit_score(n):
        (q0, nq, kts, h, j, ki, kt) = items[n]
        sp = sbanks[n % NSB]
        MM(P, sp, sp[:, 0:nq], kT[32 * j:32 * j + 32, h, kt * 128:(kt + 1) * 128], qT[32 * j:32 * j + 32, h, q0:q0 + nq], [kT, qT])

    def emit_rest(n):
        (q0, nq, kts, h, j, ki, kt) = items[n]
        nqt = nq // 128
        sp = sbanks[n % NSB]
        pt_ = pT[n % NSB]
        acc = [K.ps[0], K.ps[1]]
        A_(P, lambda e, sp=sp, pt_=pt_, nq=nq: e.activation(pt_[:, 0:nq], sp[:, 0:nq], AF.Exp, scale=scale), [sp], [pt_])
        MM(P, acc[j], acc[j][0:65, 0:nq], vaug[:, kt, h, :], pt_[:, 0:nq], [vaug, pt_], start=(ki == 0), stop=(ki == len(kts) - 1))
        if ki != len(kts) - 1:
            return
        V(P, lambda e, j=j, nq=nq, a=acc[j]: e.tensor_copy(osb[j][:, 0:nq], a[0:65, 0:nq]), [acc[j]], [osb[j]])
        if j != 1:
            return
        for qt in range(nqt):
            for jj in range(2):
                TR(P, tp, tp[:, jj * 128:jj * 128 + 65], osb[jj][:, qt * 128:(qt + 1) * 128], K.ident[0:65, 0:65], [osb[jj], K.ident])
            V(P, lambda e: e.reciprocal(rz[:].rearrange("p (a b) -> p a b", a=2), tp[:, 0:256].rearrange("p (a b) -> p a b", a=2)[:, :, 64:65]), [tp], [rz])
            V(P, lambda e: e.tensor_tensor(rz[:, 1:2], rz[:, 1:2], lamv[:, 2:3], ALU.mult), [rz, lamv], [rz])
            V(P, lambda e, qt=qt, h=h: e.tensor_scalar(obuf[:, qt, h * 64:(h + 1) * 64], tp[:, 0:64], rz[:, 0:1], 1.0, ALU.mult, ALU.mult), [tp, rz], [obuf])
            V(P, lambda e: e.tensor_scalar(o1t[:], tp[:, 128:192], rz[:, 1:2], 1.0, ALU.mult, ALU.mult), [tp, rz], [o1t])
            G_(P, lambda e, qt=qt, h=h: e.tensor_tensor(obuf[:, qt, h * 64:(h + 1) * 64], obuf[:, qt, h * 64:(h + 1) * 64], o1t[:], ALU.add), [obuf, o1t], [obuf])
        if h != 3:
            return
        for qt in range(nqt):
            ov = T(obuf.name, obuf[:, qt, :])
            head_rms_gate(K, ov, None, subw, asq, ass, 4, 64, 1.0 - lam_init)
            r0 = q0 + qt * 128
            P.dma("sync", K.O[r0:r0 + 128, 768:1024], obuf[:, qt, :], reads=[obuf])

    for idx in range(len(items) + LA):
        if idx < len(items):
            emit_score(idx)
        if idx >= LA:
            emit_rest(idx - LA)
    P.reset(m0)


EXTRA_D = {"attn": phase_attn}


HQW = 1536


def hg_consts(c):
    j = np.arange(128)[:, None]
    i = np.arange(128)[None, :]
    same = (j // 64) == (i // 64)
    jl = j % 64
    c["mrel_f"] = (same * ((j <= i).astype(np.float32) - (jl <= 31).astype(np.float32))).astype(np.float32)
    c["mrel_b"] = (same * ((j >= i).astype(np.float32) - (jl >= 32).astype(np.float32))).astype(np.float32)


def phase_hg_pre(K, l):
    P = K.P
    m0 = P.mark()
    raw = P.tile("lbraw", [4, 256])
    P.dma("sync", raw[:], K.hg_lb_raw.rearrange("a b -> (a b)").partition_broadcast(128), writes=[raw])
    lb = P.tile("lb", [256]); oml = P.tile("oml", [256]); den = P.tile("lbden", [256])
    A_(P, lambda e: e.activation(raw[:], raw[:], AF.Exp), [raw], [raw])
    V(P, lambda e: e.tensor_tensor(den[:], raw[:, 0, :], raw[:, 1, :], ALU.add), [raw], [den])
    V(P, lambda e: e.tensor_tensor(den[:], den[:], raw[:, 2, :], ALU.add), [raw, den], [den])
    V(P, lambda e: e.tensor_tensor(den[:], den[:], raw[:, 3, :], ALU.add), [raw, den], [den])
    V(P, lambda e: e.reciprocal(den[:], den[:]), [den], [den])
    V(P, lambda e: e.memset(lb[:], 0.0), [], [lb])
    for ll in range(1, l + 1):
        V(P, lambda e, ll=ll: e.tensor_tensor(lb[:], lb[:], raw[:, ll, :], ALU.add), [lb, raw], [lb])
    V(P, lambda e: e.tensor_tensor(lb[:], lb[:], den[:], ALU.mult), [lb, den], [lb])
    V(P, lambda e: e.tensor_scalar(oml[:], lb[:], -1.0, 1.0, ALU.mult, ALU.add), [lb], [oml])
    xin = [P.tile("hgx%d" % b, [1024]) for b in range(2)]
    ho = [P.tile("hgo%d" % b, [HQW]) for b in range(2)]
    sg = P.tile("hgs", [512]); tt = P.tile("hgt", [512])

    def loads(i):
        b = i % 2
        P.dma("sync", xin[b][:], K.Pj[i * 128:(i + 1) * 128, C_HG:C_HG + 1024], writes=[xin[b]])
    loads(0)
    for i in range(NT):
        b = i % 2
        if i + 1 < NT:
            loads(i + 1)
        x = xin[b]; o = ho[b]
        A_(P, lambda e, x=x, o=o: e.activation(o[:, 0:256], x[:, 0:256], AF.Silu), [x], [o])
        G_(P, lambda e, x=x, o=o: e.tensor_copy(o[:, 256:512], x[:, 256:512]), [x], [o])
        A_(P, lambda e, x=x: e.activation(sg[:], x[:, 512:1024], AF.Sigmoid), [x], [sg])
        s3 = sg[:].rearrange("p (d c) -> p d c", d=2)
        t3 = tt[:].rearrange("p (d c) -> p d c", d=2)
        V(P, lambda e, s3=s3, t3=t3: e.tensor_tensor(t3, s3, oml[:].unsqueeze(1).broadcast_to([128, 2, 256]), ALU.mult), [sg, oml], [tt])
        o4 = o[:, 512:1536].rearrange("p (d k c) -> p d k c", d=2, k=2)
        G_(P, lambda e, s3=s3, t3=t3: e.tensor_tensor(s3, t3, lb[:].unsqueeze(1).broadcast_to([128, 2, 256]), ALU.add), [tt, lb], [sg])
        A_(P, lambda e, o4=o4, s3=s3, o=o: e.activation(o4[:, :, 0, :], s3, AF.Ln), [sg], [o])
        V(P, lambda e, o4=o4, t3=t3, o=o: e.scalar_tensor_tensor(o4[:, :, 1, :], t3, -1.0, oml[:].unsqueeze(1).broadcast_to([128, 2, 256]), ALU.mult, ALU.add), [tt, oml], [o])
        P.dma("sync", K.HQ[i * 128:(i + 1) * 128, :], o[:], reads=[o])
    P.reset(m0)


def hg_scan_gen(K, l, d, nps):
    P = K.P
    tri = P.tile("htri", [128]); tail = P.tile("htail", [128]); mrel = P.tile("hmrel", [128])
    trim = P.tile("htrim", [128], U8)
    P.dma("sync", tri[:], K.cin["tri_f" if d == 0 else "tri_b"], writes=[tri])
    P.dma("sync", tail[:], K.cin["tri_bs" if d == 0 else "tri_fs"], writes=[tail])
    P.dma("sync", mrel[:], K.cin["mrel_f" if d == 0 else "mrel_b"], writes=[mrel])
    V(P, lambda e: e.tensor_copy(trim[:], tri[:]), [tri], [trim])
    zeros = P.tile("hzeros", [128])
    V(P, lambda e: e.memset(zeros[:], 0.0), [], [zeros])
    ident, ones = K.ident, K.ones
    S = [P.tile("hS%d" % p, [64]) for p in range(2)]
    for p in range(2):
        V(P, lambda e, p=p: e.memset(S[p][:], 0.0), [], [S[p]])
    qv = [P.tile("hqv%d" % b, [2, 2, 64]) for b in range(2)]
    fk = [P.tile("hfk%d" % b, [2, 2, 64]) for b in range(2)]
    ot = [P.tile("hot%d" % b, [2, 64]) for b in range(2)]
    ex = [P.tile("hex%d" % k, [128]) for k in range(4)]
    qe = P.tile("hqe", [2, 64]); ke = P.tile("hke", [2, 64])

    def pt(name, free):
        return [P.tile("%s%d" % (name, p), free) for p in range(2)]
    keT = pt("hkeT", [128]); qeT = pt("hqeT", [128]); aT = pt("haT", [128]); qgb = pt("hqgb", [128]); kendb = pt("hkendb", [128])
    lfb = pt("hlfb", [128]); qgT = pt("hqgT", [128]); ege = pt("hege", [1])
    for p in range(2):
        for t in (qgb[p], kendb[p], lfb[p]):
            G_(P, lambda e, t=t: e.memset(t[:], 0.0), [], [t])
    ctx_ch = list(range(0, TC // 64))
    lat_ch = list(range(TC // 64, TT // 64))
    order = ctx_ch + lat_ch if d == 0 else ctx_ch[::-1] + lat_ch[::-1]
    c0 = 512 + d * 512

    def loads(ci):
        c = order[ci]
        b = ci % 2
        r0 = c * 64
        for hh in range(2):
            src = K.HQ[r0:r0 + 64, 0:512].rearrange("r (t pr hh e) -> r t pr hh e", t=2, pr=2, hh=2)[:, :, :, hh, :]
            P.dma("sync", qv[b][hh * 64:(hh + 1) * 64, :, :, :], src, writes=[qv[b]])
            src = K.HQ[r0:r0 + 64, c0:c0 + 512].rearrange("r (t pr hh e) -> r t pr hh e", t=2, pr=2, hh=2)[:, :, :, hh, :]
            P.dma("sync", fk[b][hh * 64:(hh + 1) * 64, :, :, :], src, writes=[fk[b]])
    loads(0)
    for ci in range(len(order)):
        c = order[ci]
        b = ci % 2
        if ci + 1 < len(order):
            loads(ci + 1)
        Q = qv[b]; F = fk[b]
        lf3 = F[:, 0, :, :].rearrange("p a b -> p (a b)")
        mats = (mrel, None, tri, tail)
        pss = []
        for k, M in enumerate(mats):
            if M is None:
                pss.append(None)
                continue
            ps = nps()
            MM(P, ps, ps[:, 0:128], M[:], lf3, [M, F])
            pss.append(ps)
        yield
        A_(P, lambda e, ps=pss[0]: e.activation(ex[0][:], ps[:, 0:128], AF.Exp), [pss[0]], [ex[0]])
        A_(P, lambda e, ps=pss[0]: e.activation(ex[1][:], ps[:, 0:128], AF.Exp, scale=-1.0), [pss[0]], [ex[1]])
        A_(P, lambda e, ps=pss[2]: e.activation(ex[2][:], ps[:, 0:128], AF.Exp), [pss[2]], [ex[2]])
        A_(P, lambda e, ps=pss[3]: e.activation(ex[3][:], ps[:, 0:128], AF.Exp), [pss[3]], [ex[3]])
        yield
        V(P, lambda e, Q=Q: e.tensor_tensor(qe[:], Q[:, 0, :, :], ex[0][:].rearrange("p (a b) -> p a b", a=2), ALU.mult), [Q, ex[0]], [qe])
        G_(P, lambda e, F=F: e.tensor_tensor(ke[:], F[:, 1, :, :], ex[1][:].rearrange("p (a b) -> p a b", a=2), ALU.mult), [F, ex[1]], [ke])
        yield

        def pair_gen(p, Q=Q, F=F, b=b):
            vv = Q[:, 1, p, :]
            for hh in range(2):
                r = slice(hh * 64, hh * 64 + 64)
                V(P, lambda e, p=p, r=r, Q=Q: e.tensor_tensor(qgb[p][r, r], Q[r, 0, p, :], ex[2][r, p * 64:(p + 1) * 64], ALU.mult), [Q, ex[2]], [qgb[p]])
                G_(P, lambda e, p=p, r=r, F=F: e.tensor_tensor(kendb[p][r, r], F[r, 1, p, :], ex[3][r, p * 64:(p + 1) * 64], ALU.mult), [F, ex[3]], [kendb[p]])
                G_(P, lambda e, p=p, r=r, F=F: e.tensor_copy(lfb[p][r, r], F[r, 0, p, :]), [F], [lfb[p]])
            yield
            ps = nps(); ps2 = nps()
            TR(P, ps, ps[0:64, 0:128], ke[:, p, :], ident[:], [ke, ident])
            TR(P, ps2, ps2[0:64, 0:128], qe[:, p, :], ident[:], [qe, ident])
            yield
            A_(P, lambda e, p=p, ps=ps: e.copy(keT[p][0:64, :], ps[0:64, 0:128]), [ps], [keT[p]])
            V(P, lambda e, p=p, ps=ps2: e.tensor_copy(qeT[p][0:64, :], ps[0:64, 0:128]), [ps2], [qeT[p]])
            yield
            ps = nps(); ps2 = nps()
            MM(P, ps, ps[:, 0:128], keT[p][0:64, :], qeT[p][0:64, :], [keT[p], qeT[p]])
            TR(P, ps2, ps2[:, 0:128], qgb[p][:], ident[:], [qgb[p], ident])
            MM(P, ps2, ps2[:, 128:129], lfb[p][:], ones[:, 0:1], [lfb[p], ones])
            yield
            V(P, lambda e, p=p, ps=ps: e.select(aT[p][:], trim[:], ps[:, 0:128], zeros[:]), [ps, trim, zeros], [aT[p]])
            A_(P, lambda e, p=p, ps=ps2: e.copy(qgT[p][:], ps[:, 0:128]), [ps2], [qgT[p]])
            A_(P, lambda e, p=p, ps=ps2: e.activation(ege[p][:], ps[:, 128:129], AF.Exp), [ps2], [ege[p]])
            yield
            ps = nps(); ps2 = nps()
            MM(P, ps, ps[:, 0:64], qgT[p][:], S[p][:], [qgT[p], S[p]], start=True, stop=False)
            MM(P, ps, ps[:, 0:64], aT[p][:], vv, [aT[p], Q], start=False, stop=True)
            MM(P, ps2, ps2[:, 0:64], kendb[p][:], vv, [kendb[p], Q])
            yield
            A_(P, lambda e, p=p, ps=ps, b=b: e.copy(ot[b][:, p, :], ps[:, 0:64]), [ps], [ot[b]])
            V(P, lambda e, p=p, ps=ps2: e.scalar_tensor_tensor(S[p][:], S[p][:], ege[p][:, 0:1], ps[:, 0:64], ALU.mult, ALU.add), [S[p], ege[p], ps2], [S[p]])
        pg = [pair_gen(0), pair_gen(1)]
        while pg:
            for gg_ in pg[:]:
                try:
                    next(gg_)
                except StopIteration:
                    pg.remove(gg_)
            yield
        r0 = c * 64
        for hh in range(2):
            dst = K.OG[d][r0:r0 + 64, :].rearrange("r (pr hh e) -> r pr hh e", pr=2, hh=2)[:, :, hh, :]
            P.dma("sync", dst, ot[b][hh * 64:(hh + 1) * 64, :, :], reads=[ot[b]])


def phase_hg_scan_both(K, l):
    P = K.P
    m0 = P.mark()
    psn = [0]

    def nps():
        psn[0] = (psn[0] + 1) % 8
        return K.ps[psn[0]]
    lockstep([hg_scan_gen(K, l, 0, nps), hg_scan_gen(K, l, 1, nps)])
    P.reset(m0)


def phase_dir_fin(K, l, nw_ap, gate_c0, out_c0):
    P = K.P
    m0 = P.mark()
    nw = P.tile("fnw", [64])
    P.dma("sync", nw[:], nw_ap.partition_broadcast(128), writes=[nw])
    o0 = [P.tile("fo0_%d" % b, [256]) for b in range(2)]
    o1 = [P.tile("fo1_%d" % b, [256]) for b in range(2)]
    zt = [P.tile("fzt%d" % b, [256]) for b in range(2)]
    sq = P.tile("fsq", [256]); ss = P.tile("fss", [4])

    def loads(i):
        b = i % 2
        rs = slice(i * 128, (i + 1) * 128)
        P.dma("sync", o0[b][:], K.OG[0][rs, :], writes=[o0[b]])
        P.dma("sync", o1[b][:], K.OG[1][rs, :], writes=[o1[b]])
        P.dma("sync", zt[b][:], K.Pj[rs, gate_c0:gate_c0 + 256], writes=[zt[b]])
    loads(0)
    for i in range(NT):
        b = i % 2
        if i + 1 < NT:
            loads(i + 1)
        o = o0[b]
        V(P, lambda e, o=o, b=b: e.tensor_tensor(o[:], o[:], o1[b][:], ALU.add), [o, o1[b]], [o])
        head_rms_gate(K, o, zt[b], nw, sq, ss, 4, 64, 1.0)
        P.dma("sync", K.O[i * 128:(i + 1) * 128, out_c0:out_c0 + 256], o[:], reads=[o])
    P.reset(m0)


def phase_hg(K, l):
    phase_hg_pre(K, l)
    phase_hg_scan_both(K, l)
    phase_dir_fin(K, l, K.hg_norm_w[l], C_HG + 1024, 512)


EXTRA_E = {"hg": phase_hg, "hgpre": phase_hg_pre}


HY_STREAMS = ((TC, 0), (TL, TC))


def hy_consts(c):
    for (n, _) in HY_STREAMS:
        N = 2 * n
        N1 = N // 128
        rows = np.arange(N)
        lag = np.where(rows < n, rows, N - rows).astype(np.float64)
        valid = (rows != n).astype(np.float64)
        t01 = lag / max(n - 1, 1)
        bands = np.linspace(1e-4, 15, 16)
        ang = (2.0 * np.pi / n) * lag[:, None] * bands
        z = np.concatenate([t01[:, None], np.cos(ang), -np.sin(ang)], axis=-1)
        c["hy_zT%d" % n] = np.ascontiguousarray(z.T).astype(np.float32)
        c["hy_t01_%d" % n] = np.ascontiguousarray((-t01).reshape(N1, 128).T).astype(np.float32)
        c["hy_val%d" % n] = np.ascontiguousarray(valid.reshape(N1, 128).T).astype(np.float32)
        n1 = np.arange(N1)[:, None]; k1 = np.arange(N1)[None, :]
        th = 2 * np.pi * n1 * k1 / N1
        c["hy_wf1_%d" % n] = np.concatenate([np.cos(th), -np.sin(th)], axis=1).astype(np.float32)
        c["hy_cf%d" % n] = (np.cos(th).T[:, :N1 // 2] / N).astype(np.float32)
        c["hy_nsf%d" % n] = (-np.sin(th).T[:, :N1 // 2] / N).astype(np.float32)
        n2 = np.arange(128)[None, :, None]; k2 = np.arange(128)[None, None, :]; kk1 = np.arange(N1)[:, None, None]
        th2 = 2 * np.pi * n2 * (kk1 + N1 * k2) / N
        c["hy_c2_%d" % n] = np.cos(th2).astype(np.float32)
        c["hy_s2_%d" % n] = np.sin(th2).astype(np.float32)
        c["hy_c2t_%d" % n] = np.ascontiguousarray(np.cos(th2).transpose(0, 2, 1)).astype(np.float32)
        c["hy_s2t_%d" % n] = np.ascontiguousarray(np.sin(th2).transpose(0, 2, 1)).astype(np.float32)


def phase_hy_pre(K, l):
    P = K.P
    m0 = P.mark()
    cw = P.tile("hcw", [3, 768])
    for k in range(3):
        P.dma("sync", cw[:, k, :], K.hy_conv_w[l][k].partition_broadcast(128), writes=[cw])
    x3 = [[P.tile("hx3_%d_%d" % (b, k), [768]) for k in range(3)] for b in range(2)]
    acc = [P.tile("hacc%d" % b, [768]) for b in range(2)]
    t0 = P.tile("ht0", [768]); t2 = P.tile("ht2", [768])
    load_shift3(K, "sync", x3[0], K.Pj, 0, C_HY, C_HY + 768)
    for i in range(NT):
        b = i % 2
        if i + 1 < NT:
            load_shift3(K, "sync", x3[1 - b], K.Pj, i + 1, C_HY, C_HY + 768)
        conv3(K, x3[b], cw, acc[b], t0, t2, 768)
        P.dma("sync", K.HC[i * 128:(i + 1) * 128, :], acc[b][:], reads=[acc[b]])
    P.reset(m0)


def hy_filters(K, l, n):
    P = K.P
    N = 2 * n
    N1 = N // 128
    m0 = P.mark()
    w1 = P.tile("hw1", [64], parts=33); w2 = P.tile("hw2", [64], parts=64); w3 = P.tile("hw3", [1024], parts=64)
    P.dma("sync", w1[:], K.hy_w1[l], writes=[w1]); P.dma("sync", w2[:], K.hy_w2[l], writes=[w2]); P.dma("sync", w3[:], K.hy_w3[l], writes=[w3])
    pv = P.tile("hpv", [4], parts=64)
    for k, src in enumerate((K.hy_f1, K.hy_b1, K.hy_f2, K.hy_b2)):
        P.dma("sync", pv[:, k:k + 1], src[l].rearrange("(a b) -> a b", b=1), writes=[pv])
    sc = P.tile("hsc", [8], parts=64)
    for (o, fi, bi) in ((0, 0, 1), (3, 2, 3)):
        V(P, lambda e, o=o, fi=fi: e.tensor_scalar(sc[:, o:o + 1], pv[:, fi:fi + 1], 0.25, 1.0, ALU.mult, ALU.mult), [pv], [sc])
        V(P, lambda e, o=o, fi=fi, bi=bi: e.tensor_tensor(sc[:, o + 1:o + 2], sc[:, o:o + 1], pv[:, bi:bi + 1], ALU.mult), [pv, sc], [sc])
        V(P, lambda e, o=o: e.tensor_scalar(sc[:, o + 2:o + 3], sc[:, o + 1:o + 2], math.pi / 2, 1.0, ALU.add, ALU.mult), [sc], [sc])
    nad = P.tile("hnad", [512])
    P.dma("sync", nad[:], K.hy_decay[l].rearrange("a b -> (a b)").partition_broadcast(128), writes=[nad])
    tneg = P.tile("htneg", [512])
    V(P, lambda e: e.tensor_scalar(tneg[:], nad[:], -1.0, 1.0, ALU.mult, ALU.mult), [nad], [tneg])
    V(P, lambda e: e.tensor_tensor(nad[:], nad[:], tneg[:], ALU.max), [nad, tneg], [nad])
    t01 = P.tile("ht01", [N1]); val = P.tile("hval", [N1])
    P.dma("sync", t01[:], K.cin["hy_t01_%d" % n], writes=[t01]); P.dma("sync", val[:], K.cin["hy_val%d" % n], writes=[val])
    zT = [P.tile("hzT%d" % b, [128], parts=33) for b in range(2)]
    sT = P.tile("hsT", [128], parts=64); cT = P.tile("hcT", [128], parts=64); hh = P.tile("hhh", [128], parts=64); h2 = P.tile("hh2", [128], parts=64)
    win = P.tile("hwin", [512]); ko = [P.tile("hko%d" % b, [512]) for b in range(2)]

    def sin_layer(ps, o, out):
        A_(P, lambda e: e.activation(sT[:], ps[0:64, 0:128], AF.Sin, bias=sc[:, o + 1:o + 2], scale=sc[:, o:o + 1]), [ps, sc], [sT])
        A_(P, lambda e: e.activation(cT[:], ps[0:64, 0:128], AF.Sin, bias=sc[:, o + 2:o + 3], scale=sc[:, o:o + 1]), [ps, sc], [cT])
        V(P, lambda e: e.tensor_tensor(cT[:], cT[:], sT[:], ALU.mult), [cT, sT], [cT])
        V(P, lambda e: e.tensor_tensor(sT[:], sT[:], sT[:], ALU.mult), [sT], [sT])
        V(P, lambda e: e.tensor_scalar(sT[:], sT[:], -2.0, 1.0, ALU.mult, ALU.add), [sT], [sT])
        V(P, lambda e: e.scalar_tensor_tensor(out[:], cT[:], 4.0, sT[:], ALU.mult, ALU.mult), [cT, sT], [out])

    P.dma("sync", zT[0][:], K.cin["hy_zT%d" % n][:, 0:128], writes=[zT[0]])
    for rt in range(N1):
        b = rt % 2
        if rt + 1 < N1:
            P.dma("sync", zT[1 - b][:], K.cin["hy_zT%d" % n][:, (rt + 1) * 128:(rt + 2) * 128], writes=[zT[1 - b]])
        side = 0 if rt < N1 // 2 else 1
        ps = K.ps[rt % 2]
        MM(P, ps, ps[0:64, 0:128], w1[:], zT[b][:], [w1, zT[b]])
        sin_layer(ps, 0, hh)
        ps = K.ps[2 + rt % 2]
        MM(P, ps, ps[0:64, 0:128], w2[:], hh[:], [w2, hh])
        sin_layer(ps, 3, h2)
        ps = K.ps[4 + rt % 2]
        MM(P, ps, ps[:, 0:512], h2[:], w3[:, side * 512:(side + 1) * 512], [h2, w3])
        A_(P, lambda e, rt=rt: e.activation(win[:], nad[:], AF.Exp, scale=t01[:, rt:rt + 1]), [nad, t01], [win])
        V(P, lambda e, ps=ps, rt=rt, b=b: e.scalar_tensor_tensor(ko[b][:], ps[:, 0:512], val[:, rt:rt + 1], win[:], ALU.mult, ALU.mult), [ps, val, win], [ko[b]])
        P.dma("sync", K.KERN[rt * 128:(rt + 1) * 128, :], ko[b][:], reads=[ko[b]])
    P.reset(m0)
    hy_dft_fwd(K, n, K.KERN, 512, N1, kernel=True)


def hy_views(K, n, ncol):
    N1 = 2 * n // 128
    A = K.Aflat[0:2 * N1 * 128 * ncol].rearrange("(a b c) -> a b c", a=2 * N1, b=128)
    return A


def hy_dft_fwd(K, n, src, ncol, nrows1, kernel=False, filt=0, dst_rows=None, skip_ap=None, gate_c0=None, u_c0=None, out_ap=None, out_c0=0, row0=0):
    P = K.P
    N = 2 * n
    N1 = N // 128
    A = hy_views(K, n, ncol)
    m0 = P.mark()
    wf1 = P.tile("hwf1", [2 * N1], parts=N1)
    P.dma("sync", wf1[:], K.cin["hy_wf1_%d" % n], writes=[wf1])
    CH = 2048 // ncol * 1
    CH = max(1, 2048 // ncol)
    xin = [P.tile("hxin%d" % b, [CH * ncol], parts=nrows1) for b in range(2)]
    ao = [P.tile("hao%d" % b, [CH * ncol], parts=2 * N1) for b in range(2)]
    srcv = src[0:nrows1 * 128, :].rearrange("(a b) c -> a b c", b=128)
    nch = 128 // CH
    P.dma("sync", xin[0][:].rearrange("p (b c) -> p b c", b=CH), srcv[:, 0:CH, :], writes=[xin[0]])
    for ch in range(nch):
        b = ch % 2
        if ch + 1 < nch:
            P.dma("sync", xin[1 - b][:].rearrange("p (b c) -> p b c", b=CH), srcv[:, (ch + 1) * CH:(ch + 2) * CH, :], writes=[xin[1 - b]])
        for q in range(CH * ncol // 512):
            ps = K.ps[q % 4]
            MM(P, ps, ps[0:2 * N1, 0:512], wf1[0:nrows1, :], xin[b][:, q * 512:(q + 1) * 512], [wf1, xin[b]])
            if q % 2 == 0:
                A_(P, lambda e, ps=ps, q=q, b=b: e.copy(ao[b][:, q * 512:(q + 1) * 512], ps[0:2 * N1, 0:512]), [ps], [ao[b]])
            else:
                V(P, lambda e, ps=ps, q=q, b=b: e.tensor_copy(ao[b][:, q * 512:(q + 1) * 512], ps[0:2 * N1, 0:512]), [ps], [ao[b]])
        P.dma("sync", A[:, ch * CH:(ch + 1) * CH, :], ao[b][:].rearrange("p (b c) -> p b c", b=CH), reads=[ao[b]])
    P.reset(m0)
    m0 = P.mark()
    NC2 = 2 * ncol
    Av = A.rearrange("(ri k1) n2 c -> k1 n2 ri c", ri=2)
    r1 = [P.tile("hr1_%d" % b, [2, ncol]) for b in range(2)]
    r2 = [P.tile("hr2_%d" % b, [2, ncol]) for b in range(2)]
    tb = [[P.tile("htb%d_%d" % (b, k), [128]) for k in range(4 if not kernel else 2)] for b in range(2)]
    xo = [P.tile("hxo%d" % b, [NC2]) for b in range(2)]
    if not kernel:
        kf = [P.tile("hkf%d" % b, [2, ncol]) for b in range(2)]
        ta = P.tile("hta", [2, ncol]); tbb = P.tile("htbb", [2, ncol])
        y1 = P.tile("hy1", [2, ncol]); y2 = P.tile("hy2", [2, ncol])
        Bv = K.Bflat[0:N1 * 128 * 2 * ncol].rearrange("(k1 n2 ri c) -> k1 n2 ri c", k1=N1, n2=128, ri=2)
    tabs = ("hy_c2_%d" % n, "hy_s2_%d" % n, "hy_c2t_%d" % n, "hy_s2t_%d" % n)

    def loads(k1):
        b = k1 % 2
        P.dma("sync", r1[b][:], Av[k1], writes=[r1[b]])
        for k in range(len(tb[b])):
            P.dma("sync", tb[b][k][:], K.cin[tabs[k]][k1], writes=[tb[b][k]])
        if not kernel:
            P.dma("sync", kf[b][:], K.Kf[k1, :, :, filt * 256:(filt + 1) * 256], writes=[kf[b]])
    loads(0)
    for k1 in range(N1):
        b = k1 % 2
        if k1 + 1 < N1:
            loads(k1 + 1)
        G_(P, lambda e, b=b: e.tensor_copy(r2[b][:, 0, :], r1[b][:, 1, :]), [r1[b]], [r2[b]])
        A_(P, lambda e, b=b: e.mul(r2[b][:, 1, :], r1[b][:, 0, :], -1.0), [r1[b]], [r2[b]])
        f1 = r1[b][:].rearrange("p a c -> p (a c)")
        f2 = r2[b][:].rearrange("p a c -> p (a c)")
        nh = NC2 // 512
        pss = []
        for q in range(nh):
            ps = K.ps[q % 2] if kernel else K.ps[0]
            if kernel:
                ps = K.ps[(k1 * nh + q) % 4]
            MM(P, ps, ps[:, 0:512], tb[b][0][:], f1[:, q * 512:(q + 1) * 512], [tb[b][0], r1[b]], start=True, stop=False)
            MM(P, ps, ps[:, 0:512], tb[b][1][:], f2[:, q * 512:(q + 1) * 512], [tb[b][1], r2[b]], start=False, stop=True)
            pss.append(ps)
            if kernel:
                if q % 2 == 0:
                    A_(P, lambda e, ps=ps, q=q, b=b: e.copy(xo[b][:, q * 512:(q + 1) * 512], ps[:, 0:512]), [ps], [xo[b]])
                else:
                    V(P, lambda e, ps=ps, q=q, b=b: e.tensor_copy(xo[b][:, q * 512:(q + 1) * 512], ps[:, 0:512]), [ps], [xo[b]])
        if kernel:
            P.dma("sync", K.Kf[k1], xo[b][:].rearrange("p (a c) -> p a c", a=2), reads=[xo[b]])
            continue
        ps = pss[0]
        X3 = ps[:, 0:512].rearrange("p (a c) -> p a c", a=2)
        V(P, lambda e, X3=X3, b=b: e.tensor_tensor(ta[:], X3, kf[b][:, 0:1, :].broadcast_to([128, 2, ncol]), ALU.mult), [ps, kf[b]], [ta])
        V(P, lambda e, X3=X3, b=b: e.tensor_tensor(tbb[:], X3, kf[b][:, 1:2, :].broadcast_to([128, 2, ncol]), ALU.mult), [ps, kf[b]], [tbb])
        G_(P, lambda e: e.tensor_tensor(y1[:, 0, :], ta[:, 0, :], tbb[:, 1, :], ALU.subtract), [ta, tbb], [y1])
        G_(P, lambda e: e.tensor_tensor(y1[:, 1, :], tbb[:, 0, :], ta[:, 1, :], ALU.add), [ta, tbb], [y1])
        A_(P, lambda e: e.mul(y2[:, 0, :], y1[:, 1, :], -1.0), [y1], [y2])
        G_(P, lambda e: e.tensor_copy(y2[:, 1, :], y1[:, 0, :]), [y1], [y2])
        ps2 = K.ps[1 + k1 % 2]
        MM(P, ps2, ps2[:, 0:512], tb[b][2][:], y1[:].rearrange("p a c -> p (a c)"), [tb[b][2], y1], start=True, stop=False)
        MM(P, ps2, ps2[:, 0:512], tb[b][3][:], y2[:].rearrange("p a c -> p (a c)"), [tb[b][3], y2], start=False, stop=True)
        A_(P, lambda e, ps2=ps2, b=b: e.copy(xo[b][:], ps2[:, 0:512]), [ps2], [xo[b]])
        P.dma("sync", Bv[k1], xo[b][:].rearrange("p (a c) -> p a c", a=2), reads=[xo[b]])
    P.reset(m0)
    if kernel:
        return
    m0 = P.mark()
    H = N1 // 2
    cf = P.tile("hcf", [H], parts=N1); nsf = P.tile("hnsf", [H], parts=N1)
    P.dma("sync", cf[:], K.cin["hy_cf%d" % n], writes=[cf]); P.dma("sync", nsf[:], K.cin["hy_nsf%d" % n], writes=[nsf])
    skp = P.tile("hskp", [256])
    P.dma("sync", skp[:], skip_ap.partition_broadcast(128), writes=[skp])
    CH2 = 4
    br = [P.tile("hbr%d" % b, [CH2, 2, 256], parts=N1) for b in range(2)]
    uu = [P.tile("huu%d" % b, [CH2, 256], parts=H) for b in range(2)]
    gg = [P.tile("hgg%d" % b, [CH2, 256], parts=H) for b in range(2)]
    yo = [P.tile("hyo%d" % b, [CH2, 256], parts=H) for b in range(2)]
    usrc = u_c0[0][row0 if u_c0[2] else 0:(row0 if u_c0[2] else 0) + n, u_c0[1]:u_c0[1] + 256].rearrange("(a b) c -> a b c", b=128)
    gsrc = K.HC[row0:row0 + n, gate_c0:gate_c0 + 256].rearrange("(a b) c -> a b c", b=128)
    dsrc = out_ap[(row0 if out_ap is K.O else 0):(row0 if out_ap is K.O else 0) + n, out_c0:out_c0 + 256].rearrange("(a b) c -> a b c", b=128)
    Bk = Bv.rearrange("k1 n2 ri c -> k1 n2 ri c")

    def loads3(ch):
        b = ch % 2
        P.dma("sync", br[b][:], Bk[:, ch * CH2:(ch + 1) * CH2, :, :], writes=[br[b]])
        P.dma("sync", uu[b][:], usrc[:, ch * CH2:(ch + 1) * CH2, :], writes=[uu[b]])
        P.dma("sync", gg[b][:], gsrc[:, ch * CH2:(ch + 1) * CH2, :], writes=[gg[b]])
    loads3(0)
    nch = 128 // CH2
    for ch in range(nch):
        b = ch % 2
        if ch + 1 < nch:
            loads3(ch + 1)
        for q in range(CH2 // 2):
            ps = K.ps[4 + q % 2]
            o3 = ps[0:H, 0:512].rearrange("p (a c) -> p a c", a=2)
            MM(P, ps, o3, cf[:], br[b][:, 2 * q:2 * q + 2, 0, :], [cf, br[b]], start=True, stop=False)
            MM(P, ps, o3, nsf[:], br[b][:, 2 * q:2 * q + 2, 1, :], [nsf, br[b]], start=False, stop=True)
            us = uu[b][:, 2 * q:2 * q + 2, :]
            V(P, lambda e, us=us: e.tensor_tensor(us, us, skp[0:H, :].unsqueeze(1).broadcast_to([H, 2, 256]), ALU.mult), [uu[b], skp], [uu[b]])
            V(P, lambda e, us=us, o3=o3: e.tensor_tensor(us, us, o3, ALU.add), [uu[b], ps], [uu[b]])
            G_(P, lambda e, us=us, b=b, q=q: e.tensor_tensor(yo[b][:, 2 * q:2 * q + 2, :], us, gg[b][:, 2 * q:2 * q + 2, :], ALU.mult), [uu[b], gg[b]], [yo[b]])
        P.dma("sync", dsrc[:, ch * CH2:(ch + 1) * CH2, :], yo[b][:], reads=[yo[b]])
    P.reset(m0)


def phase_hy(K, l):
    phase_hy_pre(K, l)
    for (n, row0) in HY_STREAMS:
        hy_filters(K, l, n)
        N1 = 2 * n // 128
        hy_dft_fwd(K, n, K.HC[row0:row0 + n, 0:256], 256, N1 // 2, filt=0, skip_ap=K.hy_bias[l][0], gate_c0=256,
                   u_c0=(K.HC, 0, True), out_ap=K.Zd, out_c0=0, row0=row0)
        hy_dft_fwd(K, n, K.Zd[0:n, :], 256, N1 // 2, filt=1, skip_ap=K.hy_bias[l][1], gate_c0=512,
                   u_c0=(K.Zd, 0, False), out_ap=K.O, out_c0=256, row0=row0)


EXTRA_F = {"hy": phase_hy, "hypre": phase_hy_pre}


def build(stop_after=None, dbg=(), pj_input=False, plan=None, ext=()):
    nc = bass.Bass("TRN2", target_bir_lowering=False)
    P = Prog(nc)
    K = Ctx(); K.P = P; K.nc = nc
    def din(name, shape, dt=F32):
        return nc.dram_tensor(name, list(shape), dt, kind="ExternalInput").ap()
    def dscr(name, shape, dt=F32):
        if name in ext:
            return din(name, shape, dt)
        kind = "ExternalOutput" if name in dbg else "Internal"
        return nc.dram_tensor(name, list(shape), dt, kind=kind).ap()
    K.xin = din("xin", [TT, D])
    K.sTin = din("sTin", [128, 8, 2])
    K.mod_w = din("mod_w", [L, D, 6 * D]); K.mod_b = din("mod_b", [L, 6 * D]); K.mod_bT = din("mod_bT", [L, 128, 48])
    K.ln1Tin = din("ln1T", [128, L, 8]); K.ln2Tin = din("ln2T", [128, L, 8])
    K.w_in = din("w_in", [L, D, PIN])
    K.gdn_conv_w = din("gdn_conv_w", [L, 3, 768]); K.gdn_a_log = din("gdn_a_log", [L, 2, 4]); K.gdn_dt_bias = din("gdn_dt_bias", [L, 2, 4])
    K.gdn_norm_w = din("gdn_norm_w", [L, 64])
    K.w_out = din("w_out", [L, D, D]); K.mlp_w1 = din("mlp_w1", [L, D, 4 * D]); K.mlp_w2 = din("mlp_w2", [L, 4 * D, D])
    K.final_ops = []
    K.da_q_norm = din("da_q_norm", [L, 32]); K.da_k_norm = din("da_k_norm", [L, 32]); K.da_lam = din("da_lam", [L, 4, 32]); K.da_subln = din("da_subln", [L, 64])
    hc = host_consts()
    hc["ropeC"], hc["ropeS"] = rope_tables()
    hg_consts(hc)
    hy_consts(hc)
    K.hy_conv_w = din("hy_conv_w", [L, 3, 768]); K.hy_w1 = din("hy_w1", [L, 33, 64]); K.hy_w2 = din("hy_w2", [L, 64, 64]); K.hy_w3 = din("hy_w3", [L, 64, 1024])
    K.hy_b1 = din("hy_b1", [L, 64]); K.hy_f1 = din("hy_f1", [L, 64]); K.hy_b2 = din("hy_b2", [L, 64]); K.hy_f2 = din("hy_f2", [L, 64])
    K.hy_decay = din("hy_decay", [L, 2, 256]); K.hy_bias = din("hy_bias", [L, 2, 256])
    K.HC = dscr("HC", [TT, 768]); K.KERN = dscr("KERN", [8192, 512]); K.Aflat = dscr("Aflat", [128 * 128 * 512]); K.Bflat = dscr("Bflat", [64 * 128 * 2 * 256])
    K.Kf = dscr("Kf", [64, 128, 2, 512]); K.Zd = dscr("Zd", [TL, 256])
    K.hg_lb_raw = din("hg_lb_raw", [L, 256]); K.hg_norm_w = din("hg_norm_w", [L, 64])
    K.HQ = dscr("HQ", [TT, HQW])
    K.cin = {k: din("c_" + k, v.shape) for k, v in hc.items()}
    K.out = nc.dram_tensor("out", [TL, D], F32, kind="ExternalOutput").ap()
    K.X = dscr("X", [TT, D]); K.Pj = din("Pj", [TT, PIN]) if pj_input else dscr("Pj", [TT, PIN])
    K.GQ = dscr("GQ", [TT, GQW]); K.OG = [dscr("OG%d" % d, [TT, 256]) for d in range(2)]; K.O = dscr("O", [TT, D])
    K.dbgfm = None
    if "dbgfm" in dbg:
        K.dbgfm = nc.dram_tensor("dbgfm", [128, 64 + 16 + 16], F32, kind="ExternalOutput").ap()
        K.dbgG = nc.dram_tensor("dbgG", [128, 4096], F32, kind="ExternalOutput").ap()
    K.ps = [P.ps("ps%d" % i, [128, 512]) for i in range(8)]
    K.ident = P.tile("ident", [128]); K.identb = P.tile("identb", [128], BF16)
    K.ones = P.tile("ones", [128])
    K.epsc = P.tile("epsc", [1]); K.onec = P.tile("onec", [1])
    P.op("vector", lambda e: e.memset(K.onec[:], 1.0), writes=[K.onec])
    K.sT = P.tile("sT", [8, 2])
    K.fm = P.tile("fm", [4, 8, 2]); K.A1 = P.tile("A1", [8, 2]); K.A2 = P.tile("A2", [8, 2])
    K.ln1T = P.tile("ln1T", [L, 8]); K.ln2T = P.tile("ln2T", [L, 8])
    K.G = [[P.tile("G%d%d" % (g, s), [D]) for s in range(2)] for g in range(2)]
    P.dma("sync", K.ident[:], K.cin["ident"], writes=[K.ident])
    P.dma("gpsimd", K.identb[:], K.cin["ident"], writes=[K.identb])
    P.dma("sync", K.ones[:], K.cin["ones"], writes=[K.ones])
    P.dma("sync", K.ln1T[:], K.ln1Tin, writes=[K.ln1T]); P.dma("sync", K.ln2T[:], K.ln2Tin, writes=[K.ln2T])
    P.op("vector", lambda e: e.memset(K.epsc[:], EPS), writes=[K.epsc])
    sraw = P.tile("sraw", [8, 2])
    P.dma("sync", sraw[:], K.sTin, writes=[sraw])
    P.op("scalar", lambda e: e.activation(K.sT[:], sraw[:], AF.Silu), reads=[sraw], writes=[K.sT])
    last = []
    for i in range(0, TT, 544):
        last.append(P.dma("sync" if (i // 544) % 2 == 0 else "gpsimd", K.X[i:i + 544, :], K.xin[i:i + 544, :]))
    P.barrier()
    fin = []
    def finish():
      if (not pj_input) and K.dbgfm is not None:
        fin.append(P.dma("sync", K.dbgfm[:, 0:64], K.fm[:].rearrange("p a b c -> p (a b c)"), reads=[K.fm]))
        fin.append(P.dma("sync", K.dbgfm[:, 64:80], K.A1[:].rearrange("p a b -> p (a b)"), reads=[K.A1]))
        fin.append(P.dma("sync", K.dbgfm[:, 80:96], K.A2[:].rearrange("p a b -> p (a b)"), reads=[K.A2]))
        for g in range(2):
            for s in range(2):
                fin.append(P.dma("sync", K.dbgG[:, (g * 2 + s) * 1024:(g * 2 + s + 1) * 1024], K.G[g][s][:], reads=[K.G[g][s]]))
      P.barrier()
      if not K.final_ops:
          fin.append(P.dma("sync", K.out[0:128, :], K.X[0:128, :]))
      P.finalize(fin + last + K.final_ops)
      return nc, hc
    if plan is not None:
        phases = {"mods": phase_mods, "proj": phase_proj, "gdnpre": phase_gdn_pre, "gdns0": lambda K, l: phase_gdn_scan(K, l, 0),
                  "gdns1": lambda K, l: phase_gdn_scan(K, l, 1), "gdnfin": phase_gdn_fin, "gdnscan": phase_gdn_scan_both, "wout": phase_wout,
                  "mlp": lambda K, l: phase_mlp(K, l, False), "mlplast": lambda K, l: phase_mlp(K, l, True)}
        phases.update(EXTRA_PHASES); phases.update(EXTRA_D); phases.update(EXTRA_E); phases.update(EXTRA_F)
        for (nm, l) in plan:
            phases[nm](K, l)
        pj_input = "mods" not in [p for p, _ in plan]
        return finish()
    for l in range(L):
        if not pj_input:
            phase_mods(K, l)
            if stop_after == ("mods", l): return finish()
            phase_proj(K, l)
            if stop_after == ("proj", l): return finish()
        phase_gdn_pre(K, l)
        if stop_after == ("gdnpre", l): return finish()
        phase_gdn_scan(K, l, 0)
        if stop_after == ("gdns0", l): return finish()
        phase_gdn_scan(K, l, 1)
        phase_gdn_fin(K, l)
        if stop_after == ("gdn", l): return finish()
    return finish()

EXTRA_PHASES = {}

def host_inputs(inputs, hc):
    silu_in = []
    ins = []
    f = lambda a: np.ascontiguousarray(a, dtype=np.float32)
    ln1T = f(inputs["ln1_w"].reshape(L, 8, 128).transpose(2, 0, 1))
    ln2T = f(inputs["ln2_w"].reshape(L, 8, 128).transpose(2, 0, 1))
    mod_bT = f(inputs["mod_b"].reshape(L, 48, 128).transpose(0, 2, 1))
    for b in range(8):
        d = {}
        d["xin"] = f(np.concatenate([inputs["ctx"][b], inputs["x"][b]], axis=0))
        sT = np.stack([inputs["c"][b].reshape(8, 128).T, inputs["c_ctx"].reshape(8, 128).T], axis=-1)
        d["sTin"] = f(sT)
        d["mod_w"] = f(inputs["mod_w"]); d["mod_b"] = f(inputs["mod_b"]); d["mod_bT"] = mod_bT
        d["ln1T"] = ln1T; d["ln2T"] = ln2T
        d["w_in"] = f(inputs["w_in"])
        for k in ("w_out", "mlp_w1", "mlp_w2", "da_q_norm", "da_k_norm", "da_lam", "da_subln", "hg_lb_raw", "hg_norm_w", "hy_conv_w", "hy_w1", "hy_w2", "hy_w3", "hy_b1", "hy_f1", "hy_b2", "hy_f2", "hy_decay", "hy_bias"):
            d[k] = f(inputs[k])
        for k in ("gdn_conv_w", "gdn_a_log", "gdn_dt_bias", "gdn_norm_w"):
            d[k] = f(inputs[k])
        for k, v in hc.items():
            d["c_" + k] = v
        ins.append(d)
    return ins


FULL_PLAN = []
for _l in range(L):
    for _p in ("mods", "proj", "gdnpre", "gdnscan", "gdnfin", "hy", "hg", "attn", "wout"):
        FULL_PLAN.append((_p, _l))
    FULL_PLAN.append(("mlplast" if _l == L - 1 else "mlp", _l))


def kernel(**inputs):
    nc, hc = build(plan=FULL_PLAN)
    ins = host_inputs({k: np.asarray(v) for k, v in inputs.items()}, hc)
    res = run_bass_kernel_spmd(nc, ins, core_ids=list(range(8)))
    return np.stack([np.asarray(r["out"], dtype=np.float32) for r in res.results], axis=0)
```

```python
import numpy as np
from contextlib import ExitStack
import concourse.bass as bass
import concourse.mybir as mybir
from concourse.bass_utils import run_bass_kernel_spmd

F32 = mybir.dt.float32
BF16 = mybir.dt.bfloat16
AF = mybir.ActivationFunctionType
ALU = mybir.AluOpType
AX = mybir.AxisListType

COMPUTE = ("tensor", "vector", "scalar", "gpsimd")
NSLOT = 16


class Op:
    __slots__ = ("stream", "eng", "fn", "waits", "inc", "idx", "is_dma", "slot", "slot_use", "signal")

    def __init__(self):
        self.waits = []
        self.inc = None
        self.signal = False


class T:
    def __init__(self, name, ap):
        self.name = name
        self.ap = ap

    def __getitem__(self, k):
        return self.ap[k]


class Prog:
    def __init__(self, nc, arena_words=53000):
        self.nc = nc
        self.es = ExitStack()
        self.arena = self.es.enter_context(nc.sbuf_tensor("arena", [128, arena_words], F32))
        self.arena_words = arena_words
        self.bump = 0
        self.barrier_ops = []
        self.barrier_seen = set()
        self.ntile = 0
        self.ops = []
        self.last_w = {}
        self.readers = {}
        self.streams = {}
        self.sem = {}
        self.cnt = {}
        self.dma_slots = {}
        self.slot_uses = {}

    def tile(self, name, free, dt=F32, parts=128):
        free = list(free)
        n = 1
        for f in free:
            n *= f
        words = n if dt == F32 else (n + 1) // 2
        words = (words + 7) // 8 * 8
        assert self.bump + words <= self.arena_words, (name, self.bump, words)
        v = self.arena[0:parts, self.bump:self.bump + words]
        if dt != F32:
            v = v.bitcast(dt)
        v = v[:, 0:n]
        if len(free) == 2:
            v = v.rearrange("p (a b) -> p a b", a=free[0])
        elif len(free) == 3:
            v = v.rearrange("p (a b c) -> p a b c", a=free[0], b=free[1])
        self.bump += words
        self.ntile += 1
        return T("%s#%d" % (name, self.ntile), v)

    def mark(self):
        return self.bump

    def reset(self, mark):
        self.barrier()
        self.bump = mark

    def barrier(self):
        ops = []
        for st, lst in self.streams.items():
            n = NSLOT if st.startswith("dma_") else 1
            ops.extend(lst[-n:])
        for o in ops:
            o.signal = True
        self.barrier_ops = ops
        self.barrier_seen = set()

    def ps(self, name, shape, dt=F32):
        return T(name, self.es.enter_context(self.nc.psum_tensor(name, list(shape), dt))[:])

    def dram(self, name, shape, dt=F32, kind="Internal"):
        return self.nc.dram_tensor(name, list(shape), dt, kind=kind)

    @staticmethod
    def _k(r):
        if isinstance(r, (str, int)):
            return r
        if isinstance(r, tuple):
            return tuple(Prog._k(x) for x in r)
        return "T:" + r.name

    def _record(self, op, reads, writes, acc=False):
        reads = [self._k(r) for r in reads]
        writes = [self._k(r) for r in writes]
        deps = set()
        for r in reads:
            w = self.last_w.get(r)
            if w is not None:
                deps.add(w)
            if isinstance(r, str) and r.startswith("T:ps"):
                for rd in self.readers.get(r, ()):
                    if rd.stream != op.stream:
                        deps.add(rd)
        for wr in writes:
            w = self.last_w.get(wr)
            if w is not None and not (acc and w.stream == "tensor" and op.stream == "tensor"):
                deps.add(w)
            for rd in self.readers.get(wr, ()):
                deps.add(rd)
        if op.eng not in self.barrier_seen:
            self.barrier_seen.add(op.eng)
            for b in self.barrier_ops:
                deps.add(b)
        deps.discard(op)
        for d in deps:
            if d.stream == "tensor" and op.stream == "tensor":
                continue
            d.signal = True
            op.waits.append(d)
        for r in reads:
            self.readers.setdefault(r, []).append(op)
        for wr in writes:
            self.last_w[wr] = op
            self.readers[wr] = []
        op.idx = len(self.ops)
        self.ops.append(op)
        self.streams.setdefault(op.stream, []).append(op)

    def op(self, eng, fn, reads=(), writes=(), acc=False):
        o = Op()
        o.stream = eng
        o.eng = eng
        o.fn = fn
        o.is_dma = False
        self._record(o, reads, writes, acc)
        return o

    def dma(self, queue, out, in_, reads=(), writes=(), **kw):
        o = Op()
        o.stream = "dma_" + queue
        o.eng = queue
        o.fn = lambda e, out=out, in_=in_, kw=kw: e.dma_start(out=out, in_=in_, **kw)
        o.is_dma = True
        s = self.dma_slots.get(queue, 0)
        self.dma_slots[queue] = (s + 1) % NSLOT
        o.slot = s
        u = self.slot_uses.get((queue, s), 0) + 1
        self.slot_uses[(queue, s)] = u
        o.slot_use = u
        o.signal = True
        self._record(o, reads, writes)
        return o

    def finalize(self, final_wait_ops=()):
        nc = self.nc
        es = self.es
        for st in COMPUTE:
            self.sem[st] = es.enter_context(nc.semaphore("s_" + st))
        for q in self.dma_slots:
            for s in range(NSLOT):
                self.sem[("dma", q, s)] = es.enter_context(nc.semaphore("d_%s_%d" % (q, s)))
        for st in COMPUTE:
            c = 0
            for o in self.streams.get(st, ()):
                if o.signal:
                    c += 1
                    o.inc = c
        per_eng = {}
        for o in self.ops:
            per_eng.setdefault(o.eng, []).append(o)
        block = es.enter_context(nc.Block())

        def target(d):
            if d.is_dma:
                return self.sem[("dma", d.eng, d.slot)], 16 * d.slot_use
            return self.sem[d.stream], d.inc

        def emit(engname):
            ops = per_eng.get(engname, [])

            def body(e):
                waited = {}
                for o in ops:
                    ws = {}
                    for d in o.waits:
                        sem, val = target(d)
                        k = id(sem)
                        if waited.get(k, 0) >= val:
                            continue
                        if k not in ws or ws[k][1] < val:
                            ws[k] = (sem, val)
                    if o.is_dma and o.slot_use > 1:
                        sem = self.sem[("dma", o.eng, o.slot)]
                        val = 16 * (o.slot_use - 1)
                        k = id(sem)
                        if waited.get(k, 0) < val and (k not in ws or ws[k][1] < val):
                            ws[k] = (sem, val)
                    for k, (sem, val) in ws.items():
                        e.wait_ge(sem, val)
                        waited[k] = val
                    ins = o.fn(e)
                    if o.is_dma:
                        ins.then_inc(self.sem[("dma", o.eng, o.slot)], 16)
                    elif o.signal:
                        ins.then_inc(self.sem[o.stream], 1)
                if engname == "sync":
                    for d in final_wait_ops:
                        sem, val = target(d)
                        e.wait_ge(sem, val)
            return body

        for engname in ("sync", "scalar", "vector", "gpsimd", "tensor"):
            if engname in per_eng or engname == "sync":
                getattr(block, engname)(emit(engname))
        es.close()

import math
import numpy as np

U8 = mybir.dt.uint8
L = 4
D = 1024
TC = 256
TL = 4096
TT = TC + TL
NT = TT // 128
PIN = 3856
EPS = 1e-6
C_GDN = 0
C_HY = 1040
C_HG = 1808
C_DA = 3088


def host_consts():
    c = {}
    c["ident"] = np.eye(128, dtype=np.float32)
    c["ones"] = np.ones((128, 128), np.float32)
    j = np.arange(128)[:, None]
    i = np.arange(128)[None, :]
    same = (j // 64) == (i // 64)
    c["tri_f"] = (same & (j <= i)).astype(np.float32)
    c["tri_fs"] = (same & (j < i)).astype(np.float32)
    c["tri_b"] = (same & (j >= i)).astype(np.float32)
    c["tri_bs"] = (same & (j > i)).astype(np.float32)
    c["blk"] = same.astype(np.float32)
    return c


class Ctx:
    pass


def load_bcast(P, q, tile, src_ap, n):
    P.dma(q, tile[:], src_ap.partition_broadcast(128), writes=[tile])


def phase_mods(K, l):
    P = K.P
    nc = K.nc
    m0 = P.mark()
    sT = K.sT
    srep = P.tile("srep", [8, 2, 128])
    P.op("vector", lambda e: e.tensor_copy(srep[:], sT[:].unsqueeze(3).broadcast_to([128, 8, 2, 128])), reads=[sT], writes=[srep])
    fm = K.fm
    wblk = [P.tile("mw%d" % i, [8, 512]) for i in range(2)]
    mb = P.tile("mb", [48])
    P.dma("sync", mb[:], K.mod_bT[l], writes=[mb])
    grp = {0: 0, 1: 1, 3: 2, 4: 3}
    for gi, g in ((0, 2), (1, 5)):
        for s in range(2):
            load_bcast(P, "sync", K.G[gi][s], K.mod_b[l][g * 1024:(g + 1) * 1024], 1024)
    pb = 0
    for g in range(6):
        for hb in range(2):
            w = wblk[(g * 2 + hb) % 2]
            src = K.mod_w[l][:, g * 1024 + hb * 512:g * 1024 + hb * 512 + 512].rearrange("(c p) n -> p c n", p=128)
            P.dma("sync" if (g * 2 + hb) % 2 == 0 else "gpsimd", w[:], src, writes=[w])
            if g in grp:
                ps = K.ps[pb % 2]
                pb += 1
                for fb in range(4):
                    for c in range(8):
                        P.op("tensor", lambda e, ps=ps, w=w, fb=fb, c=c: e.matmul(
                            ps[:, fb * 2:fb * 2 + 2], w[:, c, fb * 128:(fb + 1) * 128], sT[:, c, :],
                            start=(c == 0), stop=(c == 7)), reads=[w, sT], writes=[ps], acc=True)
                gi = grp[g]
                for fb in range(4):
                    ch = hb * 4 + fb
                    P.op("vector", lambda e, ps=ps, fb=fb, gi=gi, ch=ch, g=g: e.tensor_scalar(
                        fm[:, gi, ch, :], ps[:, fb * 2:fb * 2 + 2], mb[:, g * 8 + ch:g * 8 + ch + 1], 1.0, ALU.add, ALU.mult),
                        reads=[ps, mb], writes=[fm])
            else:
                gi = 0 if g == 2 else 1
                for s in range(2):
                    ps = K.ps[2 + (pb % 2)]
                    pb += 1
                    for c in range(8):
                        P.op("tensor", lambda e, ps=ps, w=w, c=c, s=s: e.matmul(
                            ps[:, 0:512], srep[:, c, s, :], w[:, c, :], start=(c == 0), stop=(c == 7)),
                            reads=[w, srep], writes=[ps], acc=True)
                    G = K.G[gi][s]
                    P.op("vector", lambda e, ps=ps, G=G, hb=hb: e.tensor_tensor(
                        G[:, hb * 512:(hb + 1) * 512], ps[:, 0:512], G[:, hb * 512:(hb + 1) * 512], ALU.add),
                        reads=[ps, G], writes=[G])
    for (Av, lnw, si) in ((K.A1, K.ln1T, 1), (K.A2, K.ln2T, 3)):
        P.op("vector", lambda e, Av=Av, si=si: e.tensor_scalar(Av[:], fm[:, si, :, :], 1.0, 1.0, ALU.add, ALU.mult), reads=[fm], writes=[Av])
        P.op("vector", lambda e, Av=Av, lnw=lnw: e.tensor_tensor(
            Av[:], Av[:], lnw[:, l, :].unsqueeze(2).broadcast_to([128, 8, 2]), ALU.mult), reads=[Av, lnw], writes=[Av])
    P.reset(m0)


def rms_tile(K, xt, xn_bf, scr, ss, rs):
    P = K.P
    P.op("scalar", lambda e: e.activation(scr[:], xt[:], AF.Square, accum_out=ss[:]), reads=[xt], writes=[scr, ss])
    P.op("scalar", lambda e: e.activation(rs[:], ss[:], AF.Sqrt, bias=K.epsc[:, 0:1], scale=1.0 / D), reads=[ss, K.epsc], writes=[rs])
    P.op("vector", lambda e: e.reciprocal(rs[:], rs[:]), reads=[rs], writes=[rs])
    P.op("vector", lambda e: e.tensor_scalar(xn_bf[:], xt[:], rs[:, 0:1], 1.0, ALU.mult, ALU.mult), reads=[xt, rs], writes=[xn_bf])


def transpose_mod(K, xn_bf, hT, Av, Bv, s, psb_t, col0=0):
    P = K.P
    psb = psb_t.ap[:, 0:512].bitcast(BF16).rearrange("p (c t) -> p c t", c=8)
    for c in range(8):
        P.op("tensor", lambda e, c=c: e.transpose(psb[:, c, :], xn_bf[:, c * 128:(c + 1) * 128], K.identb[:]),
             reads=[xn_bf, K.identb], writes=[psb_t])
    tmp = K.tmod
    P.op("vector", lambda e: e.tensor_tensor(tmp[:], psb, Av[:, :, s:s + 1].broadcast_to([128, 8, 128]), ALU.mult),
         reads=[psb_t, Av], writes=[tmp])
    P.op("gpsimd", lambda e: e.tensor_tensor(hT[:, :, col0:col0 + 128], tmp[:], Bv[:, :, s:s + 1].broadcast_to([128, 8, 128]), ALU.add),
         reads=[tmp, Bv], writes=[hT])


def phase_proj(K, l):
    P = K.P
    m0 = P.mark()
    W = P.tile("win", [8, PIN], BF16)
    for c in range(8):
        P.dma("gpsimd", W[:, c, :], K.w_in[l][c * 128:(c + 1) * 128, :], writes=[W])
    xt = [P.tile("xt%d" % i, [D]) for i in range(2)]
    xn = [P.tile("xn%d" % i, [D], BF16) for i in range(2)]
    hT = [P.tile("hT%d" % i, [8, 128], BF16) for i in range(2)]
    ot = [P.tile("ot%d" % i, [PIN]) for i in range(2)]
    scr = P.tile("scr", [D])
    ss = [P.tile("ss%d" % i, [1]) for i in range(2)]
    rs = [P.tile("rs%d" % i, [1]) for i in range(2)]
    K.tmod = P.tile("tmod", [8, 128])
    Bv = K.fm
    P.dma("sync", xt[0][:], K.X[0:128, :], writes=[xt[0]])
    nblk = (PIN + 511) // 512
    for i in range(NT):
        b = i % 2
        if i + 1 < NT:
            P.dma("sync", xt[1 - b][:], K.X[(i + 1) * 128:(i + 2) * 128, :], writes=[xt[1 - b]])
        s = 1 if i < 2 else 0
        rms_tile(K, xt[b], xn[b], scr, ss[b], rs[b])
        transpose_mod(K, xn[b], hT[b], K.A1, T(K.fm.name, K.fm[:, 0, :, :]), s, K.ps[7])
        for nb in range(nblk):
            c0 = nb * 512
            w = min(512, PIN - c0)
            ps = K.ps[nb % 4]
            for c in range(8):
                P.op("tensor", lambda e, ps=ps, c=c, c0=c0, w=w, b=b: e.matmul(
                    ps[:, 0:w], hT[b][:, c, :], W[:, c, c0:c0 + w], start=(c == 0), stop=(c == 7)),
                    reads=[hT[b], W], writes=[ps], acc=True)
            if nb % 2 == 0:
                P.op("scalar", lambda e, ps=ps, c0=c0, w=w, b=b: e.copy(ot[b][:, c0:c0 + w], ps[:, 0:w]), reads=[ps], writes=[ot[b]])
            else:
                P.op("vector", lambda e, ps=ps, c0=c0, w=w, b=b: e.tensor_copy(ot[b][:, c0:c0 + w], ps[:, 0:w]), reads=[ps], writes=[ot[b]])
        P.dma("sync", K.Pj[i * 128:(i + 1) * 128, :], ot[b][:], reads=[ot[b]])
    P.reset(m0)


GQW = 768 + 16


def V(P, fn, reads, writes):
    return P.op("vector", fn, reads=reads, writes=writes)


def G_(P, fn, reads, writes):
    return P.op("gpsimd", fn, reads=reads, writes=writes)


def A_(P, fn, reads, writes):
    return P.op("scalar", fn, reads=reads, writes=writes)


def MM(P, ps, out_ap, lhsT, rhs, reads, start=True, stop=True):
    return P.op("tensor", lambda e: e.matmul(out_ap, lhsT, rhs, start=start, stop=stop), reads=reads, writes=[ps], acc=True)


def TR(P, ps, out_ap, in_ap, ident_ap, reads):
    return P.op("tensor", lambda e: e.transpose(out_ap, in_ap, ident_ap), reads=reads, writes=[ps])


def load_shift3(K, q, dst, src, i, c0, c1):
    P = K.P
    r0 = i * 128
    first = i in (0, 2)
    last = i in (1, NT - 1)
    if first:
        V(P, lambda e: e.memset(dst[0][:], 0.0), [], [dst[0]])
        P.dma(q, dst[0][1:128, :], src[r0:r0 + 127, c0:c1], writes=[dst[0]])
    else:
        P.dma(q, dst[0][:], src[r0 - 1:r0 + 127, c0:c1], writes=[dst[0]])
    P.dma(q, dst[1][:], src[r0:r0 + 128, c0:c1], writes=[dst[1]])
    if last:
        V(P, lambda e: e.memset(dst[2][:], 0.0), [], [dst[2]])
        P.dma(q, dst[2][0:127, :], src[r0 + 1:r0 + 128, c0:c1], writes=[dst[2]])
    else:
        P.dma(q, dst[2][:], src[r0 + 1:r0 + 129, c0:c1], writes=[dst[2]])


def conv3(K, x3, cw, acc, t0, t2, n):
    P = K.P
    G_(P, lambda e: e.tensor_tensor(t0[:, 0:n], x3[0][:, 0:n], cw[:, 0, 0:n], ALU.mult), [x3[0], cw], [t0])
    V(P, lambda e: e.tensor_tensor(acc[:, 0:n], x3[1][:, 0:n], cw[:, 1, 0:n], ALU.mult), [x3[1], cw], [acc])
    G_(P, lambda e: e.tensor_tensor(t2[:, 0:n], x3[2][:, 0:n], cw[:, 2, 0:n], ALU.mult), [x3[2], cw], [t2])
    V(P, lambda e: e.tensor_tensor(acc[:, 0:n], acc[:, 0:n], t0[:, 0:n], ALU.add), [acc, t0], [acc])
    V(P, lambda e: e.tensor_tensor(acc[:, 0:n], acc[:, 0:n], t2[:, 0:n], ALU.add), [acc, t2], [acc])


def phase_gdn_pre(K, l):
    P = K.P
    m0 = P.mark()
    cw = P.tile("cw", [3, 768])
    for k in range(3):
        P.dma("sync", cw[:, k, :], K.gdn_conv_w[l][k].partition_broadcast(128), writes=[cw])
    negA = P.tile("negA", [8])
    dtb = P.tile("dtb", [8])
    P.dma("sync", negA[:], K.gdn_a_log[l].rearrange("a b -> (a b)").partition_broadcast(128), writes=[negA])
    P.dma("sync", dtb[:], K.gdn_dt_bias[l].rearrange("a b -> (a b)").partition_broadcast(128), writes=[dtb])
    A_(P, lambda e: e.activation(negA[:], negA[:], AF.Exp), [negA], [negA])
    V(P, lambda e: e.tensor_scalar(negA[:], negA[:], -1.0, 1.0, ALU.mult, ALU.mult), [negA], [negA])
    x3 = [[P.tile("x3_%d_%d" % (b, k), [768]) for k in range(3)] for b in range(2)]
    zab = [P.tile("zab%d" % b, [16]) for b in range(2)]
    acc = P.tile("acc", [768])
    t0 = P.tile("t0", [768])
    t2 = P.tile("t2", [768])
    sq = P.tile("sq", [512])
    ssum = P.tile("ssum", [8])
    tg = P.tile("tg", [8])
    tb = P.tile("tb", [8])
    og = [P.tile("og%d" % b, [GQW]) for b in range(2)]

    def loads(i):
        b = i % 2
        load_shift3(K, "sync", x3[b], K.Pj, i, 0, 768)
        P.dma("sync", zab[b][:], K.Pj[i * 128:(i + 1) * 128, 1024:1040], writes=[zab[b]])
    loads(0)
    for i in range(NT):
        b = i % 2
        if i + 1 < NT:
            loads(i + 1)
        o = og[b]
        conv3(K, x3[b], cw, acc, t0, t2, 768)
        A_(P, lambda e, o=o: e.activation(o[:, 0:768], acc[:], AF.Silu), [acc], [o])
        G_(P, lambda e, o=o: e.tensor_tensor(sq[:], o[:, 0:512], o[:, 0:512], ALU.mult), [o], [sq])
        V(P, lambda e: e.tensor_reduce(ssum[:], sq[:].rearrange("p (g d) -> p g d", g=8), AX.X, ALU.add), [sq], [ssum])
        A_(P, lambda e: e.activation(ssum[:], ssum[:], AF.Sqrt, bias=K.epsc[:, 0:1], scale=1.0), [ssum, K.epsc], [ssum])
        V(P, lambda e: e.reciprocal(ssum[:], ssum[:]), [ssum], [ssum])
        V(P, lambda e: e.tensor_scalar(ssum[:, 0:4], ssum[:, 0:4], 0.125, 1.0, ALU.mult, ALU.mult), [ssum], [ssum])
        V(P, lambda e, o=o: e.tensor_tensor(o[:, 0:512].rearrange("p (g d) -> p g d", g=8), o[:, 0:512].rearrange("p (g d) -> p g d", g=8),
                                            ssum[:].unsqueeze(2).broadcast_to([128, 8, 64]), ALU.mult), [o, ssum], [o])
        z = zab[b]
        V(P, lambda e, z=z: e.tensor_tensor(tg[:], z[:, 0:8], dtb[:], ALU.add), [z, dtb], [tg])
        A_(P, lambda e: e.activation(tg[:], tg[:], AF.Exp), [tg], [tg])
        A_(P, lambda e: e.activation(tg[:], tg[:], AF.Ln, bias=K.onec[:, 0:1], scale=1.0), [tg, K.onec], [tg])
        V(P, lambda e: e.tensor_tensor(tg[:], tg[:], negA[:], ALU.mult), [tg, negA], [tg])
        A_(P, lambda e, z=z: e.activation(tb[:], z[:, 8:16], AF.Sigmoid), [z], [tb])
        for gb, src in ((0, tg), (1, tb)):
            V(P, lambda e, o=o, gb=gb, src=src: e.tensor_copy(
                o[:, 768:784].rearrange("p (d hh gb pr) -> p d hh gb pr", d=2, hh=2, gb=2)[:, :, :, gb, :],
                src[:].rearrange("p (d pr hh) -> p d hh pr", d=2, pr=2)), [src], [o])
        P.dma("sync", K.GQ[i * 128:(i + 1) * 128, :], o[:], reads=[o])
    P.reset(m0)


def gdn_scan_gen(K, l, d, nps):
    P = K.P
    tri = P.tile("tri", [128])
    tris = P.tile("tris", [128])
    blk = P.tile("blk", [128])
    P.dma("sync", tri[:], K.cin["tri_f" if d == 0 else "tri_b"], writes=[tri])
    P.dma("sync", tris[:], K.cin["tri_fs" if d == 0 else "tri_bs"], writes=[tris])
    P.dma("sync", blk[:], K.cin["blk"], writes=[blk])
    ident, ones = K.ident, K.ones
    S = [P.tile("S%d" % p, [64]) for p in range(2)]
    for p in range(2):
        V(P, lambda e, p=p: e.memset(S[p][:], 0.0), [], [S[p]])
    NB = 2
    qkv = [P.tile("qkv%d" % b, [3, 2, 64]) for b in range(NB)]
    gb = [P.tile("gb%d" % b, [2, 2]) for b in range(NB)]
    ot = [P.tile("ogo%d" % b, [2, 64]) for b in range(NB)]
    def pt(name, free):
        return [P.tile("%s%d" % (name, p), free) for p in range(2)]
    gc = P.tile("gc", [2]); egc = P.tile("egc", [2]); glt = P.tile("glt", [2]); eglt = P.tile("eglt", [2]); ekl = P.tile("ekl", [2]); nbeta = P.tile("nbeta", [2])
    dg = pt("dg", [256]); rows = pt("rows", [256]); E = pt("E", [128]); Dm = pt("Dm", [128]); Dms = pt("Dms", [128])
    kT = pt("kT", [128]); qT = pt("qT", [128]); NTm = pt("NTm", [128]); Nm = pt("Nm", [128]); PT2 = pt("PT2", [128]); P2 = pt("P2", [128])
    RT = pt("RT", [128]); aT = pt("aT", [128]); MTb = pt("MTb", [128]); kg0 = pt("kg0", [128]); kgl = pt("kgl", [128]); qgb = pt("qgb", [128])
    wT = pt("wT", [128]); qgT = pt("qgT", [128]); u = pt("u", [64]); vn = pt("vn", [64])
    for p in range(2):
        for t in (kg0[p], kgl[p], qgb[p]):
            G_(P, lambda e, t=t: e.memset(t[:], 0.0), [], [t])
    ctx_ch = list(range(0, TC // 64))
    lat_ch = list(range(TC // 64, TT // 64))
    order = ctx_ch + lat_ch if d == 0 else ctx_ch[::-1] + lat_ch[::-1]
    import os
    if os.environ.get("GDN_NCH"):
        order = order[:int(os.environ["GDN_NCH"])]

    def loads(ci):
        c = order[ci]
        b = ci % NB
        r0 = c * 64
        for hh in range(2):
            src = K.GQ[r0:r0 + 64, 0:768].rearrange("r (t pr hh e) -> r t pr hh e", t=3, pr=2, hh=2)[:, :, :, hh, :]
            P.dma("sync", qkv[b][hh * 64:(hh + 1) * 64, :, :, :], src, writes=[qkv[b]])
            c0 = 768 + d * 8 + hh * 4
            P.dma("sync", gb[b][hh * 64:(hh + 1) * 64, :, :], K.GQ[r0:r0 + 64, c0:c0 + 4].rearrange("r (a b) -> r a b", a=2), writes=[gb[b]])
    loads(0)
    for ci in range(len(order)):
        c = order[ci]
        b = ci % NB
        if ci + 1 < len(order):
            loads(ci + 1)
        Q = qkv[b]
        g = gb[b]
        ps = nps()
        MM(P, ps, ps[:, 0:2], tri[:], g[:, 0, :], [tri, g])
        MM(P, ps, ps[:, 2:4], blk[:], g[:, 0, :], [blk, g])
        V(P, lambda e, ps=ps: e.tensor_copy(gc[:], ps[:, 0:2]), [ps], [gc])
        V(P, lambda e, ps=ps: e.tensor_copy(glt[:], ps[:, 2:4]), [ps], [glt])
        A_(P, lambda e: e.activation(egc[:], gc[:], AF.Exp), [gc], [egc])
        A_(P, lambda e: e.activation(eglt[:], glt[:], AF.Exp), [glt], [eglt])
        V(P, lambda e: e.tensor_tensor(ekl[:], glt[:], gc[:], ALU.subtract), [glt, gc], [ekl])
        A_(P, lambda e: e.activation(ekl[:], ekl[:], AF.Exp), [ekl], [ekl])
        V(P, lambda e, g=g: e.tensor_scalar(nbeta[:], g[:, 1, :], -1.0, 1.0, ALU.mult, ALU.mult), [g], [nbeta])
        yield

        def pair_gen(p, Q=Q, g=g, b=b):
            kn = Q[:, 1, p, :]
            qn = Q[:, 0, p, :]
            vv = Q[:, 2, p, :]
            V(P, lambda e, p=p: e.tensor_scalar(dg[p][:, 0:128], ident[:], gc[:, p:p + 1], 1.0, ALU.mult, ALU.mult), [ident, gc], [dg[p]])
            G_(P, lambda e, p=p, g=g: e.tensor_scalar(dg[p][:, 128:256], ident[:], g[:, 1, p:p + 1], 1.0, ALU.mult, ALU.mult), [ident, g], [dg[p]])
            yield
            ps = nps()
            psb_ = nps()
            MM(P, ps, ps[:, 0:128], ones[:], dg[p][:, 0:128], [ones, dg[p]])
            MM(P, psb_, psb_[:, 0:128], ones[:], dg[p][:, 128:256], [ones, dg[p]])
            yield
            A_(P, lambda e, p=p, ps=psb_: e.copy(rows[p][:, 128:256], ps[:, 0:128]), [psb_], [rows[p]])
            V(P, lambda e, p=p, ps=ps: e.tensor_scalar(E[p][:], ps[:, 0:128], gc[:, p:p + 1], 0.0, ALU.subtract, ALU.min), [ps, gc], [E[p]])
            yield
            A_(P, lambda e, p=p: e.activation(E[p][:], E[p][:], AF.Exp), [E[p]], [E[p]])
            yield
            G_(P, lambda e, p=p: e.tensor_tensor(Dm[p][:], E[p][:], tri[:], ALU.mult), [E[p], tri], [Dm[p]])
            G_(P, lambda e, p=p: e.tensor_tensor(Dms[p][:], E[p][:], tris[:], ALU.mult), [E[p], tris], [Dms[p]])
            yield
            ps = nps()
            psb_ = nps()
            TR(P, ps, ps[0:64, 0:128], kn, ident[:], [Q, ident])
            TR(P, psb_, psb_[0:64, 0:128], qn, ident[:], [Q, ident])
            yield
            A_(P, lambda e, p=p, ps=ps: e.copy(kT[p][0:64, :], ps[0:64, 0:128]), [ps], [kT[p]])
            V(P, lambda e, p=p, ps=psb_: e.tensor_copy(qT[p][0:64, :], ps[0:64, 0:128]), [psb_], [qT[p]])
            yield
            ps = nps()
            MM(P, ps, ps[:, 0:128], kT[p][0:64, :], kT[p][0:64, :], [kT[p]])
            MM(P, ps, ps[:, 128:256], kT[p][0:64, :], qT[p][0:64, :], [kT[p], qT[p]])
            yield
            V(P, lambda e, p=p, ps=ps: e.scalar_tensor_tensor(NTm[p][:], ps[:, 0:128], nbeta[:, p:p + 1], Dms[p][:], ALU.mult, ALU.mult),
              [ps, nbeta, Dms[p]], [NTm[p]])
            V(P, lambda e, p=p, ps=ps: e.tensor_tensor(aT[p][:], ps[:, 128:256], Dm[p][:], ALU.mult), [ps, Dm[p]], [aT[p]])
            yield
            ps = nps()
            TR(P, ps, ps[:, 0:128], NTm[p][:], ident[:], [NTm[p], ident])
            yield
            A_(P, lambda e, p=p, ps=ps: e.copy(Nm[p][:], ps[:, 0:128]), [ps], [Nm[p]])
            G_(P, lambda e, p=p: e.tensor_tensor(RT[p][:], NTm[p][:], ident[:], ALU.add), [NTm[p], ident], [RT[p]])
            Pk, PTk = Nm[p], NTm[p]
            Pn_t, PTn_t = P2[p], PT2[p]
            for k in range(5):
                yield
                ps = nps()
                MM(P, ps, ps[:, 0:128], PTk[:], Pk[:], [PTk, Pk])
                if k < 4:
                    psb_ = nps()
                    MM(P, psb_, psb_[:, 0:128], Pk[:], PTk[:], [PTk, Pk])
                yield
                A_(P, lambda e, ps=ps, t=Pn_t: e.copy(t[:], ps[:, 0:128]), [ps], [Pn_t])
                if k < 4:
                    V(P, lambda e, ps=psb_, t=PTn_t: e.tensor_copy(t[:], ps[:, 0:128]), [psb_], [PTn_t])
                yield
                ps2 = nps()
                MM(P, ps2, ps2[:, 0:128], Pn_t[:], RT[p][:], [Pn_t, RT[p]])
                yield
                V(P, lambda e, ps2=ps2, p=p: e.tensor_tensor(RT[p][:], ps2[:, 0:128], RT[p][:], ALU.add), [ps2, RT[p]], [RT[p]])
                Pk, PTk, Pn_t, PTn_t = Pn_t, PTn_t, Pk, PTk
            yield
            G_(P, lambda e, p=p: e.tensor_tensor(MTb[p][:], RT[p][:], rows[p][:, 128:256], ALU.mult), [RT[p], rows[p]], [MTb[p]])
            for hh in range(2):
                r = slice(hh * 64, hh * 64 + 64)
                V(P, lambda e, p=p, r=r, kn=kn: e.tensor_scalar(kg0[p][r, r], kn[r, :], egc[r, p:p + 1], 1.0, ALU.mult, ALU.mult), [Q, egc], [kg0[p]])
                G_(P, lambda e, p=p, r=r, kn=kn: e.tensor_scalar(kgl[p][r, r], kn[r, :], ekl[r, p:p + 1], 1.0, ALU.mult, ALU.mult), [Q, ekl], [kgl[p]])
                V(P, lambda e, p=p, r=r, qn=qn: e.tensor_scalar(qgb[p][r, r], qn[r, :], egc[r, p:p + 1], 1.0, ALU.mult, ALU.mult), [Q, egc], [qgb[p]])
            yield
            ps = nps()
            psb_ = nps()
            MM(P, ps, ps[:, 0:64], MTb[p][:], vv, [MTb[p], Q])
            MM(P, psb_, psb_[:, 0:128], kg0[p][:], MTb[p][:], [kg0[p], MTb[p]])
            TR(P, ps, ps[:, 256:384], qgb[p][:], ident[:], [qgb[p], ident])
            yield
            A_(P, lambda e, p=p, ps=ps: e.copy(u[p][:], ps[:, 0:64]), [ps], [u[p]])
            V(P, lambda e, p=p, ps=psb_: e.tensor_copy(wT[p][:], ps[:, 0:128]), [psb_], [wT[p]])
            A_(P, lambda e, p=p, ps=ps: e.copy(qgT[p][:], ps[:, 256:384]), [ps], [qgT[p]])
            yield
            ps = nps()
            MM(P, ps, ps[:, 0:64], wT[p][:], S[p][:], [wT[p], S[p]])
            yield
            V(P, lambda e, p=p, ps=ps: e.tensor_tensor(vn[p][:], u[p][:], ps[:, 0:64], ALU.subtract), [u[p], ps], [vn[p]])
            yield
            ps = nps()
            psb_ = nps()
            MM(P, ps, ps[:, 0:64], qgT[p][:], S[p][:], [qgT[p], S[p]], start=True, stop=False)
            MM(P, ps, ps[:, 0:64], aT[p][:], vn[p][:], [aT[p], vn[p]], start=False, stop=True)
            MM(P, psb_, psb_[:, 0:64], kgl[p][:], vn[p][:], [kgl[p], vn[p]])
            yield
            A_(P, lambda e, p=p, ps=ps, b=b: e.copy(ot[b][:, p, :], ps[:, 0:64]), [ps], [ot[b]])
            V(P, lambda e, p=p, ps=psb_: e.scalar_tensor_tensor(S[p][:], S[p][:], eglt[:, p:p + 1], ps[:, 0:64], ALU.mult, ALU.add),
              [S[p], eglt, psb_], [S[p]])
        pg = [pair_gen(0), pair_gen(1)]
        while pg:
            for gg_ in pg[:]:
                try:
                    next(gg_)
                except StopIteration:
                    pg.remove(gg_)
            yield
        r0 = c * 64
        for hh in range(2):
            dst = K.OG[d][r0:r0 + 64, :].rearrange("r (pr hh e) -> r pr hh e", pr=2, hh=2)[:, :, hh, :]
            P.dma("sync", dst, ot[b][hh * 64:(hh + 1) * 64, :, :], reads=[ot[b]])


def lockstep(gens):
    gens = list(gens)
    while gens:
        for g_ in gens[:]:
            try:
                next(g_)
            except StopIteration:
                gens.remove(g_)


def phase_gdn_scan_both(K, l):
    P = K.P
    m0 = P.mark()
    psn = [0]

    def nps():
        psn[0] = (psn[0] + 1) % 8
        return K.ps[psn[0]]
    lockstep([gdn_scan_gen(K, l, 0, nps), gdn_scan_gen(K, l, 1, nps)])
    P.reset(m0)


def phase_gdn_scan(K, l, d):
    P = K.P
    m0 = P.mark()
    psn = [0]

    def nps():
        psn[0] = (psn[0] + 1) % 8
        return K.ps[psn[0]]
    lockstep([gdn_scan_gen(K, l, d, nps)])
    P.reset(m0)


def phase_gdn_fin(K, l):
    P = K.P
    m0 = P.mark()
    nw = P.tile("gnw", [64])
    P.dma("sync", nw[:], K.gdn_norm_w[l].partition_broadcast(128), writes=[nw])
    o0 = [P.tile("o0_%d" % b, [256]) for b in range(2)]
    o1 = [P.tile("o1_%d" % b, [256]) for b in range(2)]
    zt = [P.tile("zt%d" % b, [256]) for b in range(2)]
    sq = P.tile("sq", [256]); ss = P.tile("ss", [4])

    def loads(i):
        b = i % 2
        rs = slice(i * 128, (i + 1) * 128)
        P.dma("sync", o0[b][:], K.OG[0][rs, :], writes=[o0[b]])
        P.dma("sync", o1[b][:], K.OG[1][rs, :], writes=[o1[b]])
        P.dma("sync", zt[b][:], K.Pj[rs, 768:1024], writes=[zt[b]])
    loads(0)
    for i in range(NT):
        b = i % 2
        if i + 1 < NT:
            loads(i + 1)
        o = o0[b]
        V(P, lambda e, o=o, b=b: e.tensor_tensor(o[:], o[:], o1[b][:], ALU.add), [o, o1[b]], [o])
        head_rms_gate(K, o, zt[b], nw, sq, ss, 4, 64, 1.0)
        P.dma("sync", K.O[i * 128:(i + 1) * 128, 0:256], o[:], reads=[o])
    P.reset(m0)


def head_rms_gate(K, o, zt, nw, sq, ss, nh, hd, mult):
    P = K.P
    n = nh * hd
    G_(P, lambda e: e.tensor_tensor(sq[:, 0:n], o[:, 0:n], o[:, 0:n], ALU.mult), [o], [sq])
    V(P, lambda e: e.tensor_reduce(ss[:, 0:nh], sq[:, 0:n].rearrange("p (g d) -> p g d", g=nh), AX.X, ALU.add), [sq], [ss])
    A_(P, lambda e: e.activation(ss[:, 0:nh], ss[:, 0:nh], AF.Sqrt, bias=K.epsc[:, 0:1], scale=1.0 / hd), [ss, K.epsc], [ss])
    V(P, lambda e: e.reciprocal(ss[:, 0:nh], ss[:, 0:nh]), [ss], [ss])
    if mult != 1.0:
        V(P, lambda e: e.tensor_scalar(ss[:, 0:nh], ss[:, 0:nh], mult, 1.0, ALU.mult, ALU.mult), [ss], [ss])
    o3 = o[:, 0:n].rearrange("p (g d) -> p g d", g=nh)
    V(P, lambda e: e.tensor_tensor(o3, o3, ss[:, 0:nh].unsqueeze(2).broadcast_to([128, nh, hd]), ALU.mult), [o, ss], [o])
    G_(P, lambda e: e.tensor_tensor(o3, o3, nw[:, 0:hd].unsqueeze(1).broadcast_to([128, nh, hd]), ALU.mult), [o, nw], [o])
    if zt is not None:
        A_(P, lambda e: e.activation(zt[:, 0:n], zt[:, 0:n], AF.Silu), [zt], [zt])
        V(P, lambda e: e.tensor_tensor(o[:, 0:n], o[:, 0:n], zt[:, 0:n], ALU.mult), [o, zt], [o])


def phase_wout(K, l):
    P = K.P
    m0 = P.mark()
    W = P.tile("wout", [8, D], BF16)
    for c in range(8):
        P.dma("gpsimd", W[:, c, :], K.w_out[l][c * 128:(c + 1) * 128, :], writes=[W])
    ot = [P.tile("wo_o%d" % b, [D]) for b in range(2)]
    xt = [P.tile("wo_x%d" % b, [D]) for b in range(2)]
    ob = P.tile("wo_ob", [D], BF16)
    oT = P.tile("wo_oT", [8, 128], BF16)
    tmp = P.tile("wo_tmp", [512])
    psb_t = K.ps[7]
    psb = psb_t.ap[:, 0:512].bitcast(BF16).rearrange("p (c t) -> p c t", c=8)

    def loads(i):
        b = i % 2
        rs = slice(i * 128, (i + 1) * 128)
        P.dma("sync", ot[b][:], K.O[rs, :], writes=[ot[b]])
        P.dma("sync", xt[b][:], K.X[rs, :], writes=[xt[b]])
    loads(0)
    for i in range(NT):
        b = i % 2
        if i + 1 < NT:
            loads(i + 1)
        s = 1 if i < 2 else 0
        A_(P, lambda e, b=b: e.copy(ob[:], ot[b][:]), [ot[b]], [ob])
        for c in range(8):
            TR(P, psb_t, psb[:, c, :], ob[:, c * 128:(c + 1) * 128], K.identb[:], [ob, K.identb])
        V(P, lambda e: e.tensor_copy(oT[:], psb), [psb_t], [oT])
        for nb in range(2):
            ps = K.ps[nb]
            for c in range(8):
                MM(P, ps, ps[:, 0:512], oT[:, c, :], W[:, c, nb * 512:(nb + 1) * 512], [oT, W], start=(c == 0), stop=(c == 7))
            G1 = K.G[0][s]
            V(P, lambda e, ps=ps, nb=nb, G1=G1: e.tensor_tensor(tmp[:], ps[:, 0:512], G1[:, nb * 512:(nb + 1) * 512], ALU.mult), [ps, G1], [tmp])
            G_(P, lambda e, nb=nb, b=b: e.tensor_tensor(xt[b][:, nb * 512:(nb + 1) * 512], xt[b][:, nb * 512:(nb + 1) * 512], tmp[:], ALU.add), [xt[b], tmp], [xt[b]])
        P.dma("sync", K.X[i * 128:(i + 1) * 128, :], xt[b][:], reads=[xt[b]])
    P.reset(m0)


def phase_mlp(K, l, last):
    P = K.P
    m0 = P.mark()
    W1 = P.tile("w1", [8, 4 * D], BF16)
    W2 = P.tile("w2", [32, D], BF16)
    for c in range(8):
        P.dma("gpsimd", W1[:, c, :], K.mlp_w1[l][c * 128:(c + 1) * 128, :], writes=[W1])
    for c in range(8):
        P.dma("gpsimd", W2[:, c * 4:(c + 1) * 4, :], K.mlp_w2[l][c * 512:(c + 1) * 512, :].rearrange("(f p) n -> p f n", p=128), writes=[W2])
    GT = 2
    NG = NT // GT
    xt = [[P.tile("ml_x%d_%d" % (b, t), [D]) for t in range(GT)] for b in range(2)]
    xn = P.tile("ml_xn", [D], BF16)
    hT = P.tile("ml_hT", [8, GT * 128], BF16)
    hid = P.tile("ml_hid", [32, GT * 128], BF16)
    K.tmod = P.tile("ml_tmod", [8, 128])
    scr = T(K.tmod.name, K.tmod[:].rearrange("p a b -> p (a b)"))
    ss = P.tile("ml_ss", [1]); rs = P.tile("ml_rs", [1])
    rl = [P.tile("ml_rl%d" % b, [GT * 128]) for b in range(2)]
    tmp = P.tile("ml_tmp", [512])
    B2 = T(K.fm.name, K.fm[:, 2, :, :])

    def loads(g):
        b = g % 2
        for t in range(GT):
            i = g * GT + t
            P.dma("sync", xt[b][t][:], K.X[i * 128:(i + 1) * 128, :], writes=[xt[b][t]])
    loads(0)
    for g in range(NG):
        b = g % 2
        if g + 1 < NG:
            loads(g + 1)
        s = 1 if g == 0 else 0
        for t in range(GT):
            rms_tile(K, xt[b][t], xn, scr, ss, rs)
            transpose_mod(K, xn, hT, K.A2, B2, s, K.ps[7], col0=t * 128)
        for fb in range(32):
            ps = K.ps[fb % 4]
            for c in range(8):
                MM(P, ps, ps[:, 0:GT * 128], W1[:, c, fb * 128:(fb + 1) * 128], hT[:, c, :], [W1, hT], start=(c == 0), stop=(c == 7))
            r = rl[fb % 2]
            A_(P, lambda e, ps=ps, r=r: e.activation(r[:], ps[:, 0:GT * 128], AF.Relu), [ps], [r])
            G_(P, lambda e, r=r, fb=fb: e.tensor_tensor(hid[:, fb, :], r[:], r[:], ALU.mult), [r], [hid])
        G2 = K.G[1][s]
        for t in range(GT):
            i = g * GT + t
            x = xt[b][t]
            for nb in range(2):
                ps = K.ps[4 + nb]
                for fb in range(32):
                    MM(P, ps, ps[:, 0:512], hid[:, fb, t * 128:(t + 1) * 128], W2[:, fb, nb * 512:(nb + 1) * 512], [hid, W2], start=(fb == 0), stop=(fb == 31))
                V(P, lambda e, ps=ps, nb=nb, G2=G2: e.tensor_tensor(tmp[:], ps[:, 0:512], G2[:, nb * 512:(nb + 1) * 512], ALU.mult), [ps, G2], [tmp])
                V(P, lambda e, nb=nb, x=x: e.tensor_tensor(x[:, nb * 512:(nb + 1) * 512], x[:, nb * 512:(nb + 1) * 512], tmp[:], ALU.add), [x, tmp], [x])
            if last:
                if i >= 2:
                    K.final_ops.append(P.dma("sync", K.out[(i - 2) * 128:(i - 1) * 128, :], x[:], reads=[x]))
            else:
                P.dma("sync", K.X[i * 128:(i + 1) * 128, :], x[:], reads=[x])
    P.reset(m0)


def rope_tables():
    n_freq = 8
    inv = 10000.0 ** (-np.arange(n_freq, dtype=np.float64) / n_freq)
    t = np.arange(TL)
    ang_r = (t // 64)[:, None] * inv
    ang_c = (t % 64)[:, None] * inv
    C = np.ones((TT, 32), np.float64)
    S = np.zeros((TT, 32), np.float64)
    C[TC:, 0:8] = np.cos(ang_r); C[TC:, 8:16] = np.cos(ang_r)
    C[TC:, 16:24] = np.cos(ang_c); C[TC:, 24:32] = np.cos(ang_c)
    S[TC:, 0:8] = -np.sin(ang_r); S[TC:, 8:16] = np.sin(ang_r)
    S[TC:, 16:24] = -np.sin(ang_c); S[TC:, 24:32] = np.sin(ang_c)
    return C.astype(np.float32), S.astype(np.float32)


def phase_attn(K, l):
    P = K.P
    m0 = P.mark()
    lam_init = 0.8 - 0.6 * math.exp(-0.3 * l)
    qT = P.tile("qT", [4, TT], BF16, parts=64)
    kT = P.tile("kT", [4, TT], BF16, parts=64)
    vaug = P.tile("vaug", [NT, 4, 65], BF16)
    nw = P.tile("nwqk", [16, 32])
    subw = P.tile("subw", [64])
    lamt = P.tile("lamt", [128])
    lamv = P.tile("lamv", [4])
    for g in range(16):
        src = K.da_q_norm[l] if g < 8 else K.da_k_norm[l]
        P.dma("sync", nw[:, g, :], src.partition_broadcast(128), writes=[nw])
    P.dma("sync", subw[:], K.da_subln[l].partition_broadcast(128), writes=[subw])
    P.dma("sync", lamt[:], K.da_lam[l].rearrange("a b -> (a b)").partition_broadcast(128), writes=[lamt])
    V(P, lambda e: e.memset(vaug[:], 1.0), [], [vaug])
    V(P, lambda e: e.tensor_tensor(lamt[:, 0:32], lamt[:, 0:32], lamt[:, 32:64], ALU.mult), [lamt], [lamt])
    V(P, lambda e: e.tensor_tensor(lamt[:, 64:96], lamt[:, 64:96], lamt[:, 96:128], ALU.mult), [lamt], [lamt])
    V(P, lambda e: e.tensor_reduce(lamv[:, 0:1], lamt[:, 0:32], AX.X, ALU.add), [lamt], [lamv])
    V(P, lambda e: e.tensor_reduce(lamv[:, 1:2], lamt[:, 64:96], AX.X, ALU.add), [lamt], [lamv])
    A_(P, lambda e: e.activation(lamv[:, 0:2], lamv[:, 0:2], AF.Exp), [lamv], [lamv])
    V(P, lambda e: e.tensor_tensor(lamv[:, 2:3], lamv[:, 1:2], lamv[:, 0:1], ALU.subtract), [lamv], [lamv])
    V(P, lambda e: e.tensor_scalar(lamv[:, 2:3], lamv[:, 2:3], -lam_init, 1.0, ALU.add, ALU.mult), [lamv], [lamv])
    qk = [P.tile("qk%d" % b, [512]) for b in range(2)]
    vt = [P.tile("vt%d" % b, [256]) for b in range(2)]
    rc = [P.tile("rc%d" % b, [32]) for b in range(2)]
    rsn = [P.tile("rsn%d" % b, [32]) for b in range(2)]
    sq = P.tile("asq", [512]); ss = P.tile("ass", [16]); t1 = P.tile("at1", [512]); t2 = P.tile("at2", [512])
    qkb = P.tile("qkb", [512], BF16)
    pa, pb = K.ps[6], K.ps[7]
    pav = pa.ap[:, 0:256].bitcast(BF16).rearrange("p (c t) -> p c t", c=4)
    pbv = pb.ap[:, 0:256].bitcast(BF16).rearrange("p (c t) -> p c t", c=4)

    def loads(i):
        b = i % 2
        rs = slice(i * 128, (i + 1) * 128)
        P.dma("sync", qk[b][:], K.Pj[rs, C_DA:C_DA + 512], writes=[qk[b]])
        P.dma("sync", vt[b][:], K.Pj[rs, C_DA + 512:C_DA + 768], writes=[vt[b]])
        P.dma("sync", rc[b][:], K.cin["ropeC"][rs, :], writes=[rc[b]])
        P.dma("sync", rsn[b][:], K.cin["ropeS"][rs, :], writes=[rsn[b]])
    loads(0)
    for i in range(NT):
        b = i % 2
        if i + 1 < NT:
            loads(i + 1)
        x = qk[b]
        G_(P, lambda e, x=x: e.tensor_tensor(sq[:], x[:], x[:], ALU.mult), [x], [sq])
        V(P, lambda e: e.tensor_reduce(ss[:], sq[:].rearrange("p (g d) -> p g d", g=16), AX.X, ALU.add), [sq], [ss])
        A_(P, lambda e: e.activation(ss[:], ss[:], AF.Sqrt, bias=K.epsc[:, 0:1], scale=1.0 / 32), [ss, K.epsc], [ss])
        V(P, lambda e: e.reciprocal(ss[:], ss[:]), [ss], [ss])
        x3 = x[:].rearrange("p (g d) -> p g d", g=16)
        V(P, lambda e, x3=x3: e.tensor_tensor(x3, x3, ss[:].unsqueeze(2).broadcast_to([128, 16, 32]), ALU.mult), [x, ss], [x])
        G_(P, lambda e, x3=x3: e.tensor_tensor(x3, x3, nw[:], ALU.mult), [x, nw], [x])
        cb = rc[b]; sb = rsn[b]
        V(P, lambda e, x3=x3, cb=cb: e.tensor_tensor(t1[:].rearrange("p (g d) -> p g d", g=16), x3, cb[:].unsqueeze(1).broadcast_to([128, 16, 32]), ALU.mult), [x, cb], [t1])
        x5 = x[:].rearrange("p (g r h e) -> p g r h e", g=16, r=2, h=2)
        t5 = t2[:].rearrange("p (g r h e) -> p g r h e", g=16, r=2, h=2)
        s4 = sb[:].rearrange("p (r h e) -> p r h e", r=2, h=2)
        for h in range(2):
            G_(P, lambda e, h=h, x5=x5, t5=t5, s4=s4: e.tensor_tensor(t5[:, :, :, h, :], x5[:, :, :, 1 - h, :],
                                                                   s4[:, :, h, :].unsqueeze(1).broadcast_to([128, 16, 2, 8]), ALU.mult), [x, sb], [t2])
        V(P, lambda e: e.tensor_tensor(qkb[:], t1[:], t2[:], ALU.add), [t1, t2], [qkb])
        for h in range(4):
            TR(P, pa, pav[0:64, h, :], qkb[:, h * 64:(h + 1) * 64], K.identb[:], [qkb, K.identb])
        for h in range(4):
            TR(P, pb, pbv[0:64, h, :], qkb[:, 256 + h * 64:256 + (h + 1) * 64], K.identb[:], [qkb, K.identb])
        V(P, lambda e, i=i: e.tensor_copy(qT[:, :, i * 128:(i + 1) * 128], pav[0:64, :, :]), [pa], [qT])
        A_(P, lambda e, i=i: e.copy(kT[:, :, i * 128:(i + 1) * 128], pbv[0:64, :, :]), [pb], [kT])
        G_(P, lambda e, i=i, b=b: e.tensor_copy(vaug[:, i, :, 0:64], vt[b][:].rearrange("p (h e) -> p h e", h=4)), [vt[b]], [vaug])
    scale = 32 ** -0.5
    pT = [P.tile("pT%d" % b, [512], BF16) for b in range(3)]
    osb = [P.tile("osb%d" % j, [512], parts=65) for j in range(2)]
    obuf = P.tile("obuf", [4, 256])
    o1t = P.tile("o1t", [64]); rz = P.tile("rz", [2])
    asq = P.tile("bsq", [256]); ass = P.tile("bss", [4])
    blocks = [(0, 256, [0, 1])] + [(TC + qb * 512, 512, list(range(NT))) for qb in range(TL // 512)]
    NSB = 4
    pT = pT + [P.tile("pT3", [512], BF16)]
    sbanks = [K.ps[2], K.ps[3], K.ps[4], K.ps[5]]
    tp = K.ps[6]
    accs = [[K.ps[0], K.ps[1]], [K.ps[7], K.ps[1]]]
    items = []
    for (q0, nq, kts) in blocks:
        for h in range(4):
            for j in range(2):
                for ki, kt in enumerate(kts):
                    items.append((q0, nq, kts, h, j, ki, kt))
    LA = 3

    def emit_score(n):
        (q0, nq, kts, h, j, ki, kt) = items[n]
        sp = sbanks[n % NSB]
        MM(P, sp, sp[:, 0:nq], kT[32 * j:32 * j + 32, h, kt * 128:(kt + 1) * 128], qT[32 * j:32 * j + 32, h, q0:q0 + nq], [kT, qT])

    def emit_rest(n):
        (q0, nq, kts, h, j, ki, kt) = items[n]
        nqt = nq // 128
        sp = sbanks[n % NSB]
        pt_ = pT[n % NSB]
        acc = [K.ps[0], K.ps[1]]
        A_(P, lambda e, sp=sp, pt_=pt_, nq=nq: e.activation(pt_[:, 0:nq], sp[:, 0:nq], AF.Exp, scale=scale), [sp], [pt_])
        MM(P, acc[j], acc[j][0:65, 0:nq], vaug[:, kt, h, :], pt_[:, 0:nq], [vaug, pt_], start=(ki == 0), stop=(ki == len(kts) - 1))
        if ki != len(kts) - 1:
            return
        V(P, lambda e, j=j, nq=nq, a=acc[j]: e.tensor_copy(osb[j][:, 0:nq], a[0:65, 0:nq]), [acc[j]], [osb[j]])
        if j != 1:
            return
        for qt in range(nqt):
            for jj in range(2):
                TR(P, tp, tp[:, jj * 128:jj * 128 + 65], osb[jj][:, qt * 128:(qt + 1) * 128], K.ident[0:65, 0:65], [osb[jj], K.ident])
            V(P, lambda e: e.reciprocal(rz[:].rearrange("p (a b) -> p a b", a=2), tp[:, 0:256].rearrange("p (a b) -> p a b", a=2)[:, :, 64:65]), [tp], [rz])
            V(P, lambda e: e.tensor_tensor(rz[:, 1:2], rz[:, 1:2], lamv[:, 2:3], ALU.mult), [rz, lamv], [rz])
            V(P, lambda e, qt=qt, h=h: e.tensor_scalar(obuf[:, qt, h * 64:(h + 1) * 64], tp[:, 0:64], rz[:, 0:1], 1.0, ALU.mult, ALU.mult), [tp, rz], [obuf])
            V(P, lambda e: e.tensor_scalar(o1t[:], tp[:, 128:192], rz[:, 1:2], 1.0, ALU.mult, ALU.mult), [tp, rz], [o1t])
            G_(P, lambda e, qt=qt, h=h: e.tensor_tensor(obuf[:, qt, h * 64:(h + 1) * 64], obuf[:, qt, h * 64:(h + 1) * 64], o1t[:], ALU.add), [obuf, o1t], [obuf])
        if h != 3:
            return
        for qt in range(nqt):
            ov = T(obuf.name, obuf[:, qt, :])
            head_rms_gate(K, ov, None, subw, asq, ass, 4, 64, 1.0 - lam_init)
            r0 = q0 + qt * 128
            P.dma("sync", K.O[r0:r0 + 128, 768:1024], obuf[:, qt, :], reads=[obuf])

    for idx in range(len(items) + LA):
        if idx < len(items):
            emit_score(idx)
        if idx >= LA:
            emit_rest(idx - LA)
    P.reset(m0)


EXTRA_D = {"attn": phase_attn}


HQW = 1536


def hg_consts(c):
    j = np.arange(128)[:, None]
    i = np.arange(128)[None, :]
    same = (j // 64) == (i // 64)
    jl = j % 64
    c["mrel_f"] = (same * ((j <= i).astype(np.float32) - (jl <= 31).astype(np.float32))).astype(np.float32)
    c["mrel_b"] = (same * ((j >= i).astype(np.float32) - (jl >= 32).astype(np.float32))).astype(np.float32)


def phase_hg_pre(K, l):
    P = K.P
    m0 = P.mark()
    raw = P.tile("lbraw", [4, 256])
    P.dma("sync", raw[:], K.hg_lb_raw.rearrange("a b -> (a b)").partition_broadcast(128), writes=[raw])
    lb = P.tile("lb", [256]); oml = P.tile("oml", [256]); den = P.tile("lbden", [256])
    A_(P, lambda e: e.activation(raw[:], raw[:], AF.Exp), [raw], [raw])
    V(P, lambda e: e.tensor_tensor(den[:], raw[:, 0, :], raw[:, 1, :], ALU.add), [raw], [den])
    V(P, lambda e: e.tensor_tensor(den[:], den[:], raw[:, 2, :], ALU.add), [raw, den], [den])
    V(P, lambda e: e.tensor_tensor(den[:], den[:], raw[:, 3, :], ALU.add), [raw, den], [den])
    V(P, lambda e: e.reciprocal(den[:], den[:]), [den], [den])
    V(P, lambda e: e.memset(lb[:], 0.0), [], [lb])
    for ll in range(1, l + 1):
        V(P, lambda e, ll=ll: e.tensor_tensor(lb[:], lb[:], raw[:, ll, :], ALU.add), [lb, raw], [lb])
    V(P, lambda e: e.tensor_tensor(lb[:], lb[:], den[:], ALU.mult), [lb, den], [lb])
    V(P, lambda e: e.tensor_scalar(oml[:], lb[:], -1.0, 1.0, ALU.mult, ALU.add), [lb], [oml])
    xin = [P.tile("hgx%d" % b, [1024]) for b in range(2)]
    ho = [P.tile("hgo%d" % b, [HQW]) for b in range(2)]
    sg = P.tile("hgs", [512]); tt = P.tile("hgt", [512])

    def loads(i):
        b = i % 2
        P.dma("sync", xin[b][:], K.Pj[i * 128:(i + 1) * 128, C_HG:C_HG + 1024], writes=[xin[b]])
    loads(0)
    for i in range(NT):
        b = i % 2
        if i + 1 < NT:
            loads(i + 1)
        x = xin[b]; o = ho[b]
        A_(P, lambda e, x=x, o=o: e.activation(o[:, 0:256], x[:, 0:256], AF.Silu), [x], [o])
        G_(P, lambda e, x=x, o=o: e.tensor_copy(o[:, 256:512], x[:, 256:512]), [x], [o])
        A_(P, lambda e, x=x: e.activation(sg[:], x[:, 512:1024], AF.Sigmoid), [x], [sg])
        s3 = sg[:].rearrange("p (d c) -> p d c", d=2)
        t3 = tt[:].rearrange("p (d c) -> p d c", d=2)
        V(P, lambda e, s3=s3, t3=t3: e.tensor_tensor(t3, s3, oml[:].unsqueeze(1).broadcast_to([128, 2, 256]), ALU.mult), [sg, oml], [tt])
        o4 = o[:, 512:1536].rearrange("p (d k c) -> p d k c", d=2, k=2)
        G_(P, lambda e, s3=s3, t3=t3: e.tensor_tensor(s3, t3, lb[:].unsqueeze(1).broadcast_to([128, 2, 256]), ALU.add), [tt, lb], [sg])
        A_(P, lambda e, o4=o4, s3=s3, o=o: e.activation(o4[:, :, 0, :], s3, AF.Ln), [sg], [o])
        V(P, lambda e, o4=o4, t3=t3, o=o: e.scalar_tensor_tensor(o4[:, :, 1, :], t3, -1.0, oml[:].unsqueeze(1).broadcast_to([128, 2, 256]), ALU.mult, ALU.add), [tt, oml], [o])
        P.dma("sync", K.HQ[i * 128:(i + 1) * 128, :], o[:], reads=[o])
    P.reset(m0)


def hg_scan_gen(K, l, d, nps):
    P = K.P
    tri = P.tile("htri", [128]); tail = P.tile("htail", [128]); mrel = P.tile("hmrel", [128])
    trim = P.tile("htrim", [128], U8)
    P.dma("sync", tri[:], K.cin["tri_f" if d == 0 else "tri_b"], writes=[tri])
    P.dma("sync", tail[:], K.cin["tri_bs" if d == 0 else "tri_fs"], writes=[tail])
    P.dma("sync", mrel[:], K.cin["mrel_f" if d == 0 else "mrel_b"], writes=[mrel])
    V(P, lambda e: e.tensor_copy(trim[:], tri[:]), [tri], [trim])
    zeros = P.tile("hzeros", [128])
    V(P, lambda e: e.memset(zeros[:], 0.0), [], [zeros])
    ident, ones = K.ident, K.ones
    S = [P.tile("hS%d" % p, [64]) for p in range(2)]
    for p in range(2):
        V(P, lambda e, p=p: e.memset(S[p][:], 0.0), [], [S[p]])
    qv = [P.tile("hqv%d" % b, [2, 2, 64]) for b in range(2)]
    fk = [P.tile("hfk%d" % b, [2, 2, 64]) for b in range(2)]
    ot = [P.tile("hot%d" % b, [2, 64]) for b in range(2)]
    ex = [P.tile("hex%d" % k, [128]) for k in range(4)]
    qe = P.tile("hqe", [2, 64]); ke = P.tile("hke", [2, 64])

    def pt(name, free):
        return [P.tile("%s%d" % (name, p), free) for p in range(2)]
    keT = pt("hkeT", [128]); qeT = pt("hqeT", [128]); aT = pt("haT", [128]); qgb = pt("hqgb", [128]); kendb = pt("hkendb", [128])
    lfb = pt("hlfb", [128]); qgT = pt("hqgT", [128]); ege = pt("hege", [1])
    for p in range(2):
        for t in (qgb[p], kendb[p], lfb[p]):
            G_(P, lambda e, t=t: e.memset(t[:], 0.0), [], [t])
    ctx_ch = list(range(0, TC // 64))
    lat_ch = list(range(TC // 64, TT // 64))
    order = ctx_ch + lat_ch if d == 0 else ctx_ch[::-1] + lat_ch[::-1]
    c0 = 512 + d * 512

    def loads(ci):
        c = order[ci]
        b = ci % 2
        r0 = c * 64
        for hh in range(2):
            src = K.HQ[r0:r0 + 64, 0:512].rearrange("r (t pr hh e) -> r t pr hh e", t=2, pr=2, hh=2)[:, :, :, hh, :]
            P.dma("sync", qv[b][hh * 64:(hh + 1) * 64, :, :, :], src, writes=[qv[b]])
            src = K.HQ[r0:r0 + 64, c0:c0 + 512].rearrange("r (t pr hh e) -> r t pr hh e", t=2, pr=2, hh=2)[:, :, :, hh, :]
            P.dma("sync", fk[b][hh * 64:(hh + 1) * 64, :, :, :], src, writes=[fk[b]])
    loads(0)
    for ci in range(len(order)):
        c = order[ci]
        b = ci % 2
        if ci + 1 < len(order):
            loads(ci + 1)
        Q = qv[b]; F = fk[b]
        lf3 = F[:, 0, :, :].rearrange("p a b -> p (a b)")
        mats = (mrel, None, tri, tail)
        pss = []
        for k, M in enumerate(mats):
            if M is None:
                pss.append(None)
                continue
            ps = nps()
            MM(P, ps, ps[:, 0:128], M[:], lf3, [M, F])
            pss.append(ps)
        yield
        A_(P, lambda e, ps=pss[0]: e.activation(ex[0][:], ps[:, 0:128], AF.Exp), [pss[0]], [ex[0]])
        A_(P, lambda e, ps=pss[0]: e.activation(ex[1][:], ps[:, 0:128], AF.Exp, scale=-1.0), [pss[0]], [ex[1]])
        A_(P, lambda e, ps=pss[2]: e.activation(ex[2][:], ps[:, 0:128], AF.Exp), [pss[2]], [ex[2]])
        A_(P, lambda e, ps=pss[3]: e.activation(ex[3][:], ps[:, 0:128], AF.Exp), [pss[3]], [ex[3]])
        yield
        V(P, lambda e, Q=Q: e.tensor_tensor(qe[:], Q[:, 0, :, :], ex[0][:].rearrange("p (a b) -> p a b", a=2), ALU.mult), [Q, ex[0]], [qe])
        G_(P, lambda e, F=F: e.tensor_tensor(ke[:], F[:, 1, :, :], ex[1][:].rearrange("p (a b) -> p a b", a=2), ALU.mult), [F, ex[1]], [ke])
        yield

        def pair_gen(p, Q=Q, F=F, b=b):
            vv = Q[:, 1, p, :]
            for hh in range(2):
                r = slice(hh * 64, hh * 64 + 64)
                V(P, lambda e, p=p, r=r, Q=Q: e.tensor_tensor(qgb[p][r, r], Q[r, 0, p, :], ex[2][r, p * 64:(p + 1) * 64], ALU.mult), [Q, ex[2]], [qgb[p]])
                G_(P, lambda e, p=p, r=r, F=F: e.tensor_tensor(kendb[p][r, r], F[r, 1, p, :], ex[3][r, p * 64:(p + 1) * 64], ALU.mult), [F, ex[3]], [kendb[p]])
                G_(P, lambda e, p=p, r=r, F=F: e.tensor_copy(lfb[p][r, r], F[r, 0, p, :]), [F], [lfb[p]])
            yield
            ps = nps(); ps2 = nps()
            TR(P, ps, ps[0:64, 0:128], ke[:, p, :], ident[:], [ke, ident])
            TR(P, ps2, ps2[0:64, 0:128], qe[:, p, :], ident[:], [qe, ident])
            yield
            A_(P, lambda e, p=p, ps=ps: e.copy(keT[p][0:64, :], ps[0:64, 0:128]), [ps], [keT[p]])
            V(P, lambda e, p=p, ps=ps2: e.tensor_copy(qeT[p][0:64, :], ps[0:64, 0:128]), [ps2], [qeT[p]])
            yield
            ps = nps(); ps2 = nps()
            MM(P, ps, ps[:, 0:128], keT[p][0:64, :], qeT[p][0:64, :], [keT[p], qeT[p]])
            TR(P, ps2, ps2[:, 0:128], qgb[p][:], ident[:], [qgb[p], ident])
            MM(P, ps2, ps2[:, 128:129], lfb[p][:], ones[:, 0:1], [lfb[p], ones])
            yield
            V(P, lambda e, p=p, ps=ps: e.select(aT[p][:], trim[:], ps[:, 0:128], zeros[:]), [ps, trim, zeros], [aT[p]])
            A_(P, lambda e, p=p, ps=ps2: e.copy(qgT[p][:], ps[:, 0:128]), [ps2], [qgT[p]])
            A_(P, lambda e, p=p, ps=ps2: e.activation(ege[p][:], ps[:, 128:129], AF.Exp), [ps2], [ege[p]])
            yield
            ps = nps(); ps2 = nps()
            MM(P, ps, ps[:, 0:64], qgT[p][:], S[p][:], [qgT[p], S[p]], start=True, stop=False)
            MM(P, ps, ps[:, 0:64], aT[p][:], vv, [aT[p], Q], start=False, stop=True)
            MM(P, ps2, ps2[:, 0:64], kendb[p][:], vv, [kendb[p], Q])
            yield
            A_(P, lambda e, p=p, ps=ps, b=b: e.copy(ot[b][:, p, :], ps[:, 0:64]), [ps], [ot[b]])
            V(P, lambda e, p=p, ps=ps2: e.scalar_tensor_tensor(S[p][:], S[p][:], ege[p][:, 0:1], ps[:, 0:64], ALU.mult, ALU.add), [S[p], ege[p], ps2], [S[p]])
        pg = [pair_gen(0), pair_gen(1)]
        while pg:
            for gg_ in pg[:]:
                try:
                    next(gg_)
                except StopIteration:
                    pg.remove(gg_)
            yield
        r0 = c * 64
        for hh in range(2):
            dst = K.OG[d][r0:r0 + 64, :].rearrange("r (pr hh e) -> r pr hh e", pr=2, hh=2)[:, :, hh, :]
            P.dma("sync", dst, ot[b][hh * 64:(hh + 1) * 64, :, :], reads=[ot[b]])


def phase_hg_scan_both(K, l):
    P = K.P
    m0 = P.mark()
    psn = [0]

    def nps():
        psn[0] = (psn[0] + 1) % 8
        return K.ps[psn[0]]
    lockstep([hg_scan_gen(K, l, 0, nps), hg_scan_gen(K, l, 1, nps)])
    P.reset(m0)


def phase_dir_fin(K, l, nw_ap, gate_c0, out_c0):
    P = K.P
    m0 = P.mark()
    nw = P.tile("fnw", [64])
    P.dma("sync", nw[:], nw_ap.partition_broadcast(128), writes=[nw])
    o0 = [P.tile("fo0_%d" % b, [256]) for b in range(2)]
    o1 = [P.tile("fo1_%d" % b, [256]) for b in range(2)]
    zt = [P.tile("fzt%d" % b, [256]) for b in range(2)]
    sq = P.tile("fsq", [256]); ss = P.tile("fss", [4])

    def loads(i):
        b = i % 2
        rs = slice(i * 128, (i + 1) * 128)
        P.dma("sync", o0[b][:], K.OG[0][rs, :], writes=[o0[b]])
        P.dma("sync", o1[b][:], K.OG[1][rs, :], writes=[o1[b]])
        P.dma("sync", zt[b][:], K.Pj[rs, gate_c0:gate_c0 + 256], writes=[zt[b]])
    loads(0)
    for i in range(NT):
        b = i % 2
        if i + 1 < NT:
            loads(i + 1)
        o = o0[b]
        V(P, lambda e, o=o, b=b: e.tensor_tensor(o[:], o[:], o1[b][:], ALU.add), [o, o1[b]], [o])
        head_rms_gate(K, o, zt[b], nw, sq, ss, 4, 64, 1.0)
        P.dma("sync", K.O[i * 128:(i + 1) * 128, out_c0:out_c0 + 256], o[:], reads=[o])
    P.reset(m0)


def phase_hg(K, l):
    phase_hg_pre(K, l)
    phase_hg_scan_both(K, l)
    phase_dir_fin(K, l, K.hg_norm_w[l], C_HG + 1024, 512)


EXTRA_E = {"hg": phase_hg, "hgpre": phase_hg_pre}


HY_STREAMS = ((TC, 0), (TL, TC))


def hy_consts(c):
    for (n, _) in HY_STREAMS:
        N = 2 * n
        N1 = N // 128
        rows = np.arange(N)
        lag = np.where(rows < n, rows, N - rows).astype(np.float64)
        valid = (rows != n).astype(np.float64)
        t01 = lag / max(n - 1, 1)
        bands = np.linspace(1e-4, 15, 16)
        ang = (2.0 * np.pi / n) * lag[:, None] * bands
        z = np.concatenate([t01[:, None], np.cos(ang), -np.sin(ang)], axis=-1)
        c["hy_zT%d" % n] = np.ascontiguousarray(z.T).astype(np.float32)
        c["hy_t01_%d" % n] = np.ascontiguousarray((-t01).reshape(N1, 128).T).astype(np.float32)
        c["hy_val%d" % n] = np.ascontiguousarray(valid.reshape(N1, 128).T).astype(np.float32)
        n1 = np.arange(N1)[:, None]; k1 = np.arange(N1)[None, :]
        th = 2 * np.pi * n1 * k1 / N1
        c["hy_wf1_%d" % n] = np.concatenate([np.cos(th), -np.sin(th)], axis=1).astype(np.float32)
        c["hy_cf%d" % n] = (np.cos(th).T[:, :N1 // 2] / N).astype(np.float32)
        c["hy_nsf%d" % n] = (-np.sin(th).T[:, :N1 // 2] / N).astype(np.float32)
        n2 = np.arange(128)[None, :, None]; k2 = np.arange(128)[None, None, :]; kk1 = np.arange(N1)[:, None, None]
        th2 = 2 * np.pi * n2 * (kk1 + N1 * k2) / N
        c["hy_c2_%d" % n] = np.cos(th2).astype(np.float32)
        c["hy_s2_%d" % n] = np.sin(th2).astype(np.float32)
        c["hy_c2t_%d" % n] = np.ascontiguousarray(np.cos(th2).transpose(0, 2, 1)).astype(np.float32)
        c["hy_s2t_%d" % n] = np.ascontiguousarray(np.sin(th2).transpose(0, 2, 1)).astype(np.float32)


def phase_hy_pre(K, l):
    P = K.P
    m0 = P.mark()
    cw = P.tile("hcw", [3, 768])
    for k in range(3):
        P.dma("sync", cw[:, k, :], K.hy_conv_w[l][k].partition_broadcast(128), writes=[cw])
    x3 = [[P.tile("hx3_%d_%d" % (b, k), [768]) for k in range(3)] for b in range(2)]
    acc = [P.tile("hacc%d" % b, [768]) for b in range(2)]
    t0 = P.tile("ht0", [768]); t2 = P.tile("ht2", [768])
    load_shift3(K, "sync", x3[0], K.Pj, 0, C_HY, C_HY + 768)
    for i in range(NT):
        b = i % 2
        if i + 1 < NT:
            load_shift3(K, "sync", x3[1 - b], K.Pj, i + 1, C_HY, C_HY + 768)
        conv3(K, x3[b], cw, acc[b], t0, t2, 768)
        P.dma("sync", K.HC[i * 128:(i + 1) * 128, :], acc[b][:], reads=[acc[b]])
    P.reset(m0)


def hy_filters(K, l, n):
    P = K.P
    N = 2 * n
    N1 = N // 128
    m0 = P.mark()
    w1 = P.tile("hw1", [64], parts=33); w2 = P.tile("hw2", [64], parts=64); w3 = P.tile("hw3", [1024], parts=64)
    P.dma("sync", w1[:], K.hy_w1[l], writes=[w1]); P.dma("sync", w2[:], K.hy_w2[l], writes=[w2]); P.dma("sync", w3[:], K.hy_w3[l], writes=[w3])
    pv = P.tile("hpv", [4], parts=64)
    for k, src in enumerate((K.hy_f1, K.hy_b1, K.hy_f2, K.hy_b2)):
        P.dma("sync", pv[:, k:k + 1], src[l].rearrange("(a b) -> a b", b=1), writes=[pv])
    sc = P.tile("hsc", [8], parts=64)
    for (o, fi, bi) in ((0, 0, 1), (3, 2, 3)):
        V(P, lambda e, o=o, fi=fi: e.tensor_scalar(sc[:, o:o + 1], pv[:, fi:fi + 1], 0.25, 1.0, ALU.mult, ALU.mult), [pv], [sc])
        V(P, lambda e, o=o, fi=fi, bi=bi: e.tensor_tensor(sc[:, o + 1:o + 2], sc[:, o:o + 1], pv[:, bi:bi + 1], ALU.mult), [pv, sc], [sc])
        V(P, lambda e, o=o: e.tensor_scalar(sc[:, o + 2:o + 3], sc[:, o + 1:o + 2], math.pi / 2, 1.0, ALU.add, ALU.mult), [sc], [sc])
    nad = P.tile("hnad", [512])
    P.dma("sync", nad[:], K.hy_decay[l].rearrange("a b -> (a b)").partition_broadcast(128), writes=[nad])
    tneg = P.tile("htneg", [512])
    V(P, lambda e: e.tensor_scalar(tneg[:], nad[:], -1.0, 1.0, ALU.mult, ALU.mult), [nad], [tneg])
    V(P, lambda e: e.tensor_tensor(nad[:], nad[:], tneg[:], ALU.max), [nad, tneg], [nad])
    t01 = P.tile("ht01", [N1]); val = P.tile("hval", [N1])
    P.dma("sync", t01[:], K.cin["hy_t01_%d" % n], writes=[t01]); P.dma("sync", val[:], K.cin["hy_val%d" % n], writes=[val])
    GW = 512
    NG = N // GW
    zT = [P.tile("hzT%d" % b, [GW], parts=33) for b in range(2)]
    sTt = [P.tile("hsT%d" % b, [GW], parts=64) for b in range(2)]
    cTt = [P.tile("hcT%d" % b, [GW], parts=64) for b in range(2)]
    hht = [P.tile("hhh%d" % b, [GW], parts=64) for b in range(2)]
    h2t = [P.tile("hh2%d" % b, [GW], parts=64) for b in range(2)]
    win = [P.tile("hwin%d" % b, [512]) for b in range(2)]
    ko = [P.tile("hko%d" % b, [512]) for b in range(2)]

    def sin_layer(ps, o, out, sT, cT):
        A_(P, lambda e: e.activation(sT[:], ps[0:64, 0:GW], AF.Sin, bias=sc[:, o + 1:o + 2], scale=sc[:, o:o + 1]), [ps, sc], [sT])
        A_(P, lambda e: e.activation(cT[:], ps[0:64, 0:GW], AF.Sin, bias=sc[:, o + 2:o + 3], scale=sc[:, o:o + 1]), [ps, sc], [cT])
        V(P, lambda e: e.tensor_tensor(cT[:], cT[:], sT[:], ALU.mult), [cT, sT], [cT])
        G_(P, lambda e: e.tensor_tensor(sT[:], sT[:], sT[:], ALU.mult), [sT], [sT])
        G_(P, lambda e: e.tensor_scalar(sT[:], sT[:], -2.0, 1.0, ALU.mult, ALU.add), [sT], [sT])
        V(P, lambda e: e.scalar_tensor_tensor(out[:], cT[:], 4.0, sT[:], ALU.mult, ALU.mult), [cT, sT], [out])

    P.dma("sync", zT[0][:], K.cin["hy_zT%d" % n][:, 0:GW], writes=[zT[0]])
    for gi in range(NG):
        b = gi % 2
        if gi + 1 < NG:
            P.dma("sync", zT[1 - b][:], K.cin["hy_zT%d" % n][:, (gi + 1) * GW:(gi + 2) * GW], writes=[zT[1 - b]])
        ps = K.ps[b]
        MM(P, ps, ps[0:64, 0:GW], w1[:], zT[b][:], [w1, zT[b]])
        sin_layer(ps, 0, hht[b], sTt[b], cTt[b])
        ps = K.ps[2 + b]
        MM(P, ps, ps[0:64, 0:GW], w2[:], hht[b][:], [w2, hht[b]])
        sin_layer(ps, 3, h2t[b], sTt[b], cTt[b])
        for r4 in range(GW // 128):
            rt = gi * (GW // 128) + r4
            side = 0 if rt < N1 // 2 else 1
            bb = rt % 2
            ps = K.ps[4 + rt % 4]
            MM(P, ps, ps[:, 0:512], h2t[b][:, r4 * 128:(r4 + 1) * 128], w3[:, side * 512:(side + 1) * 512], [h2t[b], w3])
            A_(P, lambda e, rt=rt, bb=bb: e.activation(win[bb][:], nad[:], AF.Exp, scale=t01[:, rt:rt + 1]), [nad, t01], [win[bb]])
            V(P, lambda e, ps=ps, rt=rt, bb=bb: e.scalar_tensor_tensor(ko[bb][:], ps[:, 0:512], val[:, rt:rt + 1], win[bb][:], ALU.mult, ALU.mult), [ps, val, win[bb]], [ko[bb]])
            P.dma("sync", K.KERN[rt * 128:(rt + 1) * 128, :], ko[bb][:], reads=[ko[bb]])
    P.reset(m0)
    hy_dft_fwd(K, n, K.KERN, 512, N1, kernel=True)


def hy_views(K, n, ncol):
    N1 = 2 * n // 128
    A = K.Aflat[0:2 * N1 * 128 * ncol].rearrange("(a b c) -> a b c", a=2 * N1, b=128)
    return A


def hy_dft_fwd(K, n, src, ncol, nrows1, kernel=False, filt=0, dst_rows=None, skip_ap=None, gate_c0=None, u_c0=None, out_ap=None, out_c0=0, row0=0):
    P = K.P
    N = 2 * n
    N1 = N // 128
    A = hy_views(K, n, ncol)
    m0 = P.mark()
    wf1 = P.tile("hwf1", [2 * N1], parts=N1)
    P.dma("sync", wf1[:], K.cin["hy_wf1_%d" % n], writes=[wf1])
    CH = 2048 // ncol * 1
    CH = max(1, 2048 // ncol)
    xin = [P.tile("hxin%d" % b, [CH * ncol], parts=nrows1) for b in range(2)]
    ao = [P.tile("hao%d" % b, [CH * ncol], parts=2 * N1) for b in range(2)]
    srcv = src[0:nrows1 * 128, :].rearrange("(a b) c -> a b c", b=128)
    nch = 128 // CH
    P.dma("sync", xin[0][:].rearrange("p (b c) -> p b c", b=CH), srcv[:, 0:CH, :], writes=[xin[0]])
    for ch in range(nch):
        b = ch % 2
        if ch + 1 < nch:
            P.dma("sync", xin[1 - b][:].rearrange("p (b c) -> p b c", b=CH), srcv[:, (ch + 1) * CH:(ch + 2) * CH, :], writes=[xin[1 - b]])
        for q in range(CH * ncol // 512):
            ps = K.ps[q % 4]
            MM(P, ps, ps[0:2 * N1, 0:512], wf1[0:nrows1, :], xin[b][:, q * 512:(q + 1) * 512], [wf1, xin[b]])
            if q % 2 == 0:
                A_(P, lambda e, ps=ps, q=q, b=b: e.copy(ao[b][:, q * 512:(q + 1) * 512], ps[0:2 * N1, 0:512]), [ps], [ao[b]])
            else:
                V(P, lambda e, ps=ps, q=q, b=b: e.tensor_copy(ao[b][:, q * 512:(q + 1) * 512], ps[0:2 * N1, 0:512]), [ps], [ao[b]])
        P.dma("sync", A[:, ch * CH:(ch + 1) * CH, :], ao[b][:].rearrange("p (b c) -> p b c", b=CH), reads=[ao[b]])
    P.reset(m0)
    m0 = P.mark()
    NC2 = 2 * ncol
    G = 4
    Av = A.rearrange("(ri k1) n2 c -> k1 n2 ri c", ri=2)
    ntab = 2 if kernel else 4
    r1 = [P.tile("hr1_%d" % b, [2, ncol]) for b in range(2 * G)]
    tb = [[P.tile("htb%d_%d" % (b, k), [128]) for k in range(ntab)] for b in range(2 * G)]
    r2 = [P.tile("hr2_%d" % b, [2, ncol]) for b in range(G)]
    xo = [P.tile("hxo%d" % b, [NC2]) for b in range(G)]
    if not kernel:
        kf = [P.tile("hkf%d" % b, [2, ncol]) for b in range(2 * G)]
        ta = [P.tile("hta%d" % b, [2, ncol]) for b in range(G)]
        tbb = [P.tile("htbb%d" % b, [2, ncol]) for b in range(G)]
        y1 = [P.tile("hy1%d" % b, [2, ncol]) for b in range(G)]
        y2 = [P.tile("hy2%d" % b, [2, ncol]) for b in range(G)]
        Bv = K.Bflat[0:N1 * 128 * 2 * ncol].rearrange("(k1 n2 ri c) -> k1 n2 ri c", k1=N1, n2=128, ri=2)
    tabs = ("hy_c2_%d" % n, "hy_s2_%d" % n, "hy_c2t_%d" % n, "hy_s2t_%d" % n)

    def loads(k1):
        b = k1 % (2 * G)
        P.dma("sync", r1[b][:], Av[k1], writes=[r1[b]])
        for k in range(ntab):
            P.dma("sync", tb[b][k][:], K.cin[tabs[k]][k1], writes=[tb[b][k]])
        if not kernel:
            P.dma("sync", kf[b][:], K.Kf[k1, :, :, filt * 256:(filt + 1) * 256], writes=[kf[b]])

    def chain(k1):
        b = k1 % (2 * G)
        s = k1 % G
        G_(P, lambda e: e.tensor_copy(r2[s][:, 0, :], r1[b][:, 1, :]), [r1[b]], [r2[s]])
        A_(P, lambda e: e.mul(r2[s][:, 1, :], r1[b][:, 0, :], -1.0), [r1[b]], [r2[s]])
        yield
        f1 = r1[b][:].rearrange("p a c -> p (a c)")
        f2 = r2[s][:].rearrange("p a c -> p (a c)")
        nh = NC2 // 512
        pss = []
        for q in range(nh):
            ps = K.ps[(s * nh + q) % 8] if kernel else K.ps[s]
            MM(P, ps, ps[:, 0:512], tb[b][0][:], f1[:, q * 512:(q + 1) * 512], [tb[b][0], r1[b]], start=True, stop=False)
            MM(P, ps, ps[:, 0:512], tb[b][1][:], f2[:, q * 512:(q + 1) * 512], [tb[b][1], r2[s]], start=False, stop=True)
            pss.append(ps)
        yield
        if kernel:
            for q in range(nh):
                ps = pss[q]
                if q % 2 == 0:
                    A_(P, lambda e, ps=ps, q=q: e.copy(xo[s][:, q * 512:(q + 1) * 512], ps[:, 0:512]), [ps], [xo[s]])
                else:
                    V(P, lambda e, ps=ps, q=q: e.tensor_copy(xo[s][:, q * 512:(q + 1) * 512], ps[:, 0:512]), [ps], [xo[s]])
            yield
            P.dma("sync", K.Kf[k1], xo[s][:].rearrange("p (a c) -> p a c", a=2), reads=[xo[s]])
            return
        ps = pss[0]
        X3 = ps[:, 0:512].rearrange("p (a c) -> p a c", a=2)
        V(P, lambda e: e.tensor_tensor(ta[s][:], X3, kf[b][:, 0:1, :].broadcast_to([128, 2, ncol]), ALU.mult), [ps, kf[b]], [ta[s]])
        V(P, lambda e: e.tensor_tensor(tbb[s][:], X3, kf[b][:, 1:2, :].broadcast_to([128, 2, ncol]), ALU.mult), [ps, kf[b]], [tbb[s]])
        yield
        G_(P, lambda e: e.tensor_tensor(y1[s][:, 0, :], ta[s][:, 0, :], tbb[s][:, 1, :], ALU.subtract), [ta[s], tbb[s]], [y1[s]])
        G_(P, lambda e: e.tensor_tensor(y1[s][:, 1, :], tbb[s][:, 0, :], ta[s][:, 1, :], ALU.add), [ta[s], tbb[s]], [y1[s]])
        yield
        A_(P, lambda e: e.mul(y2[s][:, 0, :], y1[s][:, 1, :], -1.0), [y1[s]], [y2[s]])
        A_(P, lambda e: e.copy(y2[s][:, 1, :], y1[s][:, 0, :]), [y1[s]], [y2[s]])
        yield
        ps2 = K.ps[4 + s]
        MM(P, ps2, ps2[:, 0:512], tb[b][2][:], y1[s][:].rearrange("p a c -> p (a c)"), [tb[b][2], y1[s]], start=True, stop=False)
        MM(P, ps2, ps2[:, 0:512], tb[b][3][:], y2[s][:].rearrange("p a c -> p (a c)"), [tb[b][3], y2[s]], start=False, stop=True)
        yield
        A_(P, lambda e: e.copy(xo[s][:], ps2[:, 0:512]), [ps2], [xo[s]])
        yield
        P.dma("sync", Bv[k1], xo[s][:].rearrange("p (a c) -> p a c", a=2), reads=[xo[s]])

    for k1 in range(min(G, N1)):
        loads(k1)
    for g0 in range(0, N1, G):
        for k1 in range(g0 + G, min(g0 + 2 * G, N1)):
            loads(k1)
        lockstep([chain(k1) for k1 in range(g0, min(g0 + G, N1))])
    P.reset(m0)
    if kernel:
        return
    m0 = P.mark()
    H = N1 // 2
    cf = P.tile("hcf", [H], parts=N1); nsf = P.tile("hnsf", [H], parts=N1)
    P.dma("sync", cf[:], K.cin["hy_cf%d" % n], writes=[cf]); P.dma("sync", nsf[:], K.cin["hy_nsf%d" % n], writes=[nsf])
    skp = P.tile("hskp", [256])
    P.dma("sync", skp[:], skip_ap.partition_broadcast(128), writes=[skp])
    CH2 = 4
    br = [P.tile("hbr%d" % b, [CH2, 2, 256], parts=N1) for b in range(2)]
    uu = [P.tile("huu%d" % b, [CH2, 256], parts=H) for b in range(2)]
    gg = [P.tile("hgg%d" % b, [CH2, 256], parts=H) for b in range(2)]
    yo = [P.tile("hyo%d" % b, [CH2, 256], parts=H) for b in range(2)]
    usrc = u_c0[0][row0 if u_c0[2] else 0:(row0 if u_c0[2] else 0) + n, u_c0[1]:u_c0[1] + 256].rearrange("(a b) c -> a b c", b=128)
    gsrc = K.HC[row0:row0 + n, gate_c0:gate_c0 + 256].rearrange("(a b) c -> a b c", b=128)
    dsrc = out_ap[(row0 if out_ap is K.O else 0):(row0 if out_ap is K.O else 0) + n, out_c0:out_c0 + 256].rearrange("(a b) c -> a b c", b=128)
    Bk = Bv.rearrange("k1 n2 ri c -> k1 n2 ri c")

    def loads3(ch):
        b = ch % 2
        P.dma("sync", br[b][:], Bk[:, ch * CH2:(ch + 1) * CH2, :, :], writes=[br[b]])
        P.dma("sync", uu[b][:], usrc[:, ch * CH2:(ch + 1) * CH2, :], writes=[uu[b]])
        P.dma("sync", gg[b][:], gsrc[:, ch * CH2:(ch + 1) * CH2, :], writes=[gg[b]])
    loads3(0)
    nch = 128 // CH2
    for ch in range(nch):
        b = ch % 2
        if ch + 1 < nch:
            loads3(ch + 1)
        for q in range(CH2 // 2):
            ps = K.ps[4 + q % 2]
            o3 = ps[0:H, 0:512].rearrange("p (a c) -> p a c", a=2)
            MM(P, ps, o3, cf[:], br[b][:, 2 * q:2 * q + 2, 0, :], [cf, br[b]], start=True, stop=False)
            MM(P, ps, o3, nsf[:], br[b][:, 2 * q:2 * q + 2, 1, :], [nsf, br[b]], start=False, stop=True)
            us = uu[b][:, 2 * q:2 * q + 2, :]
            V(P, lambda e, us=us: e.tensor_tensor(us, us, skp[0:H, :].unsqueeze(1).broadcast_to([H, 2, 256]), ALU.mult), [uu[b], skp], [uu[b]])
            V(P, lambda e, us=us, o3=o3: e.tensor_tensor(us, us, o3, ALU.add), [uu[b], ps], [uu[b]])
            G_(P, lambda e, us=us, b=b, q=q: e.tensor_tensor(yo[b][:, 2 * q:2 * q + 2, :], us, gg[b][:, 2 * q:2 * q + 2, :], ALU.mult), [uu[b], gg[b]], [yo[b]])
        P.dma("sync", dsrc[:, ch * CH2:(ch + 1) * CH2, :], yo[b][:], reads=[yo[b]])
    P.reset(m0)


def phase_hy(K, l):
    phase_hy_pre(K, l)
    for (n, row0) in HY_STREAMS:
        hy_filters(K, l, n)
        N1 = 2 * n // 128
        hy_dft_fwd(K, n, K.HC[row0:row0 + n, 0:256], 256, N1 // 2, filt=0, skip_ap=K.hy_bias[l][0], gate_c0=256,
                   u_c0=(K.HC, 0, True), out_ap=K.Zd, out_c0=0, row0=row0)
        hy_dft_fwd(K, n, K.Zd[0:n, :], 256, N1 // 2, filt=1, skip_ap=K.hy_bias[l][1], gate_c0=512,
                   u_c0=(K.Zd, 0, False), out_ap=K.O, out_c0=256, row0=row0)


EXTRA_F = {"hy": phase_hy, "hypre": phase_hy_pre}


def build(stop_after=None, dbg=(), pj_input=False, plan=None, ext=()):
    nc = bass.Bass("TRN2", target_bir_lowering=False)
    P = Prog(nc)
    K = Ctx(); K.P = P; K.nc = nc
    def din(name, shape, dt=F32):
        return nc.dram_tensor(name, list(shape), dt, kind="ExternalInput").ap()
    def dscr(name, shape, dt=F32):
        if name in ext:
            return din(name, shape, dt)
        kind = "ExternalOutput" if name in dbg else "Internal"
        return nc.dram_tensor(name, list(shape), dt, kind=kind).ap()
    K.xin = din("xin", [TT, D])
    K.sTin = din("sTin", [128, 8, 2])
    K.mod_w = din("mod_w", [L, D, 6 * D]); K.mod_b = din("mod_b", [L, 6 * D]); K.mod_bT = din("mod_bT", [L, 128, 48])
    K.ln1Tin = din("ln1T", [128, L, 8]); K.ln2Tin = din("ln2T", [128, L, 8])
    K.w_in = din("w_in", [L, D, PIN])
    K.gdn_conv_w = din("gdn_conv_w", [L, 3, 768]); K.gdn_a_log = din("gdn_a_log", [L, 2, 4]); K.gdn_dt_bias = din("gdn_dt_bias", [L, 2, 4])
    K.gdn_norm_w = din("gdn_norm_w", [L, 64])
    K.w_out = din("w_out", [L, D, D]); K.mlp_w1 = din("mlp_w1", [L, D, 4 * D]); K.mlp_w2 = din("mlp_w2", [L, 4 * D, D])
    K.final_ops = []
    K.da_q_norm = din("da_q_norm", [L, 32]); K.da_k_norm = din("da_k_norm", [L, 32]); K.da_lam = din("da_lam", [L, 4, 32]); K.da_subln = din("da_subln", [L, 64])
    hc = host_consts()
    hc["ropeC"], hc["ropeS"] = rope_tables()
    hg_consts(hc)
    hy_consts(hc)
    K.hy_conv_w = din("hy_conv_w", [L, 3, 768]); K.hy_w1 = din("hy_w1", [L, 33, 64]); K.hy_w2 = din("hy_w2", [L, 64, 64]); K.hy_w3 = din("hy_w3", [L, 64, 1024])
    K.hy_b1 = din("hy_b1", [L, 64]); K.hy_f1 = din("hy_f1", [L, 64]); K.hy_b2 = din("hy_b2", [L, 64]); K.hy_f2 = din("hy_f2", [L, 64])
    K.hy_decay = din("hy_decay", [L, 2, 256]); K.hy_bias = din("hy_bias", [L, 2, 256])
    K.HC = dscr("HC", [TT, 768]); K.KERN = dscr("KERN", [8192, 512]); K.Aflat = dscr("Aflat", [128 * 128 * 512]); K.Bflat = dscr("Bflat", [64 * 128 * 2 * 256])
    K.Kf = dscr("Kf", [64, 128, 2, 512]); K.Zd = dscr("Zd", [TL, 256])
    K.hg_lb_raw = din("hg_lb_raw", [L, 256]); K.hg_norm_w = din("hg_norm_w", [L, 64])
    K.HQ = dscr("HQ", [TT, HQW])
    K.cin = {k: din("c_" + k, v.shape) for k, v in hc.items()}
    K.out = nc.dram_tensor("out", [TL, D], F32, kind="ExternalOutput").ap()
    K.X = dscr("X", [TT, D]); K.Pj = din("Pj", [TT, PIN]) if pj_input else dscr("Pj", [TT, PIN])
    K.GQ = dscr("GQ", [TT, GQW]); K.OG = [dscr("OG%d" % d, [TT, 256]) for d in range(2)]; K.O = dscr("O", [TT, D])
    K.dbgfm = None
    if "dbgfm" in dbg:
        K.dbgfm = nc.dram_tensor("dbgfm", [128, 64 + 16 + 16], F32, kind="ExternalOutput").ap()
        K.dbgG = nc.dram_tensor("dbgG", [128, 4096], F32, kind="ExternalOutput").ap()
    K.ps = [P.ps("ps%d" % i, [128, 512]) for i in range(8)]
    K.ident = P.tile("ident", [128]); K.identb = P.tile("identb", [128], BF16)
    K.ones = P.tile("ones", [128])
    K.epsc = P.tile("epsc", [1]); K.onec = P.tile("onec", [1])
    P.op("vector", lambda e: e.memset(K.onec[:], 1.0), writes=[K.onec])
    K.sT = P.tile("sT", [8, 2])
    K.fm = P.tile("fm", [4, 8, 2]); K.A1 = P.tile("A1", [8, 2]); K.A2 = P.tile("A2", [8, 2])
    K.ln1T = P.tile("ln1T", [L, 8]); K.ln2T = P.tile("ln2T", [L, 8])
    K.G = [[P.tile("G%d%d" % (g, s), [D]) for s in range(2)] for g in range(2)]
    P.dma("sync", K.ident[:], K.cin["ident"], writes=[K.ident])
    P.dma("gpsimd", K.identb[:], K.cin["ident"], writes=[K.identb])
    P.dma("sync", K.ones[:], K.cin["ones"], writes=[K.ones])
    P.dma("sync", K.ln1T[:], K.ln1Tin, writes=[K.ln1T]); P.dma("sync", K.ln2T[:], K.ln2Tin, writes=[K.ln2T])
    P.op("vector", lambda e: e.memset(K.epsc[:], EPS), writes=[K.epsc])
    sraw = P.tile("sraw", [8, 2])
    P.dma("sync", sraw[:], K.sTin, writes=[sraw])
    P.op("scalar", lambda e: e.activation(K.sT[:], sraw[:], AF.Silu), reads=[sraw], writes=[K.sT])
    last = []
    for i in range(0, TT, 544):
        last.append(P.dma("sync" if (i // 544) % 2 == 0 else "gpsimd", K.X[i:i + 544, :], K.xin[i:i + 544, :]))
    P.barrier()
    fin = []
    def finish():
      if (not pj_input) and K.dbgfm is not None:
        fin.append(P.dma("sync", K.dbgfm[:, 0:64], K.fm[:].rearrange("p a b c -> p (a b c)"), reads=[K.fm]))
        fin.append(P.dma("sync", K.dbgfm[:, 64:80], K.A1[:].rearrange("p a b -> p (a b)"), reads=[K.A1]))
        fin.append(P.dma("sync", K.dbgfm[:, 80:96], K.A2[:].rearrange("p a b -> p (a b)"), reads=[K.A2]))
        for g in range(2):
            for s in range(2):
                fin.append(P.dma("sync", K.dbgG[:, (g * 2 + s) * 1024:(g * 2 + s + 1) * 1024], K.G[g][s][:], reads=[K.G[g][s]]))
      P.barrier()
      if not K.final_ops:
          fin.append(P.dma("sync", K.out[0:128, :], K.X[0:128, :]))
      P.finalize(fin + last + K.final_ops)
      return nc, hc
    if plan is not None:
        phases = {"mods": phase_mods, "proj": phase_proj, "gdnpre": phase_gdn_pre, "gdns0": lambda K, l: phase_gdn_scan(K, l, 0),
                  "gdns1": lambda K, l: phase_gdn_scan(K, l, 1), "gdnfin": phase_gdn_fin, "gdnscan": phase_gdn_scan_both, "wout": phase_wout,
                  "mlp": lambda K, l: phase_mlp(K, l, False), "mlplast": lambda K, l: phase_mlp(K, l, True)}
        phases.update(EXTRA_PHASES); phases.update(EXTRA_D); phases.update(EXTRA_E); phases.update(EXTRA_F)
        for (nm, l) in plan:
            phases[nm](K, l)
        pj_input = "mods" not in [p for p, _ in plan]
        return finish()
    for l in range(L):
        if not pj_input:
            phase_mods(K, l)
            if stop_after == ("mods", l): return finish()
            phase_proj(K, l)
            if stop_after == ("proj", l): return finish()
        phase_gdn_pre(K, l)
        if stop_after == ("gdnpre", l): return finish()
        phase_gdn_scan(K, l, 0)
        if stop_after == ("gdns0", l): return finish()
        phase_gdn_scan(K, l, 1)
        phase_gdn_fin(K, l)
        if stop_after == ("gdn", l): return finish()
    return finish()

EXTRA_PHASES = {}

def host_inputs(inputs, hc):
    silu_in = []
    ins = []
    f = lambda a: np.ascontiguousarray(a, dtype=np.float32)
    ln1T = f(inputs["ln1_w"].reshape(L, 8, 128).transpose(2, 0, 1))
    ln2T = f(inputs["ln2_w"].reshape(L, 8, 128).transpose(2, 0, 1))
    mod_bT = f(inputs["mod_b"].reshape(L, 48, 128).transpose(0, 2, 1))
    for b in range(8):
        d = {}
        d["xin"] = f(np.concatenate([inputs["ctx"][b], inputs["x"][b]], axis=0))
        sT = np.stack([inputs["c"][b].reshape(8, 128).T, inputs["c_ctx"].reshape(8, 128).T], axis=-1)
        d["sTin"] = f(sT)
        d["mod_w"] = f(inputs["mod_w"]); d["mod_b"] = f(inputs["mod_b"]); d["mod_bT"] = mod_bT
        d["ln1T"] = ln1T; d["ln2T"] = ln2T
        d["w_in"] = f(inputs["w_in"])
        for k in ("w_out", "mlp_w1", "mlp_w2", "da_q_norm", "da_k_norm", "da_lam", "da_subln", "hg_lb_raw", "hg_norm_w", "hy_conv_w", "hy_w1", "hy_w2", "hy_w3", "hy_b1", "hy_f1", "hy_b2", "hy_f2", "hy_decay", "hy_bias"):
            d[k] = f(inputs[k])
        for k in ("gdn_conv_w", "gdn_a_log", "gdn_dt_bias", "gdn_norm_w"):
            d[k] = f(inputs[k])
        for k, v in hc.items():
            d["c_" + k] = v
        ins.append(d)
    return ins


FULL_PLAN = []
for _l in range(L):
    for _p in ("mods", "proj", "gdnpre", "gdnscan", "gdnfin", "hy", "hg", "attn", "wout"):
        FULL_PLAN.append((_p, _l))
    FULL_PLAN.append(("mlplast" if _l == L - 1 else "mlp", _l))


def kernel(**inputs):
    nc, hc = build(plan=FULL_PLAN)
    ins = host_inputs({k: np.asarray(v) for k, v in inputs.items()}, hc)
    res = run_bass_kernel_spmd(nc, ins, core_ids=list(range(8)))
    return np.stack([np.asarray(r["out"], dtype=np.float32) for r in res.results], axis=0)
```
